# Optimizing a Trainium2 kernel written in Bass

```python
import jax, jax.numpy as jnp
from jax import lax
import numpy as np

D_MODEL = 1024
BATCH = 32
SEQ = 256
DEPTH = 1
DEC_BATCH = 8
DEC_SEQ = 4096
PAST_LEN = 512

GRID_W = 64
HEAD_DIM = 64
A_HEADS = 8
A_KV_HEADS = 2
B_HEADS = 8
A_WIDTH = A_HEADS * HEAD_DIM
A_KV_WIDTH = A_KV_HEADS * HEAD_DIM
B_WIDTH = B_HEADS * HEAD_DIM
D_FF = 4 * D_MODEL
WIN_H = 8
WIN_W = 16
Q_BLOCK = 128
ROPE_BASE = 10000.0
ROPE_PAIRS = HEAD_DIM // 4
RMS_EPS = 1e-6
N_MOD = 6
IN_SPLITS = (A_WIDTH, A_KV_WIDTH, A_KV_WIDTH, B_WIDTH, B_WIDTH, B_WIDTH, D_MODEL, D_MODEL)
IN_WIDTH = sum(IN_SPLITS)
SPLIT_POINTS = tuple(int(i) for i in np.cumsum(IN_SPLITS)[:-1])
NEG_INF = -1e30

kernel_name = 'hybrid_dit_gqa_natten_prefix_step'


def _rmsnorm(x, g):
    xf = x.astype(jnp.float32)
    inv = lax.rsqrt(jnp.mean(xf * xf, axis=-1, keepdims=True) + RMS_EPS)
    return (xf * inv).astype(x.dtype) * g


def _modulation(cvec, w_mod, b_mod):
    m = jax.nn.silu(cvec) @ w_mod + b_mod
    return jnp.split(m[:, None, :], N_MOD, axis=-1)


def _axial_rope_tables(t):
    pos = jnp.arange(t, dtype=jnp.int32)
    row = (pos // GRID_W).astype(jnp.float32)
    col = (pos % GRID_W).astype(jnp.float32)
    inv = ROPE_BASE ** (-jnp.arange(ROPE_PAIRS, dtype=jnp.float32) / ROPE_PAIRS)
    ang_r = row[:, None] * inv
    ang_c = col[:, None] * inv
    return jnp.cos(ang_r), jnp.sin(ang_r), jnp.cos(ang_c), jnp.sin(ang_c)


def _rot(x, cos, sin):
    cos = cos[None, :, None, :].astype(x.dtype)
    sin = sin[None, :, None, :].astype(x.dtype)
    x1, x2 = x[..., :ROPE_PAIRS], x[..., ROPE_PAIRS:]
    return jnp.concatenate([x1 * cos - x2 * sin, x1 * sin + x2 * cos], axis=-1)


def _axial_rope(x, tables):
    cr, sr, cc, sc = tables
    half = HEAD_DIM // 2
    return jnp.concatenate([_rot(x[..., :half], cr, sr), _rot(x[..., half:], cc, sc)], axis=-1)


def _block_attention(q, k, v):
    b, t, h, dh = q.shape
    kv = k.shape[2]
    rep = h // kv
    nb = t // Q_BLOCK
    qb = q.reshape(b, nb, Q_BLOCK, kv, rep, dh).transpose(1, 0, 2, 3, 4, 5)
    scale = dh ** -0.5

    def one(qblk):
        s = jnp.einsum('bqgrd,bkgd->bgrqk', qblk, k).astype(jnp.float32) * scale
        p = jax.nn.softmax(s, axis=-1).astype(v.dtype)
        return jnp.einsum('bgrqk,bkgd->bqgrd', p, v)

    o = lax.map(one, qb)
    return o.transpose(1, 0, 2, 3, 4, 5).reshape(b, t, h * dh)


def _neighbourhood_attention(q, k, v, ctx_k, ctx_v, rel_bias):
    b, t, h, dh = q.shape
    rows = t // GRID_W
    wh = min(WIN_H, rows)
    ww = WIN_W
    scale = dh ** -0.5
    qg = q.reshape(b, rows, GRID_W, h, dh)
    kg = k.reshape(b, rows, GRID_W, h, dh)
    vg = v.reshape(b, rows, GRID_W, h, dh)
    col = jnp.arange(GRID_W, dtype=jnp.int32)
    cstart = jnp.clip(col - ww // 2, 0, GRID_W - ww)
    col_mask = (col[None, :] >= cstart[:, None]) & (col[None, :] < cstart[:, None] + ww)
    dc_idx = jnp.clip(col[None, :] - col[:, None] + WIN_W - 1, 0, 2 * WIN_W - 2)
    n_loc = wh * GRID_W

    def row_block(r):
        rs = jnp.clip(r - wh // 2, 0, rows - wh)
        kr = lax.dynamic_slice_in_dim(kg, rs, wh, axis=1)
        vr = lax.dynamic_slice_in_dim(vg, rs, wh, axis=1)
        qr = lax.dynamic_index_in_dim(qg, r, axis=1, keepdims=False)
        s_loc = jnp.einsum('bqhd,bnkhd->bhqnk', qr, kr).astype(jnp.float32) * scale
        dr_idx = rs + jnp.arange(wh, dtype=jnp.int32) - r + WIN_H - 1
        bias = rel_bias[:, dr_idx[None, :, None], dc_idx[:, None, :]].astype(jnp.float32)
        s_loc = jnp.where(col_mask[:, None, :], s_loc + bias[None], NEG_INF).reshape(b, h, GRID_W, n_loc)
        s_ctx = jnp.einsum('bqhd,bkhd->bhqk', qr, ctx_k).astype(jnp.float32) * scale
        p = jax.nn.softmax(jnp.concatenate([s_loc, s_ctx], axis=-1), axis=-1).astype(v.dtype)
        o = jnp.einsum('bhqn,bnhd->bqhd', p[..., :n_loc], vr.reshape(b, n_loc, h, dh))
        return o + jnp.einsum('bhqk,bkhd->bqhd', p[..., n_loc:], ctx_v)

    o = lax.map(row_block, jnp.arange(rows, dtype=jnp.int32))
    return o.transpose(1, 0, 2, 3, 4).reshape(b, t, h * dh)


def _mixer_inputs(h, w_in, q_norm_g, k_norm_g):
    b, t, _ = h.shape
    aq, ak, av, bq, bk, bv, ga, gb = jnp.split(h @ w_in, SPLIT_POINTS, axis=-1)
    aq = _rmsnorm(aq.reshape(b, t, A_HEADS, HEAD_DIM), q_norm_g)
    ak = _rmsnorm(ak.reshape(b, t, A_KV_HEADS, HEAD_DIM), k_norm_g)
    av = av.reshape(b, t, A_KV_HEADS, HEAD_DIM)
    bq = bq.reshape(b, t, B_HEADS, HEAD_DIM)
    bk = bk.reshape(b, t, B_HEADS, HEAD_DIM)
    bv = bv.reshape(b, t, B_HEADS, HEAD_DIM)
    return aq, ak, av, bq, bk, bv, ga, gb


def _merge(a_o, b_o, ga, gb, w_br_a, w_br_b, w_out):
    m = jax.nn.sigmoid(ga) * (a_o @ w_br_a) + jax.nn.sigmoid(gb) * (b_o @ w_br_b)
    return m @ w_out


def _mlp(h, w_mlp_in, w_mlp_out):
    return jnp.square(jax.nn.relu(h @ w_mlp_in)) @ w_mlp_out


def _context_layer(x, c_ctx, lp):
    (w_mod, b_mod, n1, n2, w_in, qg, kg, nat_bias, w_br_a, w_br_b, w_out, w1, w2) = lp
    sh1, sc1, g1, sh2, sc2, g2 = _modulation(c_ctx[None, :], w_mod, b_mod)
    h = _rmsnorm(x, n1) * (1 + sc1) + sh1
    aq, ak, av, bq, bk, bv, ga, gb = _mixer_inputs(h, w_in, qg, kg)
    a_o = _block_attention(aq, ak, av)
    b_o = _block_attention(bq, bk, bv)
    x = x + g1 * _merge(a_o, b_o, ga, gb, w_br_a, w_br_b, w_out)
    h2 = _rmsnorm(x, n2) * (1 + sc2) + sh2
    x = x + g2 * _mlp(h2, w1, w2)
    return x, (ak, av, bk, bv)


def _latent_layer(x, c, ctx_ak, ctx_av, ctx_bk, ctx_bv, lp):
    (w_mod, b_mod, n1, n2, w_in, qg, kg, nat_bias, w_br_a, w_br_b, w_out, w1, w2) = lp
    t = x.shape[1]
    sh1, sc1, g1, sh2, sc2, g2 = _modulation(c, w_mod, b_mod)
    h = _rmsnorm(x, n1) * (1 + sc1) + sh1
    aq, ak, av, bq, bk, bv, ga, gb = _mixer_inputs(h, w_in, qg, kg)
    tables = _axial_rope_tables(t)
    aq = _axial_rope(aq, tables)
    ak = _axial_rope(ak, tables)
    a_o = _block_attention(aq, jnp.concatenate([ctx_ak, ak], axis=1), jnp.concatenate([ctx_av, av], axis=1))
    b_o = _neighbourhood_attention(bq, bk, bv, ctx_bk, ctx_bv, nat_bias)
    x = x + g1 * _merge(a_o, b_o, ga, gb, w_br_a, w_br_b, w_out)
    h2 = _rmsnorm(x, n2) * (1 + sc2) + sh2
    return x + g2 * _mlp(h2, w1, w2)


def setup_inputs(seed: int = 0) -> dict:
    key = jax.random.key(seed)
    ks = jax.random.split(key, 24)
    nrm = jax.random.normal
    f32 = jnp.float32
    d = D_MODEL
    return {
        'x_prompt': nrm(ks[0], (BATCH, SEQ, d), f32),
        'x_sample': nrm(ks[1], (DEC_BATCH, DEC_SEQ, d), f32),
        'cache_a_k': nrm(ks[2], (DEC_BATCH, DEPTH, PAST_LEN, A_KV_HEADS, HEAD_DIM), f32),
        'cache_a_v': nrm(ks[3], (DEC_BATCH, DEPTH, PAST_LEN, A_KV_HEADS, HEAD_DIM), f32),
        'cache_b_k': nrm(ks[4], (DEC_BATCH, DEPTH, PAST_LEN, B_HEADS, HEAD_DIM), f32),
        'cache_b_v': nrm(ks[5], (DEC_BATCH, DEPTH, PAST_LEN, B_HEADS, HEAD_DIM), f32),
        'c': nrm(ks[6], (DEC_BATCH, d), f32),
        'c_ctx': nrm(ks[7], (d,), f32),
        'w_mod': nrm(ks[8], (DEPTH, d, N_MOD * d), f32) * (0.5 * d ** -0.5),
        'b_mod': nrm(ks[9], (DEPTH, N_MOD * d), f32) * 0.01,
        'norm1_g': 1.0 + 0.01 * nrm(ks[10], (DEPTH, d), f32),
        'norm2_g': 1.0 + 0.01 * nrm(ks[11], (DEPTH, d), f32),
        'w_in': nrm(ks[12], (DEPTH, d, IN_WIDTH), f32) * d ** -0.5,
        'q_norm_g': 1.0 + 0.01 * nrm(ks[13], (DEPTH, HEAD_DIM), f32),
        'k_norm_g': 1.0 + 0.01 * nrm(ks[14], (DEPTH, HEAD_DIM), f32),
        'nat_bias': nrm(ks[15], (DEPTH, B_HEADS, 2 * WIN_H - 1, 2 * WIN_W - 1), f32) * 0.1,
        'w_br_a': nrm(ks[16], (DEPTH, A_WIDTH, d), f32) * A_WIDTH ** -0.5,
        'w_br_b': nrm(ks[17], (DEPTH, B_WIDTH, d), f32) * B_WIDTH ** -0.5,
        'w_out': nrm(ks[18], (DEPTH, d, d), f32) * d ** -0.5,
        'w_mlp_in': nrm(ks[19], (DEPTH, d, D_FF), f32) * d ** -0.5,
        'w_mlp_out': nrm(ks[20], (DEPTH, D_FF, d), f32) * D_FF ** -0.5,
        'final_norm_g': 1.0 + 0.01 * nrm(ks[21], (d,), f32),
    }


def reference(x_prompt, x_sample, cache_a_k, cache_a_v, cache_b_k, cache_b_v, c, c_ctx,
              w_mod, b_mod, norm1_g, norm2_g, w_in, q_norm_g, k_norm_g, nat_bias,
              w_br_a, w_br_b, w_out, w_mlp_in, w_mlp_out, final_norm_g):
    xp = x_prompt
    xs = x_sample
    ak_l, av_l, bk_l, bv_l = [], [], [], []
    for l in range(DEPTH):
        lp = (w_mod[l], b_mod[l], norm1_g[l], norm2_g[l], w_in[l], q_norm_g[l], k_norm_g[l],
              nat_bias[l], w_br_a[l], w_br_b[l], w_out[l], w_mlp_in[l], w_mlp_out[l])
        xp, (ak, av, bk, bv) = _context_layer(xp, c_ctx, lp)
        ak_l.append(ak)
        av_l.append(av)
        bk_l.append(bk)
        bv_l.append(bv)
        xs = _latent_layer(xs, c, cache_a_k[:, l], cache_a_v[:, l], cache_b_k[:, l], cache_b_v[:, l], lp)
    y_prompt = _rmsnorm(xp, final_norm_g)
    y_sample = _rmsnorm(xs, final_norm_g)
    new_a_k = jnp.stack(ak_l, axis=1)
    new_a_v = jnp.stack(av_l, axis=1)
    new_b_k = jnp.stack(bk_l, axis=1)
    new_b_v = jnp.stack(bv_l, axis=1)
    return (y_prompt, y_sample, new_a_k, new_a_v, new_b_k, new_b_v)
```

```python
from contextlib import ExitStack
import numpy as np
import concourse.bass as bass
import concourse.mybir as mybir
from concourse.bass_utils import run_bass_kernel_spmd

F32 = mybir.dt.float32
BF16 = mybir.dt.bfloat16
AF = mybir.ActivationFunctionType
ALU = mybir.AluOpType
AX = mybir.AxisListType

ENGS = ("pe", "act", "dve", "pool", "sp")
EPS = 1e-6
NEG = -30000.0


class Res:
    __slots__ = ("name", "w", "r", "excl")

    def __init__(self, name, excl=False):
        self.name = name
        self.w = {}
        self.r = []
        self.excl = excl


class Ins:
    __slots__ = ("eng", "fn", "deps", "dma", "stream", "flag", "sem", "val", "waits", "clock")

    def __init__(self, eng, fn, dma, stream):
        self.eng = eng
        self.fn = fn
        self.deps = []
        self.dma = dma
        self.stream = stream
        self.flag = False
        self.sem = None
        self.val = 0
        self.waits = []
        self.clock = None


class Prog:
    def __init__(self, nc, stack):
        self.nc = nc
        self.stack = stack
        self.pending = []
        self.esem = {}
        for e in ENGS[:4]:
            self.esem[e] = stack.enter_context(nc.semaphore("sem_" + e))
        self.ecount = {e: 0 for e in ENGS}
        self.ssem = {}
        self.scount = {}
        self.know = {e: {} for e in ENGS}
        self.last = {e: None for e in ENGS}
        self.last_dma = {}
        self.barrier_deps = {e: [] for e in ENGS}
        self.all_res = []
        self.n_ins = 0
        self.n_wait = 0

    def res(self, name, excl=False):
        r = Res(name, excl)
        self.all_res.append(r)
        return r

    def add(self, eng, fn, reads=(), writes=(), dma=False, stream=None):
        ins = Ins(eng, fn, dma, stream)
        if dma:
            ins.flag = True
        writes = list(writes) + [r for r in reads if r.excl]
        reads = [r for r in reads if not r.excl]
        deps = []
        for r in reads:
            for w in r.w.values():
                deps.append((w, "raw"))
        for r in writes:
            for w in r.w.values():
                deps.append((w, "waw"))
            for rd in r.r:
                deps.append((rd, "war"))
        for d in self.barrier_deps[eng]:
            deps.append((d, "raw"))
        self.barrier_deps[eng] = []
        seen = set()
        for d, kind in deps:
            if d is ins or id(d) in seen:
                continue
            if (not d.dma) and (not dma) and d.eng == eng and eng == "pe":
                continue
            seen.add(id(d))
            d.flag = True
            ins.deps.append(d)
        for r in reads:
            r.r.append(ins)
        key = ("d", stream) if dma else eng
        for r in writes:
            r.w[key] = ins
            r.r = []
        self.pending.append(ins)
        if dma:
            self.last_dma[stream] = ins
        else:
            self.last[eng] = ins
        return ins

    def barrier(self):
        alls = [i for i in self.last.values() if i is not None] + list(self.last_dma.values())
        for e in ENGS:
            self.barrier_deps[e] = list(alls)

    @staticmethod
    def _semkey(ins):
        return ("s", ins.stream) if ins.dma else ("e", ins.eng)

    def flush(self, final=False):
        nc = self.nc
        lasts = [i for i in self.last.values() if i is not None] + list(self.last_dma.values())
        for d in lasts:
            if d.val == 0:
                d.flag = True
        if final:
            fin = Ins("sp", None, False, None)
            fin.deps = lasts
            self.pending.append(fin)
        per = {e: [] for e in ENGS}
        for ins in self.pending:
            e = ins.eng
            K = self.know[e]
            for d in ins.deps:
                key = self._semkey(d)
                assert d.val > 0, "dep not yet numbered"
                if K.get(key, 0) >= d.val:
                    continue
                ins.waits.append((d.sem, d.val))
                for k2, v2 in d.clock.items():
                    if K.get(k2, 0) < v2:
                        K[k2] = v2
            if ins.flag:
                if ins.dma:
                    if ins.stream not in self.ssem:
                        self.ssem[ins.stream] = self.stack.enter_context(
                            nc.semaphore("sd_%d" % len(self.ssem)))
                        self.scount[ins.stream] = 0
                    self.scount[ins.stream] += 16
                    ins.sem = self.ssem[ins.stream]
                    ins.val = self.scount[ins.stream]
                else:
                    self.ecount[e] += 1
                    ins.sem = self.esem[e]
                    ins.val = self.ecount[e]
                ck = dict(K)
                ck[self._semkey(ins)] = ins.val
                ins.clock = ck
            per[e].append(ins)
            self.n_ins += 1
            self.n_wait += len(ins.waits)
        self.pending = []
        for r in self.all_res:
            r.w = {}
            r.r = []
        self.barrier()

        def replay(lst):
            def f(eng):
                for ins in lst:
                    for (s, v) in ins.waits:
                        eng.wait_ge(s, v)
                    if ins.fn is None:
                        continue
                    r = ins.fn(eng)
                    if ins.flag:
                        r.then_inc(ins.sem, 16 if ins.dma else 1)
            return f

        with nc.Block() as block:
            if per["sp"]:
                block.sync(replay(per["sp"]))
            if per["pool"]:
                block.gpsimd(replay(per["pool"]))
            if per["act"]:
                block.scalar(replay(per["act"]))
            if per["dve"]:
                block.vector(replay(per["dve"]))
            if per["pe"]:
                block.tensor(replay(per["pe"]))


def build(stage=99, debug=False):
    nc = bass.Bass("TRN2", target_bir_lowering=False)

    def din(name, shape, dt=F32):
        return nc.dram_tensor(name, list(shape), dt, kind="ExternalInput").ap()

    def dout(name, shape, dt=F32):
        return nc.dram_tensor(name, list(shape), dt, kind="ExternalOutput").ap()

    def dscr(name, shape, dt):
        return nc.dram_tensor(name, list(shape), dt).ap()

    xs = din("xs", [4096, 1024])
    xp = din("xp", [1024, 1024])
    cak = din("cak", [512, 128])
    cav = din("cav", [512, 128])
    cbk = din("cbk", [512, 512])
    cbv = din("cbv", [512, 512])
    cT = din("cT", [128, 16])
    w_mod = din("w_mod", [1024, 6144])
    bmodT2 = din("bmodT2", [128, 96])
    n1T2 = din("n1T2", [128, 16])
    n2T2 = din("n2T2", [128, 16])
    w_in = din("w_in", [1024, 4352])
    qg8 = din("qg8", [512])
    kg2 = din("kg2", [128])
    btab = din("btab", [128, 8 * 16 * 64])
    w_br_a = din("w_br_a", [512, 1024])
    w_br_b = din("w_br_b", [512, 1024])
    w_out = din("w_out", [1024, 1024])
    w1 = din("w1", [1024, 4096])
    w2 = din("w2", [4096, 1024])
    gf = din("gf", [1024])
    ident = din("ident", [128, 128])
    rope = din("rope", [4096, 2, 512])

    ys = dout("ys", [4096, 1024])
    yp = dout("yp", [1024, 1024])
    nak = dout("nak", [1024, 128])
    nav = dout("nav", [1024, 128])
    nbk = dout("nbk", [1024, 512])
    nbv = dout("nbv", [1024, 512])

    HT1 = dscr("HT1", [10, 128, 8, 512], BF16)
    HT2 = dscr("HT2", [10, 128, 8, 512], BF16)
    AOAd = dscr("AOAd", [10, 64, 8, 512], BF16)
    AOBd = dscr("AOBd", [10, 128, 4, 512], BF16)
    X1 = dscr("X1", [5120, 1024], F32)
    GMOD = dscr("GMOD", [4, 1024], F32)

    def xrows(tg0, n):
        if tg0 < 4096:
            return xs[tg0:tg0 + n, :]
        return xp[tg0 - 4096:tg0 - 4096 + n, :]

    def yrows(tg0, n):
        if tg0 < 4096:
            return ys[tg0:tg0 + n, :]
        return yp[tg0 - 4096:tg0 - 4096 + n, :]

    with ExitStack() as top:
        P = Prog(nc, top)

        def sbuf(st, name, shape, dt):
            return st.enter_context(nc.sbuf_tensor(name, list(shape), dt))

        def MM(out, lhsT, rhs, start, stop, reads, writes):
            return P.add("pe", lambda e: e.matmul(out, lhsT=lhsT, rhs=rhs, start=start, stop=stop,
                                                  skip_group_check=True), reads, writes)

        def TR(out, in_, idn, reads, writes):
            return P.add("pe", lambda e: e.transpose(out=out, in_=in_, identity=idn), reads, writes)

        def ACT(out, in_, func, reads, writes, scale=None, accum=None):
            kw = {}
            if scale is not None:
                kw["scale"] = scale
            if accum is not None:
                kw["accum_out"] = accum
            return P.add("act", lambda e: e.activation(out=out, in_=in_, func=func, **kw), reads, writes)

        def TS(out, in0, s1, s2, op0, op1, reads, writes, eng="dve"):
            if s2 is None:
                return P.add(eng, lambda e: e.tensor_scalar(out=out, in0=in0, scalar1=s1, scalar2=None,
                                                            op0=op0), reads, writes)
            return P.add(eng, lambda e: e.tensor_scalar(out=out, in0=in0, scalar1=s1, scalar2=s2,
                                                        op0=op0, op1=op1), reads, writes)

        def TT(out, in0, in1, op, reads, writes, eng="dve"):
            return P.add(eng, lambda e: e.tensor_tensor(out=out, in0=in0, in1=in1, op=op), reads, writes)

        def STT(out, in0, scalar, in1, op0, op1, reads, writes, eng="dve"):
            return P.add(eng, lambda e: e.scalar_tensor_tensor(out=out, in0=in0, scalar=scalar, in1=in1,
                                                               op0=op0, op1=op1), reads, writes)

        def CP(out, in_, reads, writes, eng="dve"):
            return P.add(eng, lambda e: e.tensor_copy(out=out, in_=in_), reads, writes)

        def RECIP(out, in_, reads, writes):
            return P.add("dve", lambda e: e.reciprocal(out=out, in_=in_), reads, writes)

        def RED(out, in_, reads, writes):
            return P.add("dve", lambda e: e.tensor_reduce(out=out, in_=in_, axis=AX.X, op=ALU.add), reads, writes)

        def MEMSET(ap, val, writes, eng="dve"):
            return P.add(eng, lambda e: e.memset(ap, val), [], writes)

        def DMA(q, out, in_, reads, writes, stream, slow=False):
            if slow:
                return P.add(q, lambda e: e.dma_start(out=out, in_=in_, allow_slow_non_contiguous=True),
                             reads, writes, dma=True, stream=stream)
            return P.add(q, lambda e: e.dma_start(out=out, in_=in_), reads, writes, dma=True, stream=stream)

        ps = [top.enter_context(nc.psum_tensor("ps%d" % i, [128, 512], F32)) for i in range(8)]
        r_ps = [P.res("ps%d" % i, excl=True) for i in range(8)]

        class Rot:
            def __init__(self, idxs):
                self.idxs = idxs
                self.i = 0

            def __call__(self):
                k = self.idxs[self.i % len(self.idxs)]
                self.i += 1
                return k

        idf = sbuf(top, "idf", [128, 128], F32)
        idb = sbuf(top, "idb", [128, 128], BF16)
        onesf = sbuf(top, "onesf", [128, 128], F32)
        epst = sbuf(top, "epst", [128, 1], F32)
        MODS = sbuf(top, "MODS", [128, 4, 16], F32)
        r_c = P.res("consts")

        def tile_info(i):
            return (i * 512, 0 if i < 8 else 1)

        with ExitStack() as st:
            cTt = sbuf(st, "cTt", [128, 16], F32)
            sT = sbuf(st, "sT", [128, 16], BF16)
            wm = [sbuf(st, "wm%d" % k, [128, 8, 512], BF16) for k in range(2)]
            r_wm = [P.res("wm%d" % k) for k in range(2)]
            modT = sbuf(st, "modT", [128, 96], F32)
            bmt = sbuf(st, "bmt", [128, 96], F32)
            n1t = sbuf(st, "n1t", [128, 16], F32)
            n2t = sbuf(st, "n2t", [128, 16], F32)
            r_l = P.res("a0loads")
            r_sT = P.res("sT")
            r_mod = P.res("modT")
            DMA("sp", idf[:, :], ident[:, :], [], [r_c], "c0")
            DMA("sp", cTt[:, :], cT[:, :], [], [r_l], "c1")
            DMA("sp", bmt[:, :], bmodT2[:, :], [], [r_l], "c2")
            DMA("sp", n1t[:, :], n1T2[:, :], [], [r_l], "c3")
            DMA("sp", n2t[:, :], n2T2[:, :], [], [r_l], "c4")
            MEMSET(onesf[:, :], 1.0, [r_c])
            MEMSET(epst[:, :], EPS, [r_c])
            CP(idb[:, :], idf[:, :], [r_c], [r_c])
            ACT(sT[:, :], cTt[:, :], AF.Silu, [r_l], [r_sT])
            wmv = w_mod.rearrange("(c p) n -> p c n", p=128)
            for k in range(12):
                DMA("pool", wm[k % 2][:, :, :], wmv[:, :, k * 512:(k + 1) * 512], [], [r_wm[k % 2]], "wm%d" % (k % 2))
                for j in range(4):
                    fc = 4 * k + j
                    for c in range(8):
                        MM(ps[0][:, fc * 2:fc * 2 + 2], wm[k % 2][:, c, j * 128:(j + 1) * 128],
                           sT[:, c * 2:c * 2 + 2], c == 0, c == 7, [r_wm[k % 2], r_sT], [r_ps[0]])
            TT(modT[:, :], ps[0][:, 0:96], bmt[:, :], ALU.add, [r_ps[0], r_l], [r_mod])
            STT(MODS[:, 0, :], modT[:, 16:32], 1.0, n1t[:, :], ALU.add, ALU.mult, [r_mod, r_l], [r_c])
            CP(MODS[:, 1, :], modT[:, 0:16], [r_mod], [r_c])
            STT(MODS[:, 2, :], modT[:, 64:80], 1.0, n2t[:, :], ALU.add, ALU.mult, [r_mod, r_l], [r_c])
            CP(MODS[:, 3, :], modT[:, 48:64], [r_mod], [r_c])
            for which, base in ((0, 32), (1, 80)):
                for v in range(2):
                    row = which * 2 + v
                    dst = bass.AP(GMOD.tensor, row * 1024, [[1, 128], [128, 8]])
                    s0 = modT[:, base + v:base + v + 1]
                    src = bass.AP(s0.tensor, s0.offset, [[s0.ap[0][0], 128], [2, 8]])
                    DMA("sp", dst, src, [r_mod], [], "gm%d" % row, slow=True)
            P.flush()

        A1 = lambda c, v: MODS[:, 0, c * 2 + v:c * 2 + v + 1]
        SH1 = lambda c, v: MODS[:, 1, c * 2 + v:c * 2 + v + 1]
        A2 = lambda c, v: MODS[:, 2, c * 2 + v:c * 2 + v + 1]
        SH2 = lambda c, v: MODS[:, 3, c * 2 + v:c * 2 + v + 1]

        def hT_chain(src_ap, r_src, junk, r_junk, stat, r_stat, xn, r_xn):
            ACT(junk[:, :], src_ap, AF.Square, [r_src], [r_junk, r_stat], scale=1.0 / 32.0, accum=stat[:, 0:1])
            TS(stat[:, 1:2], stat[:, 0:1], EPS, None, ALU.add, None, [r_stat], [r_stat])
            ACT(stat[:, 2:3], stat[:, 1:2], AF.Sqrt, [r_stat], [r_stat])
            RECIP(stat[:, 3:4], stat[:, 2:3], [r_stat], [r_stat])
            ACT(xn[:, :], src_ap, AF.Copy, [r_src, r_stat], [r_xn], scale=stat[:, 3:4])

        def hT_tr(xn, r_xn, dst_fn, r_dst, Afn, Sfn, v, rot):
            for half in range(2):
                b = rot()
                for cc in range(4):
                    c = half * 4 + cc
                    TR(ps[b][:, cc * 128:(cc + 1) * 128], xn[:, c * 128:(c + 1) * 128], idf[:, :],
                       [r_xn, r_c], [r_ps[b]])
                for cc in range(4):
                    c = half * 4 + cc
                    TS(dst_fn(c), ps[b][:, cc * 128:(cc + 1) * 128], Afn(c, v), Sfn(c, v), ALU.mult, ALU.add,
                       [r_ps[b], r_c], [r_dst])

        if stage < 1:
            P.flush(final=True)
            return nc

        with ExitStack() as kv:
            KTA = sbuf(kv, "KTA", [128, 2, 4608], BF16)
            VA = sbuf(kv, "VA", [128, 37, 2, 65], BF16)
            KTB = sbuf(kv, "KTB", [128, 4, 4608], BF16)
            LB = sbuf(kv, "LB", [128, 36, 4, 160], BF16)
            r_kt = [P.res("kt%d" % t) for t in range(36)]

            def phaseA(tag, tilesA, do_ctx):
              with ExitStack() as st0:
                _sb = sbuf
                def sbuf_(st_, name, shape, dt):
                    return _sb(st_, name + tag, shape, dt)
                st = st0
                wA = sbuf_(st, "wA", [128, 8, 256], BF16)
                wBK = sbuf_(st, "wBK", [128, 8, 512], BF16)
                wBV = sbuf_(st, "wBV", [128, 8, 512], BF16)
                r_w = P.res("wA")
                wv = w_in.rearrange("(c p) n -> p c n", p=128)
                DMA("pool", wA[:, :, :], wv[:, :, 512:768], [], [r_w], "w0")
                DMA("pool", wBK[:, :, :], wv[:, :, 1280:1792], [], [r_w], "w1")
                DMA("pool", wBV[:, :, :], wv[:, :, 1792:2304], [], [r_w], "w2")
                kgt = sbuf_(st, "kgt", [128, 128], F32)
                DMA("sp", kgt[:, :], bass.AP(kg2.tensor, 0, [[0, 128], [1, 128]]), [], [r_w], "c5")
                if do_ctx:
                    MEMSET(VA[:, :, :, :], 1.0, r_kt)
                    MEMSET(KTA[64:128, :, :], 0.0, r_kt)
                    MEMSET(LB[:, :, :, 64:96], 0.0, r_kt)
                    MEMSET(LB[:, :, :, 64:65], 1.0, r_kt)

                xt = [sbuf_(st, "xt%d" % k, [128, 4, 1024], F32) for k in range(2)]
                r_xt = [P.res("xt%d" % k) for k in range(2)]
                junk = sbuf_(st, "junk", [128, 1024], BF16)
                r_junk = P.res("junk")
                stat = [sbuf_(st, "stat%d" % k, [128, 4], F32) for k in range(2)]
                r_stat = [P.res("stat%d" % k) for k in range(2)]
                xn = [sbuf_(st, "xn%d" % k, [128, 1024], F32) for k in range(2)]
                r_xn = [P.res("xn%d" % k) for k in range(2)]
                hT = [sbuf_(st, "hT%d" % k, [128, 8, 512], BF16) for k in range(2)]
                r_hT = [[P.res("hT%d_%d" % (k, s)) for s in range(4)] for k in range(2)]
                ropeT = [sbuf_(st, "ropeT%d" % k, [128, 2, 128], F32) for k in range(2)]
                r_rope = [P.res("rope%d" % k) for k in range(2)]
                akf = [sbuf_(st, "akf%d" % k, [128, 128], F32) for k in range(2)]
                r_akf = [P.res("akf%d" % k) for k in range(2)]
                sqk = sbuf_(st, "sqk", [128, 128], F32)
                kst = [sbuf_(st, "kst%d" % k, [128, 8], F32) for k in range(2)]
                akn = [sbuf_(st, "akn%d" % k, [128, 128], F32) for k in range(3)]
                r_akn = [P.res("akn%d" % k) for k in range(3)]
                akr = [sbuf_(st, "akr%d" % k, [128, 128], F32) for k in range(3)]
                r_akr = [P.res("akr%d" % k) for k in range(3)]
                t1 = sbuf_(st, "t1", [128, 128], F32)
                t2 = sbuf_(st, "t2", [128, 128], F32)
                r_tmp = P.res("tmpA")
                r_t1 = P.res("t1A")
                r_t2 = P.res("t2A")
                r_kst = [P.res("kst%d" % k) for k in range(2)]
                stg = [sbuf_(st, "stg%d" % k, [128, 512], F32) for k in range(3)]
                r_stg = [P.res("stg%d" % k) for k in range(3)]
                stg_i = [0]
                ctx32 = [sbuf_(st, "ctx32_%d" % k, [128, 512], F32) for k in range(2)]
                r_ctx = [P.res("ctx32_%d" % k) for k in range(2)]
                rot = Rot([0, 1, 2, 3, 4, 5, 6, 7])

                def next_stg():
                    k = stg_i[0] % 3
                    stg_i[0] += 1
                    return k

                for t in (range(4) if do_ctx else []):
                    rk = [r_kt[t]]
                    a = ctx32[t % 2]
                    ra = r_ctx[t % 2]
                    DMA("sp", a[:, 0:128], cak[t * 128:(t + 1) * 128, :], [], [ra], "cx0")
                    b = rot()
                    for g in range(2):
                        TR(ps[b][0:64, g * 128:(g + 1) * 128], a[:, g * 64:(g + 1) * 64], idf[:, :], [ra, r_c], [r_ps[b]])
                    CP(KTA[0:64, :, t * 128:(t + 1) * 128], ps[b][0:64, 0:256].rearrange("p (g n) -> p g n", g=2),
                       [r_ps[b]], rk)
                    DMA("sp", a[:, 128:256], cav[t * 128:(t + 1) * 128, :], [], [ra], "cx1")
                    CP(VA[:, t, :, 0:64], a[:, 128:256].rearrange("p (g d) -> p g d", g=2), [ra], rk)
                    a2 = ctx32[(t + 1) % 2]
                    ra2 = r_ctx[(t + 1) % 2]
                    DMA("sp", a2[:, :], cbk[t * 128:(t + 1) * 128, :], [], [ra2], "cx2")
                    b = rot()
                    for j in range(4):
                        TR(ps[b][:, j * 128:(j + 1) * 128], a2[:, j * 128:(j + 1) * 128], idf[:, :], [ra2, r_c], [r_ps[b]])
                    CP(KTB[:, :, t * 128:(t + 1) * 128], ps[b][:, :].rearrange("p (j n) -> p j n", j=4), [r_ps[b]], rk)
                    DMA("sp", a[:, :], cbv[t * 128:(t + 1) * 128, :], [], [ra], "cx3")
                    av4 = a[:, :].rearrange("p (j e d) -> p j e d", j=4, e=2)
                    CP(LB[:, t, :, 0:64], av4[:, :, 0, :], [ra], rk)
                    CP(LB[:, t, :, 96:160], av4[:, :, 1, :], [ra], rk)


                def loadA(idx):
                    tg0, T, v, kb, isp, pr0 = tilesA[idx]
                    k = idx % 2
                    ns = T // 128
                    DMA("sp", xt[k][:, 0:ns, :], xrows(tg0, T).rearrange("(s p) d -> p s d", p=128), [], [r_xt[k]],
                        "xt%d" % k)

                pend_tr = [None]
                pend_q = []
                nT = len(tilesA)

                def chainA(idx, s_):
                    k_ = idx % 2
                    q_ = s_ % 2
                    hT_chain(xt[k_][:, s_, :], r_xt[k_], junk, r_junk, stat[q_], r_stat[q_], xn[q_], r_xn[q_])

                def trA(idx, s_):
                    k_ = idx % 2
                    q_ = s_ % 2
                    v_ = tilesA[idx][2]
                    hT_tr(xn[q_], r_xn[q_], lambda c, k_=k_, s_=s_: hT[k_][:, c, s_ * 128:(s_ + 1) * 128],
                          r_hT[k_][s_], A1, SH1, v_, rot)

                def bounceA(idx):
                    tg0, T, v, kb, isp, pr0 = tilesA[idx]
                    k_ = idx % 2
                    ns_ = T // 128
                    ti = tg0 // 512
                    co = tg0 % 512
                    DMA("sp", HT1[ti, :, :, co:co + T], hT[k_][:, :, 0:T], r_hT[k_][0:ns_], [], "ht%d" % k_)

                def stage2_sub(idx, s):
                    tg0, T, v, kb, isp, pr0 = tilesA[idx]
                    k = idx % 2
                    q = s % 2
                    q3 = s % 3
                    kt = (kb + s * 128) // 128
                    rk = [r_kt[kt]]
                    koff = kb + s * 128
                    rh = [r_hT[k][s], r_w]
                    b = rot()
                    for c in range(8):
                        MM(ps[b][:, 0:256], hT[k][:, c, s * 128:(s + 1) * 128], wA[:, c, :], c == 0, c == 7,
                           rh, [r_ps[b]])
                    ACT(akf[q][:, :], ps[b][:, 0:128], AF.Copy, [r_ps[b]], [r_akf[q]])
                    CP(VA[:, kt, :, 0:64], ps[b][:, 128:256].rearrange("p (g d) -> p g d", g=2), [r_ps[b]], rk)
                    if isp:
                        sk = next_stg()
                        ACT(stg[sk][:, 0:128], ps[b][:, 128:256], AF.Copy, [r_ps[b]], [r_stg[sk]])
                        DMA("sp", nav[pr0 + s * 128:pr0 + (s + 1) * 128, :], stg[sk][:, 0:128], [r_stg[sk]], [],
                            "stg%d" % sk)
                    TT(sqk[:, :], akf[q][:, :], akf[q][:, :], ALU.mult, [r_akf[q]], [r_tmp], eng="pool")
                    RED(kst[q][:, 0:2], sqk[:, :].rearrange("p (g d) -> p g d", g=2), [r_tmp], [r_kst[q]])
                    TS(kst[q][:, 2:4], kst[q][:, 0:2], 1.0 / 64.0, EPS, ALU.mult, ALU.add, [r_kst[q]], [r_kst[q]])
                    ACT(kst[q][:, 4:6], kst[q][:, 2:4], AF.Sqrt, [r_kst[q]], [r_kst[q]])
                    RECIP(kst[q][:, 6:8], kst[q][:, 4:6], [r_kst[q]], [r_kst[q]])
                    for g in range(2):
                        TS(akn[q3][:, g * 64:(g + 1) * 64], akf[q][:, g * 64:(g + 1) * 64], kst[q][:, 6 + g:7 + g],
                           None, ALU.mult, None, [r_akf[q], r_kst[q]], [r_akn[q3]])
                    TT(akn[q3][:, :], akn[q3][:, :], kgt[:, :], ALU.mult, [r_akn[q3], r_w], [r_akn[q3]], eng="pool")
                    if isp:
                        DMA("sp", nak[pr0 + s * 128:pr0 + (s + 1) * 128, :], akn[q3][:, :], [r_akn[q3]], [],
                            "akn%d" % q3)
                        ksrc, rks = akn[q3], r_akn[q3]
                    else:
                        DMA("sp", ropeT[q][:, :, :], rope[tg0 + s * 128:tg0 + (s + 1) * 128, :, 0:128], [],
                            [r_rope[q]], "rope%d" % q)
                        xv_ = akn[q3][:, :].rearrange("p (a h d) -> p a h d", a=4, h=2)
                        sv_ = ropeT[q][:, 1, :].rearrange("p (a h d) -> p a h d", a=4, h=2)
                        t2v = t2[:, :].rearrange("p (a h d) -> p a h d", a=4, h=2)
                        TT(t1[:, :], akn[q3][:, :], ropeT[q][:, 0, :], ALU.mult, [r_akn[q3], r_rope[q]], [r_t1], eng="pool")
                        TT(t2v[:, :, 0, :], xv_[:, :, 1, :], sv_[:, :, 0, :], ALU.mult, [r_akn[q3], r_rope[q]], [r_t2], eng="pool")
                        TT(t2v[:, :, 1, :], xv_[:, :, 0, :], sv_[:, :, 1, :], ALU.mult, [r_akn[q3], r_rope[q]], [r_t2], eng="pool")
                        TT(akr[q3][:, :], t1[:, :], t2[:, :], ALU.add, [r_t1, r_t2], [r_akr[q3]], eng="pool")
                        ksrc, rks = akr[q3], r_akr[q3]

                    def k_tr(ksrc=ksrc, rks=rks, koff=koff, rk=rk):
                        b2 = rot()
                        for g in range(2):
                            TR(ps[b2][0:64, g * 128:(g + 1) * 128], ksrc[:, g * 64:(g + 1) * 64], idf[:, :],
                               [rks, r_c], [r_ps[b2]])
                        ACT(KTA[0:64, :, koff:koff + 128],
                            ps[b2][0:64, 0:256].rearrange("p (g n) -> p g n", g=2), AF.Copy, [r_ps[b2]], rk)
                    b = rot()
                    for c in range(8):
                        MM(ps[b][:, :], hT[k][:, c, s * 128:(s + 1) * 128], wBV[:, c, :], c == 0, c == 7, rh, [r_ps[b]])
                    pv4 = ps[b][:, :].rearrange("p (j e d) -> p j e d", j=4, e=2)
                    ACT(LB[:, kt, :, 0:64], pv4[:, :, 0, :], AF.Copy, [r_ps[b]], rk)
                    CP(LB[:, kt, :, 96:160], pv4[:, :, 1, :], [r_ps[b]], rk)
                    if isp:
                        sk = next_stg()
                        ACT(stg[sk][:, :], ps[b][:, :], AF.Copy, [r_ps[b]], [r_stg[sk]])
                        DMA("sp", nbv[pr0 + s * 128:pr0 + (s + 1) * 128, :], stg[sk][:, :], [r_stg[sk]], [],
                            "stg%d" % sk)
                        b = rot()
                        for c in range(8):
                            MM(ps[b][:, :], hT[k][:, c, s * 128:(s + 1) * 128], wBK[:, c, :], c == 0, c == 7, rh,
                               [r_ps[b]])
                        sk = next_stg()
                        CP(stg[sk][:, :], ps[b][:, :], [r_ps[b]], [r_stg[sk]])
                        DMA("sp", nbk[pr0 + s * 128:pr0 + (s + 1) * 128, :], stg[sk][:, :], [r_stg[sk]], [],
                            "stg%d" % sk)
                    pend_q.append(k_tr)
                    if len(pend_q) > 2:
                        pend_q.pop(0)()

                def stage2_tail(idx):
                    tg0, T, v, kb, isp, pr0 = tilesA[idx]
                    k = idx % 2
                    ns = T // 128
                    kts = [r_kt[(kb + s * 128) // 128] for s in range(ns)]
                    for j in range(4):
                        b = rot()
                        for c in range(8):
                            MM(ps[b][:, 0:T], wBK[:, c, j * 128:(j + 1) * 128], hT[k][:, c, 0:T], c == 0, c == 7,
                               r_hT[k][0:ns] + [r_w], [r_ps[b]])
                        if j % 2 == 0:
                            ACT(KTB[:, j, kb:kb + T], ps[b][:, 0:T], AF.Copy, [r_ps[b]], kts)
                        else:
                            CP(KTB[:, j, kb:kb + T], ps[b][:, 0:T], [r_ps[b]], kts)
                    while pend_q:
                        pend_q.pop(0)()

                nsA = tilesA[0][1] // 128
                loadA(0)
                if nT > 1:
                    loadA(1)
                chainA(0, 0)
                for s in range(nsA):
                    if s + 1 < nsA:
                        chainA(0, s + 1)
                    trA(0, s)
                bounceA(0)
                for idx in range(nT):
                    nxt = idx + 1 < nT
                    if idx + 2 < nT:
                        loadA(idx + 2)
                    if nxt:
                        chainA(idx + 1, 0)
                    for s in range(nsA):
                        stage2_sub(idx, s)
                        if nxt:
                            if s + 1 < nsA:
                                chainA(idx + 1, s + 1)
                            trA(idx + 1, s)
                    stage2_tail(idx)
                    if nxt:
                        bounceA(idx + 1)
                P.flush()

            tilesS = [(i * 512, 512, 0, 512 + i * 512, False, 0) for i in range(8)]
            tilesP = [(4096 + p * 256, 256, 1, p * 256, True, p * 256) for p in range(4)]
            qtS = [(i, 0, 512, False, i) for i in range(8)]
            qtP = [(8 + p // 2, (p % 2) * 256, 256, True, p) for p in range(4)]
            phaseA("s", tilesS, True)
            if stage < 2:
                if debug:
                    dbg = dout("dbg_kta", [128, 2, 4608], BF16)
                    dbg2 = dout("dbg_ktb", [128, 4, 4608], BF16)
                    dbg3 = dout("dbg_va", [128, 37 * 2 * 65], BF16)
                    dbg4 = dout("dbg_lb", [128, 36 * 4 * 160], BF16)
                    DMA("sp", dbg[:, :, :], KTA[:, :, :], [], [], "dbg0")
                    DMA("sp", dbg2[:, :, :], KTB[:, :, :], [], [], "dbg1")
                    DMA("sp", dbg3[:, :], VA[:, :, :, :].rearrange("p a b c -> p (a b c)"), [], [], "dbg2")
                    DMA("sp", dbg4[:, :], LB[:, :, :, :].rearrange("p a b c -> p (a b c)"), [], [], "dbg3")
                P.flush(final=True)
                return nc

            def phaseB(tag, qtiles):
              with ExitStack() as st0:
                _sb = sbuf
                def sbuf_(st_, name, shape, dt):
                    return _sb(st_, name + tag, shape, dt)
                st = st0
                wAQ = sbuf_(st, "wAQ", [128, 8, 512], BF16)
                wBQ = sbuf_(st, "wBQ", [128, 8, 512], BF16)
                r_w = P.res("wB")
                wv = w_in.rearrange("(c p) n -> p c n", p=128)
                DMA("pool", wAQ[:, :, :], wv[:, :, 0:512], [], [r_w], "w0")
                DMA("pool", wBQ[:, :, :], wv[:, :, 768:1280], [], [r_w], "w1")
                BT = sbuf_(st, "BT", [128, 8, 16, 64], BF16)
                DMA("pool", BT[:, :, :, :], btab.rearrange("p (h e q) -> p h e q", h=8, e=16), [], [r_w], "w2")
                qgt = sbuf_(st, "qgt", [128, 512], F32)
                DMA("sp", qgt[:, :], bass.AP(qg8.tensor, 0, [[0, 128], [1, 512]]), [], [r_w], "c5")

                hT = [sbuf_(st, "hTb%d" % k, [128, 8, 512], BF16) for k in range(1)] * 2
                r_hT = [P.res("hTb%d" % k) for k in range(1)] * 2
                QTA = sbuf_(st, "QTA", [128, 8, 512], BF16)
                r_qta = [P.res("qta%d" % s) for s in range(4)]
                QTBe = sbuf_(st, "QTBe", [128, 4, 512], BF16)
                QTBo = sbuf_(st, "QTBo", [128, 4, 512], BF16)
                r_qtb = [P.res("qtb%d" % j) for j in range(4)]
                MEMSET(QTA[64:128, :, :], 0.0, r_qta)
                MEMSET(QTBe[64:128, :, :], 0.0, r_qtb)
                MEMSET(QTBo[0:64, :, :], 0.0, r_qtb)
                PT = [sbuf_(st, "PT%d" % k, [128, 512], BF16) for k in range(4)]
                r_pt = [P.res("PT%d" % k) for k in range(4)]
                pt_i = [0]
                AOA = [sbuf_(st, "AOA%d" % k, [128, 8, 512], BF16) for k in range(1)] * 2
                r_aoa = [P.res("AOA%d" % k) for k in range(1)] * 2
                AOB = [sbuf_(st, "AOB%d" % k, [128, 4, 512], BF16) for k in range(1)] * 2
                r_aob = [P.res("AOB%d" % k) for k in range(1)] * 2
                ropeQ = [sbuf_(st, "ropeQ%d" % k, [128, 2, 512], F32) for k in range(1)] * 2
                r_rope = [P.res("ropeQ%d" % k) for k in range(1)] * 2
                aqf = [sbuf_(st, "aqf%d" % k, [128, 512], F32) for k in range(1)] * 2
                r_aqf = [P.res("aqf%d" % k) for k in range(1)] * 2
                aqn = [sbuf_(st, "aqn%d" % k, [128, 512], F32) for k in range(2)]
                r_aqn = [P.res("aqn%d" % k) for k in range(2)]
                tq1 = sbuf_(st, "tq1", [128, 512], F32)
                tq2 = sbuf_(st, "tq2", [128, 512], F32)
                qst = [sbuf_(st, "qst%d" % k, [128, 32], F32) for k in range(2)]
                r_tmp = P.res("tmpB")
                r_tq1 = P.res("tq1B")
                r_tq2 = P.res("tq2B")
                r_qst = [P.res("qstB%d" % k) for k in range(2)]
                oT = [sbuf_(st, "oT%d" % k, [128, 512], F32) for k in range(2)]
                r_oT = [P.res("oT%d" % k) for k in range(2)]
                rrow = [sbuf_(st, "rrow%d" % k, [128, 512], F32) for k in range(2)]
                r_rrow = [P.res("rrow%d" % k) for k in range(2)]
                fin_i = [0]
                deferred = []
                defer_n = [2]
                rotS = Rot([0, 1, 2, 3])
                rotO = Rot([4, 5])
                rotX = Rot([6, 7])

                def next_pt():
                    k = pt_i[0] % 4
                    pt_i[0] += 1
                    return k

                def finalize(bo, T, rows, dp, dst_ap, r_dst):
                    f = fin_i[0] % 2
                    fin_i[0] += 1
                    r0, r1 = rows
                    ACT(rrow[f][dp:dp + 1, 0:T], ps[bo][dp:dp + 1, 0:T], AF.Ln, [r_ps[bo]], [r_rrow[f]])
                    ACT(rrow[f][dp:dp + 1, 0:T], rrow[f][dp:dp + 1, 0:T], AF.Exp, [r_rrow[f]], [r_rrow[f]], scale=-1.0)
                    CP(oT[f][r0:r1, 0:T], ps[bo][r0:r1, 0:T], [r_ps[bo]], [r_oT[f]])

                    def part_b():
                        bx = rotX()
                        MM(ps[bx][:, 0:T], onesf[dp:dp + 1, :], rrow[f][dp:dp + 1, 0:T], True, True,
                           [r_rrow[f], r_c], [r_ps[bx]])
                        TT(dst_ap, oT[f][r0:r1, 0:T], ps[bx][r0:r1, 0:T], ALU.mult, [r_oT[f], r_ps[bx]], [r_dst])
                    deferred.append([defer_n[0], part_b])


                def loadB(qi):
                    ti, co, T, isp, sp_ = qtiles[qi]
                    k = qi % 2
                    DMA("sp", hT[k][:, :, 0:T], HT1[ti, :, :, co:co + T], [], [r_hT[k]], "hb0")

                loadB(0)
                for qi in range(len(qtiles)):
                    ti, co, T, isp, sp_ = qtiles[qi]
                    k = qi % 2
                    ns = T // 128
                    defer_n[0] = 2 if isp else 6
                    rh = [r_hT[k], r_w]
                    def aq_chain(s):
                        q = s % 2
                        b = rotS()
                        for c in range(8):
                            MM(ps[b][:, :], hT[k][:, c, s * 128:(s + 1) * 128], wAQ[:, c, :], c == 0, c == 7, rh, [r_ps[b]])
                        CP(aqf[q][:, :], ps[b][:, :], [r_ps[b]], [r_aqf[q]])
                        TT(tq1[:, :], aqf[q][:, :], aqf[q][:, :], ALU.mult, [r_aqf[q]], [r_tq1], eng="pool")
                        RED(qst[q][:, 0:8], tq1[:, :].rearrange("p (g d) -> p g d", g=8), [r_tq1], [r_qst[q]])
                        TS(qst[q][:, 8:16], qst[q][:, 0:8], 1.0 / 64.0, EPS, ALU.mult, ALU.add, [r_qst[q]], [r_qst[q]])
                        ACT(qst[q][:, 16:24], qst[q][:, 8:16], AF.Ln, [r_qst[q]], [r_qst[q]])
                        ACT(qst[q][:, 24:32], qst[q][:, 16:24], AF.Exp, [r_qst[q]], [r_qst[q]], scale=-0.5)
                        for h in range(8):
                            TS(aqn[q][:, h * 64:(h + 1) * 64], aqf[q][:, h * 64:(h + 1) * 64], qst[q][:, 24 + h:25 + h],
                               0.125, ALU.mult, ALU.mult, [r_aqf[q], r_qst[q]], [r_aqn[q]])
                        TT(aqn[q][:, :], aqn[q][:, :], qgt[:, :], ALU.mult, [r_aqn[q], r_w], [r_aqn[q]], eng="pool")
                        if not isp:
                            t0 = sp_ * 512 + s * 128
                            DMA("sp", ropeQ[q][:, :, :], rope[t0:t0 + 128, :, :], [], [r_rope[q]], "ropeq0")
                            xv_ = aqn[q][:, :].rearrange("p (a h d) -> p a h d", a=16, h=2)
                            sv_ = ropeQ[q][:, 1, :].rearrange("p (a h d) -> p a h d", a=16, h=2)
                            t2v = tq2[:, :].rearrange("p (a h d) -> p a h d", a=16, h=2)
                            TT(tq1[:, :], aqn[q][:, :], ropeQ[q][:, 0, :], ALU.mult, [r_aqn[q], r_rope[q]], [r_tq1], eng="pool")
                            TT(t2v[:, :, 0, :], xv_[:, :, 1, :], sv_[:, :, 0, :], ALU.mult, [r_aqn[q], r_rope[q]], [r_tq2], eng="pool")
                            TT(t2v[:, :, 1, :], xv_[:, :, 0, :], sv_[:, :, 1, :], ALU.mult, [r_aqn[q], r_rope[q]], [r_tq2], eng="pool")
                            TT(aqn[q][:, :], tq1[:, :], tq2[:, :], ALU.add, [r_tq1, r_tq2], [r_aqn[q]], eng="pool")

                    def aq_tr(s):
                        q = s % 2
                        for hb in range(2):
                            b2 = rotS()
                            for hh in range(4):
                                h = hb * 4 + hh
                                TR(ps[b2][0:64, hh * 128:(hh + 1) * 128], aqn[q][:, h * 64:(h + 1) * 64], idf[:, :],
                                   [r_aqn[q], r_c], [r_ps[b2]])
                            CP(QTA[0:64, hb * 4:hb * 4 + 4, s * 128:(s + 1) * 128],
                               ps[b2][0:64, :].rearrange("p (g n) -> p g n", g=4), [r_ps[b2]], [r_qta[s]])
                    for j in range(4):
                        b = rotS()
                        for c in range(8):
                            MM(ps[b][:, 0:T], wBQ[:, c, j * 128:(j + 1) * 128], hT[k][:, c, 0:T], c == 0, c == 7, rh,
                               [r_ps[b]])
                        TS(QTBe[0:64, j, 0:T], ps[b][0:64, 0:T], 0.125, None, ALU.mult, None, [r_ps[b]], [r_qtb[j]])
                        TS(QTBo[64:128, j, 0:T], ps[b][64:128, 0:T], 0.125, None, ALU.mult, None, [r_ps[b]], [r_qtb[j]])
                    steps = []

                    def add_dense_step(KT_ap, Q_ap, rd, V_ap, vr, bo, M, first, last, fin):
                        cell = {}

                        def front():
                            b_ = rotS()
                            MM(ps[b_][:, 0:T], KT_ap, Q_ap, True, True, rd, [r_ps[b_]])
                            pk = next_pt()
                            cell["pk"] = pk
                            ACT(PT[pk][:, 0:T], ps[b_][:, 0:T], AF.Exp, [r_ps[b_]], [r_pt[pk]])

                        def back():
                            pk = cell["pk"]
                            MM(ps[bo][0:M, 0:T], V_ap, PT[pk][:, 0:T], first, last, vr + [r_pt[pk]], [r_ps[bo]])
                            if fin is not None:
                                fin()
                        steps.append((front, back))

                    def add_local_step(blocks, Qcols, rq_, h_, bo, M, fin):
                        cell = {}
                        cnt = len(blocks)

                        def front():
                            b_ = rotS()
                            for jj, (KT_ap, rk_, e_, V_ap) in enumerate(blocks):
                                MM(ps[b_][:, jj * 64:(jj + 1) * 64], KT_ap, Qcols, True, False, [rk_] + rq_, [r_ps[b_]])
                                MM(ps[b_][:, jj * 64:(jj + 1) * 64], idb[:, :], BT[:, h_, e_, :], False, True,
                                   [r_c, r_w], [r_ps[b_]])
                            pk = next_pt()
                            cell["pk"] = pk
                            ACT(PT[pk][:, 0:cnt * 64], ps[b_][:, 0:cnt * 64], AF.Exp, [r_ps[b_]], [r_pt[pk]])

                        def back():
                            pk = cell["pk"]
                            for jj, (KT_ap, rk_, e_, V_ap) in enumerate(blocks):
                                MM(V_ap[0], V_ap[1], PT[pk][:, jj * 64:(jj + 1) * 64], False, jj == cnt - 1,
                                   [rk_, r_pt[pk]], [r_ps[bo]])
                            if fin is not None:
                                fin()
                        steps.append((front, back))

                    def mkfin(bo, rows, dp, dst, r_dst):
                        return lambda: finalize(bo, T, rows, dp, dst, r_dst)

                    steps = []
                    if isp:
                        ktl = [2 * sp_, 2 * sp_ + 1]
                    else:
                        ktl = list(range(36))
                    for h in range(8):
                        g = h // 4
                        bo = rotO()
                        for n_, kt in enumerate(ktl):
                            last = n_ == len(ktl) - 1
                            v0 = VA[:, kt, g, 0:1]
                            vfull = bass.AP(v0.tensor, v0.offset, [[v0.ap[0][0], 128], [1, 128]])
                            add_dense_step(KTA[:, g, kt * 128:(kt + 1) * 128], QTA[:, h, 0:T],
                                           [r_kt[kt]] + r_qta[0:ns], vfull, [r_kt[kt]], bo, 128,
                                           n_ == 0, last,
                                           mkfin(bo, (0, 64), 64, AOA[k][0:64, h, 0:T], r_aoa[k]) if last else None)
                    stepsA = steps
                    steps = []
                    head_end = []
                    for h in range(8):
                        j = h // 2
                        half = h % 2
                        P0 = 64 * half
                        if half == 0:
                            l0, l1, dp, M = 0, 128, 64, 128
                            QTB = QTBe
                        else:
                            l0, l1, dp, M = 32, 160, 32, 128
                            QTB = QTBo
                        bo = rotO()
                        rq = [r_qtb[j]]
                        fin = mkfin(bo, (P0, P0 + 64), dp, AOB[k][P0:P0 + 64, j, 0:T], r_aob[k])
                        if isp:
                            ktl = [2 * sp_, 2 * sp_ + 1]
                        else:
                            ktl = [0, 1, 2, 3]
                        for n_, kt in enumerate(ktl):
                            last = isp and n_ == len(ktl) - 1
                            add_dense_step(KTB[:, j, kt * 128:(kt + 1) * 128], QTB[:, j, 0:T],
                                           [r_kt[kt]] + rq, LB[:, kt, j, l0:l1], [r_kt[kt]], bo, M, n_ == 0, last,
                                           fin if last else None)
                        if not isp:
                            for rr in range(8):
                                r = sp_ * 8 + rr
                                rs = min(max(r - 4, 0), 56)
                                if rs % 2 == 1:
                                    n0, cnt = rs - 1, 5
                                else:
                                    n0, cnt = rs, 4
                                blocks = []
                                for jj in range(cnt):
                                    n = n0 + 2 * jj
                                    kt = 4 + n // 2
                                    off = 512 + n * 64
                                    e_ = n - r + 7
                                    if cnt == 5 and jj == 0:
                                        e_ = 14
                                    elif cnt == 5 and jj == 4:
                                        e_ = 15
                                    blocks.append((KTB[:, j, off:off + 128], r_kt[kt], e_,
                                                   (ps[bo][0:M, rr * 64:(rr + 1) * 64], LB[:, kt, j, l0:l1])))
                                add_local_step(blocks, QTB[:, j, rr * 64:(rr + 1) * 64], rq, h, bo, M,
                                               fin if rr == 7 else None)
                    stepsB = steps
                    nop = lambda: None
                    spb = len(stepsB) // 8
                    inj = {}
                    if qi + 1 < len(qtiles):
                        pre = [lambda qn=qi + 1: loadB(qn)]
                    else:
                        pre = []
                    if ns == 4:
                        inj = {2: [lambda: aq_tr(0), lambda: aq_chain(2)],
                               4: [lambda: aq_tr(1), lambda: aq_chain(3)] + pre,
                               6: [lambda: aq_tr(2)], 8: [lambda: aq_tr(3)]}
                    else:
                        inj = {1: pre, 4: [lambda: aq_tr(0)], 8: [lambda: aq_tr(1)]}
                    steps = [(lambda: aq_chain(0), nop), (lambda: aq_chain(1), nop)]
                    for hh_ in range(8):
                        steps += stepsB[hh_ * spb:(hh_ + 1) * spb]
                        for fn_ in inj.get(hh_ + 1, []):
                            steps.append((fn_, nop))
                    steps += stepsA
                    LA = 3
                    for i_ in range(len(steps) + LA):
                        if i_ < len(steps):
                            steps[i_][0]()
                        for d_ in deferred:
                            d_[0] -= 1
                        while deferred and deferred[0][0] <= 0:
                            deferred.pop(0)[1]()
                        if i_ >= LA:
                            steps[i_ - LA][1]()
                    while deferred:
                        deferred.pop(0)[1]()
                    DMA("sp", AOAd[ti, :, :, co:co + T], AOA[k][0:64, :, 0:T], [r_aoa[k]], [], "aoa0")
                    DMA("sp", AOBd[ti, :, :, co:co + T], AOB[k][:, :, 0:T], [r_aob[k]], [], "aob0")
                P.flush()

            phaseB("s", qtS)
            phaseA("p", tilesP, False)
            phaseB("p", qtP)

        if stage < 3:
            P.flush(final=True)
            return nc

        with ExitStack() as st:
            wGA = sbuf(st, "wGA", [128, 8, 1024], BF16)
            wGB = sbuf(st, "wGB", [128, 8, 1024], BF16)
            wBRA = sbuf(st, "wBRA", [128, 8, 1024], BF16)
            wBRB = sbuf(st, "wBRB", [128, 4, 1024], BF16)
            wOUT = sbuf(st, "wOUT", [128, 8, 1024], BF16)
            r_w = P.res("wC1")
            wv = w_in.rearrange("(c p) n -> p c n", p=128)
            r_wga, r_wgb, r_wbra, r_wbrb, r_wout = [P.res("wc1_%d" % i_) for i_ in range(5)]
            DMA("pool", wGA[:, :, :], wv[:, :, 2304:3328], [], [r_wga], "w0")
            DMA("pool", wBRA[0:64, :, :], w_br_a.rearrange("(h d) n -> d h n", d=64), [], [r_wbra], "w2")
            DMA("pool", wGB[:, :, :], wv[:, :, 3328:4352], [], [r_wgb], "w1")
            DMA("pool", wBRB[:, :, :], w_br_b.rearrange("(c p) n -> p c n", p=128), [], [r_wbrb], "w3")
            DMA("pool", wOUT[:, :, :], w_out.rearrange("(c p) n -> p c n", p=128), [], [r_wout], "w4")
            G1 = [sbuf(st, "G1_%d" % v, [128, 1024], F32) for v in range(2)]
            for v in range(2):
                DMA("sp", G1[v][:, :], bass.AP(GMOD.tensor, v * 1024, [[0, 128], [1, 1024]]), [], [r_w], "c%d" % (5 + v))
            hT = [sbuf(st, "hTc%d" % k, [128, 8, 512], BF16) for k in range(2)]
            AOA = [sbuf(st, "AOAc%d" % k, [128, 8, 512], BF16) for k in range(2)]
            AOB = [sbuf(st, "AOBc%d" % k, [128, 4, 512], BF16) for k in range(2)]
            xsb = [sbuf(st, "xsb%d" % k, [128, 1024], F32) for k in range(4)]
            r_xsb = [P.res("xsb%d" % k) for k in range(4)]
            r_in = [P.res("inC%d" % k) for k in range(2)]
            sg = [sbuf(st, "sg%d" % k, [128, 512], F32) for k in range(2)]
            r_sg = [P.res("sg%d" % k) for k in range(2)]
            m1 = [sbuf(st, "m1_%d" % k, [128, 512], F32) for k in range(2)]
            r_m1 = [P.res("m1_%d" % k) for k in range(2)]
            m2 = [sbuf(st, "m2_%d" % k, [128, 512], F32) for k in range(2)]
            r_m2 = [P.res("m2_%d" % k) for k in range(2)]
            MT = [sbuf(st, "MT%d" % k, [128, 8, 512], BF16) for k in range(2)]
            r_mt = [[P.res("MT%d_%d" % (k, f)) for f in range(8)] for k in range(2)]
            x1 = [sbuf(st, "x1_%d" % k, [128, 1024], F32) for k in range(2)]
            r_x1 = [P.res("x1_%d" % k) for k in range(2)]
            tt_ = [sbuf(st, "ttc%d" % k, [128, 512], F32) for k in range(2)]
            r_tt = [P.res("ttc%d" % k) for k in range(2)]
            junk = sbuf(st, "junkc", [128, 1024], BF16)
            r_junk = P.res("junkc")
            stat = [sbuf(st, "statc%d" % k, [128, 4], F32) for k in range(2)]
            r_stat = [P.res("statc%d" % k) for k in range(2)]
            xn = [sbuf(st, "xnc%d" % k, [128, 1024], F32) for k in range(2)]
            r_xn = [P.res("xnc%d" % k) for k in range(2)]
            h2T = [sbuf(st, "h2T%d" % k, [128, 8, 512], BF16) for k in range(2)]
            r_h2 = [[P.res("h2T%d_%d" % (k, s)) for s in range(4)] for k in range(2)]
            rot = Rot([0, 1, 2, 3, 4, 5, 6, 7])

            def loadC1(i):
                k = i % 2
                DMA("sp", hT[k][:, :, :], HT1[i, :, :, :], [], [r_in[k]], "c1h%d" % k)
                DMA("sp", AOA[k][0:64, :, :], AOAd[i, :, :, :], [], [r_in[k]], "c1a%d" % k)
                DMA("sp", AOB[k][:, :, :], AOBd[i, :, :, :], [], [r_in[k]], "c1b%d" % k)

            def loadX(i):
                tg0, v = tile_info(i)
                for s_ in range(4):
                    DMA("sp", xsb[s_][:, :], xrows(tg0 + s_ * 128, 128), [], [r_xsb[s_]], "c1x%d" % s_)

            gi_ = [0]

            def fchunk(i, f):
                k = i % 2
                fs = slice(f * 128, (f + 1) * 128)
                b = rot()
                for c in range(8):
                    MM(ps[b][:, :], wGA[:, c, fs], hT[k][:, c, :], c == 0, c == 7, [r_in[k], r_wga], [r_ps[b]])
                ga = gi_[0] % 2
                gi_[0] += 1
                ACT(sg[ga][:, :], ps[b][:, :], AF.Sigmoid, [r_ps[b]], [r_sg[ga]])
                b = rot()
                for h in range(8):
                    MM(ps[b][:, :], wBRA[0:64, h, fs], AOA[k][0:64, h, :], h == 0, h == 7, [r_in[k], r_wbra], [r_ps[b]])
                mi = f % 2
                TT(m1[mi][:, :], sg[ga][:, :], ps[b][:, :], ALU.mult, [r_sg[ga], r_ps[b]], [r_m1[mi]])
                b = rot()
                for c in range(8):
                    MM(ps[b][:, :], wGB[:, c, fs], hT[k][:, c, :], c == 0, c == 7, [r_in[k], r_wgb], [r_ps[b]])
                gb = gi_[0] % 2
                gi_[0] += 1
                ACT(sg[gb][:, :], ps[b][:, :], AF.Sigmoid, [r_ps[b]], [r_sg[gb]])
                b = rot()
                for j in range(4):
                    MM(ps[b][:, :], wBRB[:, j, fs], AOB[k][:, j, :], j == 0, j == 3, [r_in[k], r_wbrb], [r_ps[b]])
                TT(m2[mi][:, :], sg[gb][:, :], ps[b][:, :], ALU.mult, [r_sg[gb], r_ps[b]], [r_m2[mi]])
                TT(MT[k][:, f, :], m1[mi][:, :], m2[mi][:, :], ALU.add, [r_m1[mi], r_m2[mi]], [r_mt[k][f]], eng="pool")

            def outproj(i, s):
                tg0, v = tile_info(i)
                k = i % 2
                q = s % 2
                for hh in range(2):
                    b = rot()
                    for c in range(8):
                        MM(ps[b][:, :], MT[k][:, c, s * 128:(s + 1) * 128], wOUT[:, c, hh * 512:(hh + 1) * 512],
                           c == 0, c == 7, r_mt[k] + [r_wout], [r_ps[b]])
                    TT(tt_[hh][:, :], ps[b][:, :], G1[v][:, hh * 512:(hh + 1) * 512], ALU.mult, [r_ps[b], r_w],
                       [r_tt[hh]])
                    TT(x1[q][:, hh * 512:(hh + 1) * 512], tt_[hh][:, :], xsb[s][:, hh * 512:(hh + 1) * 512],
                       ALU.add, [r_tt[hh], r_xsb[s]], [r_x1[q]], eng="pool")
                DMA("sp", X1[tg0 + s * 128:tg0 + (s + 1) * 128, :], x1[q][:, :], [r_x1[q]], [], "x1s%d" % q)

            def chainC(i, s):
                q = s % 2
                hT_chain(x1[q][:, :], r_x1[q], junk, r_junk, stat[q], r_stat[q], xn[q], r_xn[q])

            def trC(i, s):
                tg0, v = tile_info(i)
                k = i % 2
                q = s % 2
                hT_tr(xn[q], r_xn[q], lambda c, k=k, s=s: h2T[k][:, c, s * 128:(s + 1) * 128], r_h2[k][s],
                      A2, SH2, v, rot)

            def storeH2(i):
                k = i % 2
                DMA("sp", HT2[i, :, :, :], h2T[k][:, :, :], r_h2[k], [], "h2s%d" % k)

            def tail_pieces(i):
                return [
                    [lambda: outproj(i, 0)],
                    [lambda: chainC(i, 0), lambda: outproj(i, 1)],
                    [lambda: trC(i, 0), lambda: chainC(i, 1)],
                    [lambda: outproj(i, 2)],
                    [lambda: trC(i, 1), lambda: chainC(i, 2)],
                    [lambda: outproj(i, 3)],
                    [lambda: trC(i, 2), lambda: chainC(i, 3)],
                    [lambda: trC(i, 3), lambda: storeH2(i)],
                ]

            loadC1(0)
            loadC1(1)
            for f in range(8):
                fchunk(0, f)
            for i in range(10):
                if i + 2 < 10:
                    loadC1(i + 2)
                loadX(i)
                pieces = tail_pieces(i)
                for f in range(8):
                    if i + 1 < 10:
                        fchunk(i + 1, f)
                    for fn_ in pieces[f]:
                        fn_()
            P.flush()

        if stage < 4:
            P.flush(final=True)
            return nc

        with ExitStack() as st:
            W1 = sbuf(st, "W1", [128, 8, 4096], BF16)
            W2 = sbuf(st, "W2", [128, 32, 1024], BF16)
            r_w = P.res("wC2")
            w1v = w1.rearrange("(c p) n -> p c n", p=128)
            w2v = w2.rearrange("(c p) n -> p c n", p=128)
            r_w1 = [P.res("w1_%d" % q4) for q4 in range(4)]
            r_w2 = [P.res("w2_%d" % q4) for q4 in range(4)]
            for q4 in range(4):
                DMA("pool", W1[:, :, q4 * 1024:(q4 + 1) * 1024], w1v[:, :, q4 * 1024:(q4 + 1) * 1024], [], [r_w1[q4]],
                    "w%d" % q4)
            for q4 in range(4):
                DMA("pool", W2[:, q4 * 8:(q4 + 1) * 8, :], w2v[:, q4 * 8:(q4 + 1) * 8, :], [], [r_w2[q4]],
                    "w%d" % (4 + q4))
            G2 = [sbuf(st, "G2_%d" % v, [128, 1024], F32) for v in range(2)]
            GF = sbuf(st, "GF", [128, 1024], F32)
            for v in range(2):
                DMA("sp", G2[v][:, :], bass.AP(GMOD.tensor, (2 + v) * 1024, [[0, 128], [1, 1024]]), [], [r_w], "c%d" % (5 + v))
            DMA("sp", GF[:, :], bass.AP(gf.tensor, 0, [[0, 128], [1, 1024]]), [], [r_w], "c7")
            h2T = [sbuf(st, "h2d%d" % k, [128, 8, 256], BF16) for k in range(2)]
            x1t = [sbuf(st, "x1d%d" % k, [128, 2, 1024], F32) for k in range(2)]
            r_in = [P.res("inD%d" % k) for k in range(2)]
            UT = sbuf(st, "UT", [128, 32, 256], BF16)
            r_ut = [P.res("UT%d" % f) for f in range(16)]
            rl = [sbuf(st, "rl%d" % k, [128, 512], F32) for k in range(2)]
            r_rl = [P.res("rl%d" % k) for k in range(2)]
            tt_ = [sbuf(st, "ttd%d" % k, [128, 512], F32) for k in range(2)]
            r_tt = [P.res("ttd%d" % k) for k in range(2)]
            x2 = [sbuf(st, "x2_%d" % k, [128, 1024], F32) for k in range(2)]
            r_x2 = [P.res("x2_%d" % k) for k in range(2)]
            yo = [sbuf(st, "yo%d" % k, [128, 1024], F32) for k in range(2)]
            r_yo = [P.res("yo%d" % k) for k in range(2)]
            junk = sbuf(st, "junkd", [128, 1024], BF16)
            r_junk = P.res("junkd")
            stat = [sbuf(st, "statd%d" % k, [128, 4], F32) for k in range(2)]
            r_stat = [P.res("statd%d" % k) for k in range(2)]
            rot = Rot([0, 1, 2, 3, 4, 5, 6, 7])

            def loadC2(i):
                k = i % 2
                tg0 = i * 256
                DMA("sp", h2T[k][:, :, :], HT2[i // 2, :, :, (i % 2) * 256:(i % 2) * 256 + 256], [], [r_in[k]],
                    "c2h%d" % k)
                DMA("sp", x1t[k][:, :, :], X1[tg0:tg0 + 256, :].rearrange("(s p) d -> p s d", p=128), [], [r_in[k]],
                    "c2x%d" % k)

            loadC2(0)
            yi = 0
            for i in range(20):
                k = i % 2
                tg0 = i * 256
                v = 0 if tg0 < 4096 else 1
                if i + 1 < 20:
                    loadC2(i + 1)
                rin = [r_in[k], r_w]
                for fp in range(16):
                    b = rot()
                    for e2 in range(2):
                        f = fp * 2 + e2
                        for c in range(8):
                            MM(ps[b][:, e2 * 256:(e2 + 1) * 256], W1[:, c, f * 128:(f + 1) * 128], h2T[k][:, c, :],
                               c == 0, c == 7, [r_in[k], r_w1[f // 8]], [r_ps[b]])
                    a = fp % 2
                    ACT(rl[a][:, :], ps[b][:, :], AF.Relu, [r_ps[b]], [r_rl[a]])
                    TT(UT[:, fp * 2:fp * 2 + 2, :], rl[a][:, :].rearrange("p (e n) -> p e n", e=2),
                       rl[a][:, :].rearrange("p (e n) -> p e n", e=2), ALU.mult, [r_rl[a]], [r_ut[fp]],
                       eng=("pool" if fp % 2 else "dve"))
                for s in range(2):
                    q = yi % 2
                    yi += 1
                    for hh in range(2):
                        b = rot()
                        for f in range(32):
                            MM(ps[b][:, :], UT[:, f, s * 128:(s + 1) * 128], W2[:, f, hh * 512:(hh + 1) * 512],
                               f == 0, f == 31, r_ut + [r_w2[f // 8]], [r_ps[b]])
                        TT(tt_[hh][:, :], ps[b][:, :], G2[v][:, hh * 512:(hh + 1) * 512], ALU.mult, [r_ps[b], r_w],
                           [r_tt[hh]])
                        TT(x2[q][:, hh * 512:(hh + 1) * 512], tt_[hh][:, :], x1t[k][:, s, hh * 512:(hh + 1) * 512],
                           ALU.add, [r_tt[hh], r_in[k]], [r_x2[q]], eng="pool")
                    ACT(junk[:, :], x2[q][:, :], AF.Square, [r_x2[q]], [r_junk, r_stat[q]], scale=1.0 / 32.0,
                        accum=stat[q][:, 0:1])
                    TS(stat[q][:, 1:2], stat[q][:, 0:1], EPS, None, ALU.add, None, [r_stat[q]], [r_stat[q]])
                    ACT(stat[q][:, 2:3], stat[q][:, 1:2], AF.Sqrt, [r_stat[q]], [r_stat[q]])
                    RECIP(stat[q][:, 3:4], stat[q][:, 2:3], [r_stat[q]], [r_stat[q]])
                    STT(yo[q][:, :], x2[q][:, :], stat[q][:, 3:4], GF[:, :], ALU.mult, ALU.mult,
                        [r_x2[q], r_stat[q], r_w], [r_yo[q]])
                    DMA("sp", yrows(tg0 + s * 128, 128), yo[q][:, :], [r_yo[q]], [], "yo%d" % q)
            P.flush(final=True)
    return nc


def _rope_tables():
    pos = np.arange(4096)
    row = (pos // 64).astype(np.float32)
    col = (pos % 64).astype(np.float32)
    inv = (np.float32(10000.0) ** (-np.arange(16, dtype=np.float32) / np.float32(16))).astype(np.float32)
    ar = row[:, None] * inv
    ac = col[:, None] * inv
    cr, sr, cc, sc = np.cos(ar), np.sin(ar), np.cos(ac), np.sin(ac)
    C = np.concatenate([cr, cr, cc, cc], axis=1).astype(np.float32)
    S = np.concatenate([-sr, sr, -sc, sc], axis=1).astype(np.float32)
    out = np.empty((4096, 2, 512), np.float32)
    out[:, 0, :] = np.tile(C, (1, 8))
    out[:, 1, :] = np.tile(S, (1, 8))
    return out


def _bias_table(nat_bias):
    nb = nat_bias[0]
    kc = np.arange(64)[:, None]
    qc = np.arange(64)[None, :]
    cstart = np.clip(qc - 8, 0, 48)
    inwin = (kc >= cstart) & (kc < cstart + 16)
    dc = np.clip(kc - qc + 15, 0, 30)
    tab = np.full((2, 64, 8, 16, 64), NEG, np.float32)
    def blk(d):
        g = nb[:, d, :][:, dc]
        return np.where(inwin[None], g, np.float32(NEG)).transpose(1, 0, 2)
    for e in range(14):
        tab[0, :, :, e, :] = blk(e)
        tab[1, :, :, e, :] = blk(e + 1)
    tab[1, :, :, 14, :] = blk(3)
    tab[0, :, :, 15, :] = blk(10)
    return np.ascontiguousarray(tab.reshape(128, 8 * 16 * 64))


_NC_CACHE = {}


def make_in_maps(x_prompt, x_sample, cache_a_k, cache_a_v, cache_b_k, cache_b_v, c, c_ctx,
                 w_mod, b_mod, norm1_g, norm2_g, w_in, q_norm_g, k_norm_g, nat_bias,
                 w_br_a, w_br_b, w_out, w_mlp_in, w_mlp_out, final_norm_g):
    f = lambda a: np.ascontiguousarray(np.asarray(a, dtype=np.float32))
    rope = _rope_tables()
    btab = _bias_table(f(nat_bias))
    bm = f(b_mod)[0].reshape(48, 128).T
    shared = {
        "w_mod": f(w_mod)[0], "bmodT2": np.ascontiguousarray(np.repeat(bm, 2, axis=1)),
        "n1T2": np.ascontiguousarray(np.repeat(f(norm1_g)[0].reshape(8, 128).T, 2, axis=1)),
        "n2T2": np.ascontiguousarray(np.repeat(f(norm2_g)[0].reshape(8, 128).T, 2, axis=1)),
        "w_in": f(w_in)[0], "qg8": np.ascontiguousarray(np.tile(f(q_norm_g)[0], 8)),
        "kg2": np.ascontiguousarray(np.tile(f(k_norm_g)[0], 2)), "btab": btab,
        "w_br_a": f(w_br_a)[0], "w_br_b": f(w_br_b)[0], "w_out": f(w_out)[0],
        "w1": f(w_mlp_in)[0], "w2": f(w_mlp_out)[0], "gf": f(final_norm_g),
        "ident": np.eye(128, dtype=np.float32), "rope": rope,
    }
    xs_, xp_ = f(x_sample), f(x_prompt)
    cc = f(c_ctx)
    maps = []
    for b in range(8):
        cT = np.empty((128, 8, 2), np.float32)
        cT[:, :, 0] = f(c)[b].reshape(8, 128).T
        cT[:, :, 1] = cc.reshape(8, 128).T
        m = dict(shared)
        m.update({
            "xs": xs_[b], "xp": np.ascontiguousarray(xp_[4 * b:4 * b + 4].reshape(1024, 1024)),
            "cak": np.ascontiguousarray(f(cache_a_k)[b, 0].reshape(512, 128)),
            "cav": np.ascontiguousarray(f(cache_a_v)[b, 0].reshape(512, 128)),
            "cbk": np.ascontiguousarray(f(cache_b_k)[b, 0].reshape(512, 512)),
            "cbv": np.ascontiguousarray(f(cache_b_v)[b, 0].reshape(512, 512)),
            "cT": np.ascontiguousarray(cT.reshape(128, 16)),
        })
        maps.append(m)
    return maps


def kernel(**inputs):
    if "nc" not in _NC_CACHE:
        _NC_CACHE["nc"] = build()
    nc = _NC_CACHE["nc"]
    maps = make_in_maps(**inputs)
    res = run_bass_kernel_spmd(nc, maps, core_ids=list(range(8)))
    R = res.results
    y_sample = np.stack([R[b]["ys"] for b in range(8)], axis=0)
    y_prompt = np.concatenate([R[b]["yp"].reshape(4, 256, 1024) for b in range(8)], axis=0)
    nak = np.concatenate([R[b]["nak"].reshape(4, 1, 256, 2, 64) for b in range(8)], axis=0)
    nav = np.concatenate([R[b]["nav"].reshape(4, 1, 256, 2, 64) for b in range(8)], axis=0)
    nbk = np.concatenate([R[b]["nbk"].reshape(4, 1, 256, 8, 64) for b in range(8)], axis=0)
    nbv = np.concatenate([R[b]["nbv"].reshape(4, 1, 256, 8, 64) for b in range(8)], axis=0)
    return (y_prompt.astype(np.float32), y_sample.astype(np.float32), nak.astype(np.float32),
            nav.astype(np.float32), nbk.astype(np.float32), nbv.astype(np.float32))
```

```python
from contextlib import ExitStack
import numpy as np
import concourse.bass as bass
import concourse.mybir as mybir
from concourse.bass_utils import run_bass_kernel_spmd

F32 = mybir.dt.float32
BF16 = mybir.dt.bfloat16
AF = mybir.ActivationFunctionType
ALU = mybir.AluOpType
AX = mybir.AxisListType

ENGS = ("pe", "act", "dve", "pool", "sp")
EPS = 1e-6
NEG = -30000.0


class Res:
    __slots__ = ("name", "w", "r", "excl")

    def __init__(self, name, excl=False):
        self.name = name
        self.w = {}
        self.r = []
        self.excl = excl


class Ins:
    __slots__ = ("eng", "fn", "deps", "dma", "stream", "flag", "sem", "val", "waits", "clock")

    def __init__(self, eng, fn, dma, stream):
        self.eng = eng
        self.fn = fn
        self.deps = []
        self.dma = dma
        self.stream = stream
        self.flag = False
        self.sem = None
        self.val = 0
        self.waits = []
        self.clock = None


class Prog:
    def __init__(self, nc, stack):
        self.nc = nc
        self.stack = stack
        self.pending = []
        self.esem = {}
        for e in ENGS[:4]:
            self.esem[e] = stack.enter_context(nc.semaphore("sem_" + e))
        self.ecount = {e: 0 for e in ENGS}
        self.ssem = {}
        self.scount = {}
        self.know = {e: {} for e in ENGS}
        self.last = {e: None for e in ENGS}
        self.last_dma = {}
        self.barrier_deps = {e: [] for e in ENGS}
        self.all_res = []
        self.n_ins = 0
        self.n_wait = 0

    def res(self, name, excl=False):
        r = Res(name, excl)
        self.all_res.append(r)
        return r

    def add(self, eng, fn, reads=(), writes=(), dma=False, stream=None):
        ins = Ins(eng, fn, dma, stream)
        if dma:
            ins.flag = True
        writes = list(writes) + [r for r in reads if r.excl]
        reads = [r for r in reads if not r.excl]
        deps = []
        for r in reads:
            for w in r.w.values():
                deps.append((w, "raw"))
        for r in writes:
            for w in r.w.values():
                deps.append((w, "waw"))
            for rd in r.r:
                deps.append((rd, "war"))
        for d in self.barrier_deps[eng]:
            deps.append((d, "raw"))
        self.barrier_deps[eng] = []
        seen = set()
        for d, kind in deps:
            if d is ins or id(d) in seen:
                continue
            if (not d.dma) and (not dma) and d.eng == eng and eng == "pe":
                continue
            seen.add(id(d))
            d.flag = True
            ins.deps.append(d)
        for r in reads:
            r.r.append(ins)
        key = ("d", stream) if dma else eng
        for r in writes:
            r.w[key] = ins
            r.r = []
        self.pending.append(ins)
        if dma:
            self.last_dma[stream] = ins
        else:
            self.last[eng] = ins
        return ins

    def barrier(self):
        alls = [i for i in self.last.values() if i is not None] + list(self.last_dma.values())
        for e in ENGS:
            self.barrier_deps[e] = list(alls)

    @staticmethod
    def _semkey(ins):
        return ("s", ins.stream) if ins.dma else ("e", ins.eng)

    def flush(self, final=False):
        nc = self.nc
        lasts = [i for i in self.last.values() if i is not None] + list(self.last_dma.values())
        for d in lasts:
            if d.val == 0:
                d.flag = True
        if final:
            fin = Ins("sp", None, False, None)
            fin.deps = lasts
            self.pending.append(fin)
        per = {e: [] for e in ENGS}
        for ins in self.pending:
            e = ins.eng
            K = self.know[e]
            for d in ins.deps:
                key = self._semkey(d)
                assert d.val > 0, "dep not yet numbered"
                if K.get(key, 0) >= d.val:
                    continue
                ins.waits.append((d.sem, d.val))
                for k2, v2 in d.clock.items():
                    if K.get(k2, 0) < v2:
                        K[k2] = v2
            if ins.flag:
                if ins.dma:
                    if ins.stream not in self.ssem:
                        self.ssem[ins.stream] = self.stack.enter_context(
                            nc.semaphore("sd_%d" % len(self.ssem)))
                        self.scount[ins.stream] = 0
                    self.scount[ins.stream] += 16
                    ins.sem = self.ssem[ins.stream]
                    ins.val = self.scount[ins.stream]
                else:
                    self.ecount[e] += 1
                    ins.sem = self.esem[e]
                    ins.val = self.ecount[e]
                ck = dict(K)
                ck[self._semkey(ins)] = ins.val
                ins.clock = ck
            per[e].append(ins)
            self.n_ins += 1
            self.n_wait += len(ins.waits)
        self.pending = []
        for r in self.all_res:
            r.w = {}
            r.r = []
        self.barrier()

        def replay(lst):
            def f(eng):
                for ins in lst:
                    for (s, v) in ins.waits:
                        eng.wait_ge(s, v)
                    if ins.fn is None:
                        continue
                    r = ins.fn(eng)
                    if ins.flag:
                        r.then_inc(ins.sem, 16 if ins.dma else 1)
            return f

        with nc.Block() as block:
            if per["sp"]:
                block.sync(replay(per["sp"]))
            if per["pool"]:
                block.gpsimd(replay(per["pool"]))
            if per["act"]:
                block.scalar(replay(per["act"]))
            if per["dve"]:
                block.vector(replay(per["dve"]))
            if per["pe"]:
                block.tensor(replay(per["pe"]))


def build(stage=99, debug=False):
    nc = bass.Bass("TRN2", target_bir_lowering=False)

    def din(name, shape, dt=F32):
        return nc.dram_tensor(name, list(shape), dt, kind="ExternalInput").ap()

    def dout(name, shape, dt=F32):
        return nc.dram_tensor(name, list(shape), dt, kind="ExternalOutput").ap()

    def dscr(name, shape, dt):
        return nc.dram_tensor(name, list(shape), dt).ap()

    xs = din("xs", [4096, 1024])
    xp = din("xp", [1024, 1024])
    cak = din("cak", [512, 128])
    cav = din("cav", [512, 128])
    cbk = din("cbk", [512, 512])
    cbv = din("cbv", [512, 512])
    cT = din("cT", [128, 16])
    w_mod = din("w_mod", [1024, 6144])
    bmodT2 = din("bmodT2", [128, 96])
    n1T2 = din("n1T2", [128, 16])
    n2T2 = din("n2T2", [128, 16])
    w_in = din("w_in", [1024, 4352])
    qg8 = din("qg8", [512])
    kg2 = din("kg2", [128])
    btab = din("btab", [128, 8 * 16 * 64])
    w_br_a = din("w_br_a", [512, 1024])
    w_br_b = din("w_br_b", [512, 1024])
    w_out = din("w_out", [1024, 1024])
    w1 = din("w1", [1024, 4096])
    w2 = din("w2", [4096, 1024])
    gf = din("gf", [1024])
    ident = din("ident", [128, 128])
    rope = din("rope", [4096, 2, 512])

    ys = dout("ys", [4096, 1024])
    yp = dout("yp", [1024, 1024])
    nak = dout("nak", [1024, 128])
    nav = dout("nav", [1024, 128])
    nbk = dout("nbk", [1024, 512])
    nbv = dout("nbv", [1024, 512])

    HT1 = dscr("HT1", [10, 128, 8, 512], BF16)
    HT2 = dscr("HT2", [10, 128, 8, 512], BF16)
    AOAd = dscr("AOAd", [10, 64, 8, 512], BF16)
    AOBd = dscr("AOBd", [10, 128, 4, 512], BF16)
    X1 = dscr("X1", [5120, 1024], F32)
    GMOD = dscr("GMOD", [4, 1024], F32)
    W1b = dscr("W1b", [1024, 4096], BF16)
    W2b = dscr("W2b", [4096, 1024], BF16)
    WGb = dscr("WGb", [1024, 2048], BF16)
    WBRAb = dscr("WBRAb", [512, 1024], BF16)
    WBRBb = dscr("WBRBb", [512, 1024], BF16)
    WOUTb = dscr("WOUTb", [1024, 1024], BF16)

    def xrows(tg0, n):
        if tg0 < 4096:
            return xs[tg0:tg0 + n, :]
        return xp[tg0 - 4096:tg0 - 4096 + n, :]

    def yrows(tg0, n):
        if tg0 < 4096:
            return ys[tg0:tg0 + n, :]
        return yp[tg0 - 4096:tg0 - 4096 + n, :]

    with ExitStack() as top:
        P = Prog(nc, top)

        def sbuf(st, name, shape, dt):
            return st.enter_context(nc.sbuf_tensor(name, list(shape), dt))

        def MM(out, lhsT, rhs, start, stop, reads, writes):
            return P.add("pe", lambda e: e.matmul(out, lhsT=lhsT, rhs=rhs, start=start, stop=stop,
                                                  skip_group_check=True), reads, writes)

        def TR(out, in_, idn, reads, writes):
            return P.add("pe", lambda e: e.transpose(out=out, in_=in_, identity=idn), reads, writes)

        def ACT(out, in_, func, reads, writes, scale=None, accum=None):
            kw = {}
            if scale is not None:
                kw["scale"] = scale
            if accum is not None:
                kw["accum_out"] = accum
            return P.add("act", lambda e: e.activation(out=out, in_=in_, func=func, **kw), reads, writes)

        def TS(out, in0, s1, s2, op0, op1, reads, writes, eng="dve"):
            if s2 is None:
                return P.add(eng, lambda e: e.tensor_scalar(out=out, in0=in0, scalar1=s1, scalar2=None,
                                                            op0=op0), reads, writes)
            return P.add(eng, lambda e: e.tensor_scalar(out=out, in0=in0, scalar1=s1, scalar2=s2,
                                                        op0=op0, op1=op1), reads, writes)

        def TT(out, in0, in1, op, reads, writes, eng="dve"):
            return P.add(eng, lambda e: e.tensor_tensor(out=out, in0=in0, in1=in1, op=op), reads, writes)

        def STT(out, in0, scalar, in1, op0, op1, reads, writes, eng="dve"):
            return P.add(eng, lambda e: e.scalar_tensor_tensor(out=out, in0=in0, scalar=scalar, in1=in1,
                                                               op0=op0, op1=op1), reads, writes)

        def CP(out, in_, reads, writes, eng="dve"):
            return P.add(eng, lambda e: e.tensor_copy(out=out, in_=in_), reads, writes)

        def RECIP(out, in_, reads, writes):
            return P.add("dve", lambda e: e.reciprocal(out=out, in_=in_), reads, writes)

        def RED(out, in_, reads, writes):
            return P.add("dve", lambda e: e.tensor_reduce(out=out, in_=in_, axis=AX.X, op=ALU.add), reads, writes)

        def MEMSET(ap, val, writes, eng="dve"):
            return P.add(eng, lambda e: e.memset(ap, val), [], writes)

        def DMA(q, out, in_, reads, writes, stream, slow=False):
            if slow:
                return P.add(q, lambda e: e.dma_start(out=out, in_=in_, allow_slow_non_contiguous=True),
                             reads, writes, dma=True, stream=stream)
            return P.add(q, lambda e: e.dma_start(out=out, in_=in_), reads, writes, dma=True, stream=stream)

        ps = [top.enter_context(nc.psum_tensor("ps%d" % i, [128, 512], F32)) for i in range(8)]
        r_ps = [P.res("ps%d" % i, excl=True) for i in range(8)]

        class Rot:
            def __init__(self, idxs):
                self.idxs = idxs
                self.i = 0

            def __call__(self):
                k = self.idxs[self.i % len(self.idxs)]
                self.i += 1
                return k

        idf = sbuf(top, "idf", [128, 128], F32)
        idb = sbuf(top, "idb", [128, 128], BF16)
        onesf = sbuf(top, "onesf", [128, 128], F32)
        epst = sbuf(top, "epst", [128, 1], F32)
        MODS = sbuf(top, "MODS", [128, 4, 16], F32)
        r_c = P.res("consts")

        def tile_info(i):
            return (i * 512, 0 if i < 8 else 1)

        with ExitStack() as st:
            cTt = sbuf(st, "cTt", [128, 16], F32)
            sT = sbuf(st, "sT", [128, 16], BF16)
            wm = [sbuf(st, "wm%d" % k, [128, 8, 512], BF16) for k in range(2)]
            r_wm = [P.res("wm%d" % k) for k in range(2)]
            modT = sbuf(st, "modT", [128, 96], F32)
            bmt = sbuf(st, "bmt", [128, 96], F32)
            n1t = sbuf(st, "n1t", [128, 16], F32)
            n2t = sbuf(st, "n2t", [128, 16], F32)
            r_l = P.res("a0loads")
            r_sT = P.res("sT")
            r_mod = P.res("modT")
            DMA("sp", idf[:, :], ident[:, :], [], [r_c], "c0")
            DMA("sp", cTt[:, :], cT[:, :], [], [r_l], "c1")
            DMA("sp", bmt[:, :], bmodT2[:, :], [], [r_l], "c2")
            DMA("sp", n1t[:, :], n1T2[:, :], [], [r_l], "c3")
            DMA("sp", n2t[:, :], n2T2[:, :], [], [r_l], "c4")
            MEMSET(onesf[:, :], 1.0, [r_c])
            MEMSET(epst[:, :], EPS, [r_c])
            CP(idb[:, :], idf[:, :], [r_c], [r_c])
            ACT(sT[:, :], cTt[:, :], AF.Silu, [r_l], [r_sT])
            wmv = w_mod.rearrange("(c p) n -> p c n", p=128)
            for k in range(12):
                DMA("pool", wm[k % 2][:, :, :], wmv[:, :, k * 512:(k + 1) * 512], [], [r_wm[k % 2]], "wm%d" % (k % 2))
                for j in range(4):
                    fc = 4 * k + j
                    for c in range(8):
                        MM(ps[0][:, fc * 2:fc * 2 + 2], wm[k % 2][:, c, j * 128:(j + 1) * 128],
                           sT[:, c * 2:c * 2 + 2], c == 0, c == 7, [r_wm[k % 2], r_sT], [r_ps[0]])
            TT(modT[:, :], ps[0][:, 0:96], bmt[:, :], ALU.add, [r_ps[0], r_l], [r_mod])
            STT(MODS[:, 0, :], modT[:, 16:32], 1.0, n1t[:, :], ALU.add, ALU.mult, [r_mod, r_l], [r_c])
            CP(MODS[:, 1, :], modT[:, 0:16], [r_mod], [r_c])
            STT(MODS[:, 2, :], modT[:, 64:80], 1.0, n2t[:, :], ALU.add, ALU.mult, [r_mod, r_l], [r_c])
            CP(MODS[:, 3, :], modT[:, 48:64], [r_mod], [r_c])
            for which, base in ((0, 32), (1, 80)):
                for v in range(2):
                    row = which * 2 + v
                    dst = bass.AP(GMOD.tensor, row * 1024, [[1, 128], [128, 8]])
                    s0 = modT[:, base + v:base + v + 1]
                    src = bass.AP(s0.tensor, s0.offset, [[s0.ap[0][0], 128], [2, 8]])
                    DMA("sp", dst, src, [r_mod], [], "gm%d" % row, slow=True)
            P.flush()

        A1 = lambda c, v: MODS[:, 0, c * 2 + v:c * 2 + v + 1]
        SH1 = lambda c, v: MODS[:, 1, c * 2 + v:c * 2 + v + 1]
        A2 = lambda c, v: MODS[:, 2, c * 2 + v:c * 2 + v + 1]
        SH2 = lambda c, v: MODS[:, 3, c * 2 + v:c * 2 + v + 1]

        def hT_chain(src_ap, r_src, junk, r_junk, stat, r_stat, xn, r_xn):
            ACT(junk[:, :], src_ap, AF.Square, [r_src], [r_junk, r_stat], scale=1.0 / 32.0, accum=stat[:, 0:1])
            TS(stat[:, 1:2], stat[:, 0:1], EPS, None, ALU.add, None, [r_stat], [r_stat])
            ACT(stat[:, 2:3], stat[:, 1:2], AF.Sqrt, [r_stat], [r_stat])
            RECIP(stat[:, 3:4], stat[:, 2:3], [r_stat], [r_stat])
            ACT(xn[:, :], src_ap, AF.Copy, [r_src, r_stat], [r_xn], scale=stat[:, 3:4])

        def hT_tr(xn, r_xn, dst_fn, r_dst, Afn, Sfn, v, rot):
            for half in range(2):
                b = rot()
                for cc in range(4):
                    c = half * 4 + cc
                    TR(ps[b][:, cc * 128:(cc + 1) * 128], xn[:, c * 128:(c + 1) * 128], idf[:, :],
                       [r_xn, r_c], [r_ps[b]])
                for cc in range(4):
                    c = half * 4 + cc
                    TS(dst_fn(c), ps[b][:, cc * 128:(cc + 1) * 128], Afn(c, v), Sfn(c, v), ALU.mult, ALU.add,
                       [r_ps[b], r_c], [r_dst])

        if stage < 1:
            P.flush(final=True)
            return nc

        with ExitStack() as kv:
            KTA = sbuf(kv, "KTA", [128, 2, 4608], BF16)
            VA = sbuf(kv, "VA", [128, 37, 2, 65], BF16)
            KTB = sbuf(kv, "KTB", [128, 4, 4608], BF16)
            LB = sbuf(kv, "LB", [128, 36, 4, 160], BF16)
            r_kt = [P.res("kt%d" % t) for t in range(36)]

            def phaseA(tag, tilesA, do_ctx):
              with ExitStack() as st0:
                _sb = sbuf
                def sbuf_(st_, name, shape, dt):
                    return _sb(st_, name + tag, shape, dt)
                st = st0
                wA = sbuf_(st, "wA", [128, 8, 256], BF16)
                wBK = sbuf_(st, "wBK", [128, 8, 512], BF16)
                wBV = sbuf_(st, "wBV", [128, 8, 512], BF16)
                r_w = P.res("wA")
                wv = w_in.rearrange("(c p) n -> p c n", p=128)
                DMA("pool", wA[:, :, :], wv[:, :, 512:768], [], [r_w], "w0")
                DMA("pool", wBK[:, :, :], wv[:, :, 1280:1792], [], [r_w], "w1")
                DMA("pool", wBV[:, :, :], wv[:, :, 1792:2304], [], [r_w], "w2")
                kgt = sbuf_(st, "kgt", [128, 128], F32)
                DMA("sp", kgt[:, :], bass.AP(kg2.tensor, 0, [[0, 128], [1, 128]]), [], [r_w], "c5")
                if do_ctx:
                    MEMSET(VA[:, :, :, :], 1.0, r_kt)
                    MEMSET(KTA[64:128, :, :], 0.0, r_kt)
                    MEMSET(LB[:, :, :, 64:96], 0.0, r_kt)
                    MEMSET(LB[:, :, :, 64:65], 1.0, r_kt)

                xt = [sbuf_(st, "xt%d" % k, [128, 4, 1024], F32) for k in range(2)]
                r_xt = [P.res("xt%d" % k) for k in range(2)]
                junk = sbuf_(st, "junk", [128, 1024], BF16)
                r_junk = P.res("junk")
                stat = [sbuf_(st, "stat%d" % k, [128, 4], F32) for k in range(2)]
                r_stat = [P.res("stat%d" % k) for k in range(2)]
                xn = [sbuf_(st, "xn%d" % k, [128, 1024], F32) for k in range(2)]
                r_xn = [P.res("xn%d" % k) for k in range(2)]
                hT = [sbuf_(st, "hT%d" % k, [128, 8, 512], BF16) for k in range(2)]
                r_hT = [[P.res("hT%d_%d" % (k, s)) for s in range(4)] for k in range(2)]
                ropeT = [sbuf_(st, "ropeT%d" % k, [128, 2, 128], F32) for k in range(2)]
                r_rope = [P.res("rope%d" % k) for k in range(2)]
                akf = [sbuf_(st, "akf%d" % k, [128, 128], F32) for k in range(2)]
                r_akf = [P.res("akf%d" % k) for k in range(2)]
                sqk = sbuf_(st, "sqk", [128, 128], F32)
                kst = [sbuf_(st, "kst%d" % k, [128, 8], F32) for k in range(2)]
                akn = [sbuf_(st, "akn%d" % k, [128, 128], F32) for k in range(2)]
                r_akn = [P.res("akn%d" % k) for k in range(2)]
                akr = [sbuf_(st, "akr%d" % k, [128, 128], F32) for k in range(2)]
                r_akr = [P.res("akr%d" % k) for k in range(2)]
                t1 = sbuf_(st, "t1", [128, 128], F32)
                t2 = sbuf_(st, "t2", [128, 128], F32)
                r_tmp = P.res("tmpA")
                r_t1 = P.res("t1A")
                r_t2 = P.res("t2A")
                r_kst = [P.res("kst%d" % k) for k in range(2)]
                stg = [sbuf_(st, "stg%d" % k, [128, 512], F32) for k in range(3)]
                r_stg = [P.res("stg%d" % k) for k in range(3)]
                stg_i = [0]
                ctx32 = [sbuf_(st, "ctx32_%d" % k, [128, 512], F32) for k in range(2)]
                r_ctx = [P.res("ctx32_%d" % k) for k in range(2)]
                rot = Rot([0, 1, 2, 3, 4, 5, 6, 7])

                def next_stg():
                    k = stg_i[0] % 3
                    stg_i[0] += 1
                    return k

                for t in (range(4) if do_ctx else []):
                    rk = [r_kt[t]]
                    a = ctx32[t % 2]
                    ra = r_ctx[t % 2]
                    DMA("sp", a[:, 0:128], cak[t * 128:(t + 1) * 128, :], [], [ra], "cx0")
                    b = rot()
                    for g in range(2):
                        TR(ps[b][0:64, g * 128:(g + 1) * 128], a[:, g * 64:(g + 1) * 64], idf[:, :], [ra, r_c], [r_ps[b]])
                    CP(KTA[0:64, :, t * 128:(t + 1) * 128], ps[b][0:64, 0:256].rearrange("p (g n) -> p g n", g=2),
                       [r_ps[b]], rk)
                    DMA("sp", a[:, 128:256], cav[t * 128:(t + 1) * 128, :], [], [ra], "cx1")
                    CP(VA[:, t, :, 0:64], a[:, 128:256].rearrange("p (g d) -> p g d", g=2), [ra], rk)
                    a2 = ctx32[(t + 1) % 2]
                    ra2 = r_ctx[(t + 1) % 2]
                    DMA("sp", a2[:, :], cbk[t * 128:(t + 1) * 128, :], [], [ra2], "cx2")
                    b = rot()
                    for j in range(4):
                        TR(ps[b][:, j * 128:(j + 1) * 128], a2[:, j * 128:(j + 1) * 128], idf[:, :], [ra2, r_c], [r_ps[b]])
                    CP(KTB[:, :, t * 128:(t + 1) * 128], ps[b][:, :].rearrange("p (j n) -> p j n", j=4), [r_ps[b]], rk)
                    DMA("sp", a[:, :], cbv[t * 128:(t + 1) * 128, :], [], [ra], "cx3")
                    av4 = a[:, :].rearrange("p (j e d) -> p j e d", j=4, e=2)
                    CP(LB[:, t, :, 0:64], av4[:, :, 0, :], [ra], rk)
                    CP(LB[:, t, :, 96:160], av4[:, :, 1, :], [ra], rk)


                def loadA(idx):
                    tg0, T, v, kb, isp, pr0 = tilesA[idx]
                    k = idx % 2
                    ns = T // 128
                    DMA("sp", xt[k][:, 0:ns, :], xrows(tg0, T).rearrange("(s p) d -> p s d", p=128), [], [r_xt[k]],
                        "xt%d" % k)

                pend_tr = [None]
                nT = len(tilesA)

                def chainA(idx, s_):
                    k_ = idx % 2
                    q_ = s_ % 2
                    hT_chain(xt[k_][:, s_, :], r_xt[k_], junk, r_junk, stat[q_], r_stat[q_], xn[q_], r_xn[q_])

                def trA(idx, s_):
                    k_ = idx % 2
                    q_ = s_ % 2
                    v_ = tilesA[idx][2]
                    hT_tr(xn[q_], r_xn[q_], lambda c, k_=k_, s_=s_: hT[k_][:, c, s_ * 128:(s_ + 1) * 128],
                          r_hT[k_][s_], A1, SH1, v_, rot)

                def bounceA(idx):
                    tg0, T, v, kb, isp, pr0 = tilesA[idx]
                    k_ = idx % 2
                    ns_ = T // 128
                    ti = tg0 // 512
                    co = tg0 % 512
                    DMA("sp", HT1[ti, :, :, co:co + T], hT[k_][:, :, 0:T], r_hT[k_][0:ns_], [], "ht%d" % k_)

                def stage2_sub(idx, s):
                    tg0, T, v, kb, isp, pr0 = tilesA[idx]
                    k = idx % 2
                    q = s % 2
                    kt = (kb + s * 128) // 128
                    rk = [r_kt[kt]]
                    koff = kb + s * 128
                    rh = [r_hT[k][s], r_w]
                    b = rot()
                    for c in range(8):
                        MM(ps[b][:, 0:256], hT[k][:, c, s * 128:(s + 1) * 128], wA[:, c, :], c == 0, c == 7,
                           rh, [r_ps[b]])
                    ACT(akf[q][:, :], ps[b][:, 0:128], AF.Copy, [r_ps[b]], [r_akf[q]])
                    CP(VA[:, kt, :, 0:64], ps[b][:, 128:256].rearrange("p (g d) -> p g d", g=2), [r_ps[b]], rk)
                    if isp:
                        sk = next_stg()
                        ACT(stg[sk][:, 0:128], ps[b][:, 128:256], AF.Copy, [r_ps[b]], [r_stg[sk]])
                        DMA("sp", nav[pr0 + s * 128:pr0 + (s + 1) * 128, :], stg[sk][:, 0:128], [r_stg[sk]], [],
                            "stg%d" % sk)
                    TT(sqk[:, :], akf[q][:, :], akf[q][:, :], ALU.mult, [r_akf[q]], [r_tmp], eng="pool")
                    RED(kst[q][:, 0:2], sqk[:, :].rearrange("p (g d) -> p g d", g=2), [r_tmp], [r_kst[q]])
                    TS(kst[q][:, 2:4], kst[q][:, 0:2], 1.0 / 64.0, EPS, ALU.mult, ALU.add, [r_kst[q]], [r_kst[q]])
                    ACT(kst[q][:, 4:6], kst[q][:, 2:4], AF.Sqrt, [r_kst[q]], [r_kst[q]])
                    RECIP(kst[q][:, 6:8], kst[q][:, 4:6], [r_kst[q]], [r_kst[q]])
                    for g in range(2):
                        TS(akn[q][:, g * 64:(g + 1) * 64], akf[q][:, g * 64:(g + 1) * 64], kst[q][:, 6 + g:7 + g],
                           None, ALU.mult, None, [r_akf[q], r_kst[q]], [r_akn[q]])
                    TT(akn[q][:, :], akn[q][:, :], kgt[:, :], ALU.mult, [r_akn[q], r_w], [r_akn[q]], eng="pool")
                    if isp:
                        DMA("sp", nak[pr0 + s * 128:pr0 + (s + 1) * 128, :], akn[q][:, :], [r_akn[q]], [],
                            "akn%d" % q)
                        ksrc, rks = akn[q], r_akn[q]
                    else:
                        DMA("sp", ropeT[q][:, :, :], rope[tg0 + s * 128:tg0 + (s + 1) * 128, :, 0:128], [],
                            [r_rope[q]], "rope%d" % q)
                        xv_ = akn[q][:, :].rearrange("p (a h d) -> p a h d", a=4, h=2)
                        sv_ = ropeT[q][:, 1, :].rearrange("p (a h d) -> p a h d", a=4, h=2)
                        t2v = t2[:, :].rearrange("p (a h d) -> p a h d", a=4, h=2)
                        TT(t1[:, :], akn[q][:, :], ropeT[q][:, 0, :], ALU.mult, [r_akn[q], r_rope[q]], [r_t1], eng="pool")
                        TT(t2v[:, :, 0, :], xv_[:, :, 1, :], sv_[:, :, 0, :], ALU.mult, [r_akn[q], r_rope[q]], [r_t2], eng="pool")
                        TT(t2v[:, :, 1, :], xv_[:, :, 0, :], sv_[:, :, 1, :], ALU.mult, [r_akn[q], r_rope[q]], [r_t2], eng="pool")
                        TT(akr[q][:, :], t1[:, :], t2[:, :], ALU.add, [r_t1, r_t2], [r_akr[q]], eng="pool")
                        ksrc, rks = akr[q], r_akr[q]

                    def k_tr(ksrc=ksrc, rks=rks, koff=koff, rk=rk):
                        b2 = rot()
                        for g in range(2):
                            TR(ps[b2][0:64, g * 128:(g + 1) * 128], ksrc[:, g * 64:(g + 1) * 64], idf[:, :],
                               [rks, r_c], [r_ps[b2]])
                        ACT(KTA[0:64, :, koff:koff + 128],
                            ps[b2][0:64, 0:256].rearrange("p (g n) -> p g n", g=2), AF.Copy, [r_ps[b2]], rk)
                    b = rot()
                    for c in range(8):
                        MM(ps[b][:, :], hT[k][:, c, s * 128:(s + 1) * 128], wBV[:, c, :], c == 0, c == 7, rh, [r_ps[b]])
                    pv4 = ps[b][:, :].rearrange("p (j e d) -> p j e d", j=4, e=2)
                    ACT(LB[:, kt, :, 0:64], pv4[:, :, 0, :], AF.Copy, [r_ps[b]], rk)
                    CP(LB[:, kt, :, 96:160], pv4[:, :, 1, :], [r_ps[b]], rk)
                    if isp:
                        sk = next_stg()
                        ACT(stg[sk][:, :], ps[b][:, :], AF.Copy, [r_ps[b]], [r_stg[sk]])
                        DMA("sp", nbv[pr0 + s * 128:pr0 + (s + 1) * 128, :], stg[sk][:, :], [r_stg[sk]], [],
                            "stg%d" % sk)
                        b = rot()
                        for c in range(8):
                            MM(ps[b][:, :], hT[k][:, c, s * 128:(s + 1) * 128], wBK[:, c, :], c == 0, c == 7, rh,
                               [r_ps[b]])
                        sk = next_stg()
                        CP(stg[sk][:, :], ps[b][:, :], [r_ps[b]], [r_stg[sk]])
                        DMA("sp", nbk[pr0 + s * 128:pr0 + (s + 1) * 128, :], stg[sk][:, :], [r_stg[sk]], [],
                            "stg%d" % sk)
                    if pend_tr[0] is not None:
                        pend_tr[0]()
                    pend_tr[0] = k_tr

                def stage2_tail(idx):
                    tg0, T, v, kb, isp, pr0 = tilesA[idx]
                    k = idx % 2
                    ns = T // 128
                    if pend_tr[0] is not None:
                        pend_tr[0]()
                        pend_tr[0] = None
                    kts = [r_kt[(kb + s * 128) // 128] for s in range(ns)]
                    for j in range(4):
                        b = rot()
                        for c in range(8):
                            MM(ps[b][:, 0:T], wBK[:, c, j * 128:(j + 1) * 128], hT[k][:, c, 0:T], c == 0, c == 7,
                               r_hT[k][0:ns] + [r_w], [r_ps[b]])
                        if j % 2 == 0:
                            ACT(KTB[:, j, kb:kb + T], ps[b][:, 0:T], AF.Copy, [r_ps[b]], kts)
                        else:
                            CP(KTB[:, j, kb:kb + T], ps[b][:, 0:T], [r_ps[b]], kts)

                nsA = tilesA[0][1] // 128
                loadA(0)
                if nT > 1:
                    loadA(1)
                chainA(0, 0)
                for s in range(nsA):
                    if s + 1 < nsA:
                        chainA(0, s + 1)
                    trA(0, s)
                bounceA(0)
                for idx in range(nT):
                    nxt = idx + 1 < nT
                    if idx + 2 < nT:
                        loadA(idx + 2)
                    if nxt:
                        chainA(idx + 1, 0)
                    for s in range(nsA):
                        stage2_sub(idx, s)
                        if nxt:
                            if s + 1 < nsA:
                                chainA(idx + 1, s + 1)
                            trA(idx + 1, s)
                    stage2_tail(idx)
                    if nxt:
                        bounceA(idx + 1)
                P.flush()

            tilesS = [(i * 512, 512, 0, 512 + i * 512, False, 0) for i in range(8)]
            tilesP = [(4096 + p * 256, 256, 1, p * 256, True, p * 256) for p in range(4)]
            qtS = [(i, 0, 512, False, i) for i in range(8)]
            qtP = [(8 + p // 2, (p % 2) * 256, 256, True, p) for p in range(4)]
            phaseA("s", tilesS, True)
            if stage < 2:
                if debug:
                    dbg = dout("dbg_kta", [128, 2, 4608], BF16)
                    dbg2 = dout("dbg_ktb", [128, 4, 4608], BF16)
                    dbg3 = dout("dbg_va", [128, 37 * 2 * 65], BF16)
                    dbg4 = dout("dbg_lb", [128, 36 * 4 * 160], BF16)
                    DMA("sp", dbg[:, :, :], KTA[:, :, :], [], [], "dbg0")
                    DMA("sp", dbg2[:, :, :], KTB[:, :, :], [], [], "dbg1")
                    DMA("sp", dbg3[:, :], VA[:, :, :, :].rearrange("p a b c -> p (a b c)"), [], [], "dbg2")
                    DMA("sp", dbg4[:, :], LB[:, :, :, :].rearrange("p a b c -> p (a b c)"), [], [], "dbg3")
                P.flush(final=True)
                return nc

            def phaseB(tag, qtiles):
              with ExitStack() as st0:
                _sb = sbuf
                def sbuf_(st_, name, shape, dt):
                    return _sb(st_, name + tag, shape, dt)
                st = st0
                wAQ = sbuf_(st, "wAQ", [128, 8, 512], BF16)
                wBQ = sbuf_(st, "wBQ", [128, 8, 512], BF16)
                r_w = P.res("wB")
                wv = w_in.rearrange("(c p) n -> p c n", p=128)
                DMA("pool", wAQ[:, :, :], wv[:, :, 0:512], [], [r_w], "w0")
                DMA("pool", wBQ[:, :, :], wv[:, :, 768:1280], [], [r_w], "w1")
                BT = sbuf_(st, "BT", [128, 8, 16, 64], BF16)
                DMA("pool", BT[:, :, :, :], btab.rearrange("p (h e q) -> p h e q", h=8, e=16), [], [r_w], "w2")
                qgt = sbuf_(st, "qgt", [128, 512], F32)
                DMA("sp", qgt[:, :], bass.AP(qg8.tensor, 0, [[0, 128], [1, 512]]), [], [r_w], "c5")
                if tag == "s":
                    for hf in range(2):
                        DMA("pool", WGb[:, hf * 1024:(hf + 1) * 1024], w_in[:, 2304 + hf * 1024:2304 + (hf + 1) * 1024],
                            [], [], "cv%d" % hf)
                    DMA("pool", WBRAb[:, :], w_br_a[:, :], [], [], "cv2")
                    DMA("pool", WBRBb[:, :], w_br_b[:, :], [], [], "cv3")
                    DMA("pool", WOUTb[:, :], w_out[:, :], [], [], "cv4")
                    for q4 in range(4):
                        DMA("pool", W1b[q4 * 256:(q4 + 1) * 256, :], w1[q4 * 256:(q4 + 1) * 256, :], [], [], "cv%d" % (5 + q4))
                    for q4 in range(4):
                        DMA("pool", W2b[q4 * 1024:(q4 + 1) * 1024, :], w2[q4 * 1024:(q4 + 1) * 1024, :], [], [],
                            "cv%d" % (9 + q4))

                hT = [sbuf_(st, "hTb%d" % k, [128, 8, 512], BF16) for k in range(1)] * 2
                r_hT = [P.res("hTb%d" % k) for k in range(1)] * 2
                QTA = sbuf_(st, "QTA", [128, 8, 512], BF16)
                r_qta = [P.res("qta%d" % s) for s in range(4)]
                QTBe = sbuf_(st, "QTBe", [128, 4, 512], BF16)
                QTBo = sbuf_(st, "QTBo", [128, 4, 512], BF16)
                r_qtb = [P.res("qtb%d" % j) for j in range(4)]
                MEMSET(QTA[64:128, :, :], 0.0, r_qta)
                MEMSET(QTBe[64:128, :, :], 0.0, r_qtb)
                MEMSET(QTBo[0:64, :, :], 0.0, r_qtb)
                PT = [sbuf_(st, "PT%d" % k, [128, 512], BF16) for k in range(4)]
                r_pt = [P.res("PT%d" % k) for k in range(4)]
                pt_i = [0]
                AOA = [sbuf_(st, "AOA%d" % k, [128, 8, 512], BF16) for k in range(1)] * 2
                r_aoa = [P.res("AOA%d" % k) for k in range(1)] * 2
                AOB = [sbuf_(st, "AOB%d" % k, [128, 4, 512], BF16) for k in range(1)] * 2
                r_aob = [P.res("AOB%d" % k) for k in range(1)] * 2
                ropeQ = [sbuf_(st, "ropeQ%d" % k, [128, 2, 512], F32) for k in range(1)] * 2
                r_rope = [P.res("ropeQ%d" % k) for k in range(1)] * 2
                aqf = [sbuf_(st, "aqf%d" % k, [128, 512], F32) for k in range(1)] * 2
                r_aqf = [P.res("aqf%d" % k) for k in range(1)] * 2
                aqn = [sbuf_(st, "aqn%d" % k, [128, 512], F32) for k in range(2)]
                r_aqn = [P.res("aqn%d" % k) for k in range(2)]
                tq1 = sbuf_(st, "tq1", [128, 512], F32)
                tq2 = sbuf_(st, "tq2", [128, 512], F32)
                qst = [sbuf_(st, "qst%d" % k, [128, 32], F32) for k in range(2)]
                r_tmp = P.res("tmpB")
                r_tq1 = P.res("tq1B")
                r_tq2 = P.res("tq2B")
                r_qst = [P.res("qstB%d" % k) for k in range(2)]
                oT = [sbuf_(st, "oT%d" % k, [128, 512], F32) for k in range(2)]
                r_oT = [P.res("oT%d" % k) for k in range(2)]
                rrow = [sbuf_(st, "rrow%d" % k, [128, 512], F32) for k in range(2)]
                r_rrow = [P.res("rrow%d" % k) for k in range(2)]
                fin_i = [0]
                deferred = []
                defer_n = [2]
                rotS = Rot([0, 1, 2, 3])
                rotO = Rot([4, 5])
                rotX = Rot([6, 7])

                def next_pt():
                    k = pt_i[0] % 4
                    pt_i[0] += 1
                    return k

                def finalize(bo, T, rows, dp, dst_ap, r_dst):
                    f = fin_i[0] % 2
                    fin_i[0] += 1
                    r0, r1 = rows
                    ACT(rrow[f][dp:dp + 1, 0:T], ps[bo][dp:dp + 1, 0:T], AF.Ln, [r_ps[bo]], [r_rrow[f]])
                    ACT(rrow[f][dp:dp + 1, 0:T], rrow[f][dp:dp + 1, 0:T], AF.Exp, [r_rrow[f]], [r_rrow[f]], scale=-1.0)
                    CP(oT[f][r0:r1, 0:T], ps[bo][r0:r1, 0:T], [r_ps[bo]], [r_oT[f]])

                    def part_b():
                        bx = rotX()
                        MM(ps[bx][:, 0:T], onesf[dp:dp + 1, :], rrow[f][dp:dp + 1, 0:T], True, True,
                           [r_rrow[f], r_c], [r_ps[bx]])
                        TT(dst_ap, oT[f][r0:r1, 0:T], ps[bx][r0:r1, 0:T], ALU.mult, [r_oT[f], r_ps[bx]], [r_dst])
                    deferred.append([defer_n[0], part_b])


                def loadB(qi):
                    ti, co, T, isp, sp_ = qtiles[qi]
                    k = qi % 2
                    DMA("sp", hT[k][:, :, 0:T], HT1[ti, :, :, co:co + T], [], [r_hT[k]], "hb0")

                loadB(0)
                for qi in range(len(qtiles)):
                    ti, co, T, isp, sp_ = qtiles[qi]
                    k = qi % 2
                    ns = T // 128
                    defer_n[0] = 2 if isp else 6
                    rh = [r_hT[k], r_w]
                    def aq_chain(s):
                        q = s % 2
                        b = rotS()
                        for c in range(8):
                            MM(ps[b][:, :], hT[k][:, c, s * 128:(s + 1) * 128], wAQ[:, c, :], c == 0, c == 7, rh, [r_ps[b]])
                        CP(aqf[q][:, :], ps[b][:, :], [r_ps[b]], [r_aqf[q]])
                        TT(tq1[:, :], aqf[q][:, :], aqf[q][:, :], ALU.mult, [r_aqf[q]], [r_tq1], eng="pool")
                        RED(qst[q][:, 0:8], tq1[:, :].rearrange("p (g d) -> p g d", g=8), [r_tq1], [r_qst[q]])
                        TS(qst[q][:, 8:16], qst[q][:, 0:8], 1.0 / 64.0, EPS, ALU.mult, ALU.add, [r_qst[q]], [r_qst[q]])
                        ACT(qst[q][:, 16:24], qst[q][:, 8:16], AF.Ln, [r_qst[q]], [r_qst[q]])
                        ACT(qst[q][:, 24:32], qst[q][:, 16:24], AF.Exp, [r_qst[q]], [r_qst[q]], scale=-0.5)
                        for h in range(8):
                            TS(aqn[q][:, h * 64:(h + 1) * 64], aqf[q][:, h * 64:(h + 1) * 64], qst[q][:, 24 + h:25 + h],
                               0.125, ALU.mult, ALU.mult, [r_aqf[q], r_qst[q]], [r_aqn[q]])
                        TT(aqn[q][:, :], aqn[q][:, :], qgt[:, :], ALU.mult, [r_aqn[q], r_w], [r_aqn[q]], eng="pool")
                        if not isp:
                            t0 = sp_ * 512 + s * 128
                            DMA("sp", ropeQ[q][:, :, :], rope[t0:t0 + 128, :, :], [], [r_rope[q]], "ropeq0")
                            xv_ = aqn[q][:, :].rearrange("p (a h d) -> p a h d", a=16, h=2)
                            sv_ = ropeQ[q][:, 1, :].rearrange("p (a h d) -> p a h d", a=16, h=2)
                            t2v = tq2[:, :].rearrange("p (a h d) -> p a h d", a=16, h=2)
                            TT(tq1[:, :], aqn[q][:, :], ropeQ[q][:, 0, :], ALU.mult, [r_aqn[q], r_rope[q]], [r_tq1], eng="pool")
                            TT(t2v[:, :, 0, :], xv_[:, :, 1, :], sv_[:, :, 0, :], ALU.mult, [r_aqn[q], r_rope[q]], [r_tq2], eng="pool")
                            TT(t2v[:, :, 1, :], xv_[:, :, 0, :], sv_[:, :, 1, :], ALU.mult, [r_aqn[q], r_rope[q]], [r_tq2], eng="pool")
                            TT(aqn[q][:, :], tq1[:, :], tq2[:, :], ALU.add, [r_tq1, r_tq2], [r_aqn[q]], eng="pool")

                    def aq_tr(s):
                        q = s % 2
                        for hb in range(2):
                            b2 = rotS()
                            for hh in range(4):
                                h = hb * 4 + hh
                                TR(ps[b2][0:64, hh * 128:(hh + 1) * 128], aqn[q][:, h * 64:(h + 1) * 64], idf[:, :],
                                   [r_aqn[q], r_c], [r_ps[b2]])
                            CP(QTA[0:64, hb * 4:hb * 4 + 4, s * 128:(s + 1) * 128],
                               ps[b2][0:64, :].rearrange("p (g n) -> p g n", g=4), [r_ps[b2]], [r_qta[s]])
                    for j in range(4):
                        b = rotS()
                        for c in range(8):
                            MM(ps[b][:, 0:T], wBQ[:, c, j * 128:(j + 1) * 128], hT[k][:, c, 0:T], c == 0, c == 7, rh,
                               [r_ps[b]])
                        TS(QTBe[0:64, j, 0:T], ps[b][0:64, 0:T], 0.125, None, ALU.mult, None, [r_ps[b]], [r_qtb[j]])
                        TS(QTBo[64:128, j, 0:T], ps[b][64:128, 0:T], 0.125, None, ALU.mult, None, [r_ps[b]], [r_qtb[j]])
                    steps = []

                    def add_dense_step(KT_ap, Q_ap, rd, V_ap, vr, bo, M, first, last, fin):
                        cell = {}

                        def front():
                            b_ = rotS()
                            MM(ps[b_][:, 0:T], KT_ap, Q_ap, True, True, rd, [r_ps[b_]])
                            pk = next_pt()
                            cell["pk"] = pk
                            ACT(PT[pk][:, 0:T], ps[b_][:, 0:T], AF.Exp, [r_ps[b_]], [r_pt[pk]])

                        def back():
                            pk = cell["pk"]
                            MM(ps[bo][0:M, 0:T], V_ap, PT[pk][:, 0:T], first, last, vr + [r_pt[pk]], [r_ps[bo]])
                            if fin is not None:
                                fin()
                        steps.append((front, back))

                    def add_local_step(blocks, Qcols, rq_, h_, bo, M, fin):
                        cell = {}
                        cnt = len(blocks)

                        def front():
                            b_ = rotS()
                            for jj, (KT_ap, rk_, e_, V_ap) in enumerate(blocks):
                                MM(ps[b_][:, jj * 64:(jj + 1) * 64], KT_ap, Qcols, True, False, [rk_] + rq_, [r_ps[b_]])
                                MM(ps[b_][:, jj * 64:(jj + 1) * 64], idb[:, :], BT[:, h_, e_, :], False, True,
                                   [r_c, r_w], [r_ps[b_]])
                            pk = next_pt()
                            cell["pk"] = pk
                            ACT(PT[pk][:, 0:cnt * 64], ps[b_][:, 0:cnt * 64], AF.Exp, [r_ps[b_]], [r_pt[pk]])

                        def back():
                            pk = cell["pk"]
                            for jj, (KT_ap, rk_, e_, V_ap) in enumerate(blocks):
                                MM(V_ap[0], V_ap[1], PT[pk][:, jj * 64:(jj + 1) * 64], False, jj == cnt - 1,
                                   [rk_, r_pt[pk]], [r_ps[bo]])
                            if fin is not None:
                                fin()
                        steps.append((front, back))

                    def mkfin(bo, rows, dp, dst, r_dst):
                        return lambda: finalize(bo, T, rows, dp, dst, r_dst)

                    steps = []
                    if isp:
                        ktl = [2 * sp_, 2 * sp_ + 1]
                    else:
                        ktl = list(range(36))
                    for h in range(8):
                        g = h // 4
                        bo = rotO()
                        for n_, kt in enumerate(ktl):
                            last = n_ == len(ktl) - 1
                            v0 = VA[:, kt, g, 0:1]
                            vfull = bass.AP(v0.tensor, v0.offset, [[v0.ap[0][0], 128], [1, 128]])
                            add_dense_step(KTA[:, g, kt * 128:(kt + 1) * 128], QTA[:, h, 0:T],
                                           [r_kt[kt]] + r_qta[0:ns], vfull, [r_kt[kt]], bo, 128,
                                           n_ == 0, last,
                                           mkfin(bo, (0, 64), 64, AOA[k][0:64, h, 0:T], r_aoa[k]) if last else None)
                    stepsA = steps
                    steps = []
                    head_end = []
                    for h in range(8):
                        j = h // 2
                        half = h % 2
                        P0 = 64 * half
                        if half == 0:
                            l0, l1, dp, M = 0, 128, 64, 128
                            QTB = QTBe
                        else:
                            l0, l1, dp, M = 32, 160, 32, 128
                            QTB = QTBo
                        bo = rotO()
                        rq = [r_qtb[j]]
                        fin = mkfin(bo, (P0, P0 + 64), dp, AOB[k][P0:P0 + 64, j, 0:T], r_aob[k])
                        if isp:
                            ktl = [2 * sp_, 2 * sp_ + 1]
                        else:
                            ktl = [0, 1, 2, 3]
                        for n_, kt in enumerate(ktl):
                            last = isp and n_ == len(ktl) - 1
                            add_dense_step(KTB[:, j, kt * 128:(kt + 1) * 128], QTB[:, j, 0:T],
                                           [r_kt[kt]] + rq, LB[:, kt, j, l0:l1], [r_kt[kt]], bo, M, n_ == 0, last,
                                           fin if last else None)
                        if not isp:
                            for rr in range(8):
                                r = sp_ * 8 + rr
                                rs = min(max(r - 4, 0), 56)
                                if rs % 2 == 1:
                                    n0, cnt = rs - 1, 5
                                else:
                                    n0, cnt = rs, 4
                                blocks = []
                                for jj in range(cnt):
                                    n = n0 + 2 * jj
                                    kt = 4 + n // 2
                                    off = 512 + n * 64
                                    e_ = n - r + 7
                                    if cnt == 5 and jj == 0:
                                        e_ = 14
                                    elif cnt == 5 and jj == 4:
                                        e_ = 15
                                    blocks.append((KTB[:, j, off:off + 128], r_kt[kt], e_,
                                                   (ps[bo][0:M, rr * 64:(rr + 1) * 64], LB[:, kt, j, l0:l1])))
                                add_local_step(blocks, QTB[:, j, rr * 64:(rr + 1) * 64], rq, h, bo, M,
                                               fin if rr == 7 else None)
                    stepsB = steps
                    nop = lambda: None
                    spb = len(stepsB) // 8
                    inj = {}
                    if qi + 1 < len(qtiles):
                        pre = [lambda qn=qi + 1: loadB(qn)]
                    else:
                        pre = []
                    if ns == 4:
                        inj = {2: [lambda: aq_tr(0), lambda: aq_chain(2)],
                               4: [lambda: aq_tr(1), lambda: aq_chain(3)] + pre,
                               6: [lambda: aq_tr(2)], 8: [lambda: aq_tr(3)]}
                    else:
                        inj = {1: pre, 4: [lambda: aq_tr(0)], 8: [lambda: aq_tr(1)]}
                    steps = [(lambda: aq_chain(0), nop), (lambda: aq_chain(1), nop)]
                    for hh_ in range(8):
                        steps += stepsB[hh_ * spb:(hh_ + 1) * spb]
                        for fn_ in inj.get(hh_ + 1, []):
                            steps.append((fn_, nop))
                    steps += stepsA
                    LA = 3
                    for i_ in range(len(steps) + LA):
                        if i_ < len(steps):
                            steps[i_][0]()
                        for d_ in deferred:
                            d_[0] -= 1
                        while deferred and deferred[0][0] <= 0:
                            deferred.pop(0)[1]()
                        if i_ >= LA:
                            steps[i_ - LA][1]()
                    while deferred:
                        deferred.pop(0)[1]()
                    DMA("sp", AOAd[ti, :, :, co:co + T], AOA[k][0:64, :, 0:T], [r_aoa[k]], [], "aoa0")
                    DMA("sp", AOBd[ti, :, :, co:co + T], AOB[k][:, :, 0:T], [r_aob[k]], [], "aob0")
                P.flush()

            phaseB("s", qtS)
            phaseA("p", tilesP, False)
            phaseB("p", qtP)

        if stage < 3:
            P.flush(final=True)
            return nc

        with ExitStack() as st:
            wGA = sbuf(st, "wGA", [128, 8, 1024], BF16)
            wGB = sbuf(st, "wGB", [128, 8, 1024], BF16)
            wBRA = sbuf(st, "wBRA", [128, 8, 1024], BF16)
            wBRB = sbuf(st, "wBRB", [128, 4, 1024], BF16)
            wOUT = sbuf(st, "wOUT", [128, 8, 1024], BF16)
            r_w = P.res("wC1")
            wgv = WGb.rearrange("(c p) n -> p c n", p=128)
            r_wga, r_wgb, r_wbra, r_wbrb, r_wout = [P.res("wc1_%d" % i_) for i_ in range(5)]
            DMA("pool", wGA[:, :, :], wgv[:, :, 0:1024], [], [r_wga], "w0")
            DMA("pool", wBRA[0:64, :, :], WBRAb.rearrange("(h d) n -> d h n", d=64), [], [r_wbra], "w2")
            DMA("pool", wGB[:, :, :], wgv[:, :, 1024:2048], [], [r_wgb], "w1")
            DMA("pool", wBRB[:, :, :], WBRBb.rearrange("(c p) n -> p c n", p=128), [], [r_wbrb], "w3")
            DMA("pool", wOUT[:, :, :], WOUTb.rearrange("(c p) n -> p c n", p=128), [], [r_wout], "w4")
            G1 = [sbuf(st, "G1_%d" % v, [128, 1024], F32) for v in range(2)]
            for v in range(2):
                DMA("sp", G1[v][:, :], bass.AP(GMOD.tensor, v * 1024, [[0, 128], [1, 1024]]), [], [r_w], "c%d" % (5 + v))
            hT = [sbuf(st, "hTc%d" % k, [128, 8, 512], BF16) for k in range(2)]
            AOA = [sbuf(st, "AOAc%d" % k, [128, 8, 512], BF16) for k in range(2)]
            AOB = [sbuf(st, "AOBc%d" % k, [128, 4, 512], BF16) for k in range(2)]
            xsb = [sbuf(st, "xsb%d" % k, [128, 1024], F32) for k in range(4)]
            r_xsb = [P.res("xsb%d" % k) for k in range(4)]
            r_in = [P.res("inC%d" % k) for k in range(2)]
            sg = [sbuf(st, "sg%d" % k, [128, 512], F32) for k in range(2)]
            r_sg = [P.res("sg%d" % k) for k in range(2)]
            m1 = [sbuf(st, "m1_%d" % k, [128, 512], F32) for k in range(2)]
            r_m1 = [P.res("m1_%d" % k) for k in range(2)]
            m2 = [sbuf(st, "m2_%d" % k, [128, 512], F32) for k in range(2)]
            r_m2 = [P.res("m2_%d" % k) for k in range(2)]
            MT = [sbuf(st, "MT%d" % k, [128, 8, 512], BF16) for k in range(2)]
            r_mt = [[P.res("MT%d_%d" % (k, f)) for f in range(8)] for k in range(2)]
            x1 = [sbuf(st, "x1_%d" % k, [128, 1024], F32) for k in range(2)]
            r_x1 = [P.res("x1_%d" % k) for k in range(2)]
            tt_ = [sbuf(st, "ttc%d" % k, [128, 512], F32) for k in range(2)]
            r_tt = [P.res("ttc%d" % k) for k in range(2)]
            junk = sbuf(st, "junkc", [128, 1024], BF16)
            r_junk = P.res("junkc")
            stat = [sbuf(st, "statc%d" % k, [128, 4], F32) for k in range(2)]
            r_stat = [P.res("statc%d" % k) for k in range(2)]
            xn = [sbuf(st, "xnc%d" % k, [128, 1024], F32) for k in range(2)]
            r_xn = [P.res("xnc%d" % k) for k in range(2)]
            h2T = [sbuf(st, "h2T%d" % k, [128, 8, 512], BF16) for k in range(2)]
            r_h2 = [[P.res("h2T%d_%d" % (k, s)) for s in range(4)] for k in range(2)]
            rot = Rot([0, 1, 2, 3, 4, 5, 6, 7])

            def loadC1(i):
                k = i % 2
                DMA("sp", hT[k][:, :, :], HT1[i, :, :, :], [], [r_in[k]], "c1h%d" % k)
                DMA("sp", AOA[k][0:64, :, :], AOAd[i, :, :, :], [], [r_in[k]], "c1a%d" % k)
                DMA("sp", AOB[k][:, :, :], AOBd[i, :, :, :], [], [r_in[k]], "c1b%d" % k)

            def loadX(i):
                tg0, v = tile_info(i)
                for s_ in range(4):
                    DMA("sp", xsb[s_][:, :], xrows(tg0 + s_ * 128, 128), [], [r_xsb[s_]], "c1x%d" % s_)

            gi_ = [0]

            def fchunk(i, f):
                k = i % 2
                fs = slice(f * 128, (f + 1) * 128)
                b = rot()
                for c in range(8):
                    MM(ps[b][:, :], wGA[:, c, fs], hT[k][:, c, :], c == 0, c == 7, [r_in[k], r_wga], [r_ps[b]])
                ga = gi_[0] % 2
                gi_[0] += 1
                ACT(sg[ga][:, :], ps[b][:, :], AF.Sigmoid, [r_ps[b]], [r_sg[ga]])
                b = rot()
                for h in range(8):
                    MM(ps[b][:, :], wBRA[0:64, h, fs], AOA[k][0:64, h, :], h == 0, h == 7, [r_in[k], r_wbra], [r_ps[b]])
                mi = f % 2
                TT(m1[mi][:, :], sg[ga][:, :], ps[b][:, :], ALU.mult, [r_sg[ga], r_ps[b]], [r_m1[mi]])
                b = rot()
                for c in range(8):
                    MM(ps[b][:, :], wGB[:, c, fs], hT[k][:, c, :], c == 0, c == 7, [r_in[k], r_wgb], [r_ps[b]])
                gb = gi_[0] % 2
                gi_[0] += 1
                ACT(sg[gb][:, :], ps[b][:, :], AF.Sigmoid, [r_ps[b]], [r_sg[gb]])
                b = rot()
                for j in range(4):
                    MM(ps[b][:, :], wBRB[:, j, fs], AOB[k][:, j, :], j == 0, j == 3, [r_in[k], r_wbrb], [r_ps[b]])
                TT(m2[mi][:, :], sg[gb][:, :], ps[b][:, :], ALU.mult, [r_sg[gb], r_ps[b]], [r_m2[mi]])
                TT(MT[k][:, f, :], m1[mi][:, :], m2[mi][:, :], ALU.add, [r_m1[mi], r_m2[mi]], [r_mt[k][f]], eng="pool")

            def outproj(i, s):
                tg0, v = tile_info(i)
                k = i % 2
                q = s % 2
                for hh in range(2):
                    b = rot()
                    for c in range(8):
                        MM(ps[b][:, :], MT[k][:, c, s * 128:(s + 1) * 128], wOUT[:, c, hh * 512:(hh + 1) * 512],
                           c == 0, c == 7, r_mt[k] + [r_wout], [r_ps[b]])
                    TT(tt_[hh][:, :], ps[b][:, :], G1[v][:, hh * 512:(hh + 1) * 512], ALU.mult, [r_ps[b], r_w],
                       [r_tt[hh]])
                    TT(x1[q][:, hh * 512:(hh + 1) * 512], tt_[hh][:, :], xsb[s][:, hh * 512:(hh + 1) * 512],
                       ALU.add, [r_tt[hh], r_xsb[s]], [r_x1[q]], eng="pool")
                DMA("sp", X1[tg0 + s * 128:tg0 + (s + 1) * 128, :], x1[q][:, :], [r_x1[q]], [], "x1s%d" % q)

            def chainC(i, s):
                q = s % 2
                hT_chain(x1[q][:, :], r_x1[q], junk, r_junk, stat[q], r_stat[q], xn[q], r_xn[q])

            def trC(i, s):
                tg0, v = tile_info(i)
                k = i % 2
                q = s % 2
                hT_tr(xn[q], r_xn[q], lambda c, k=k, s=s: h2T[k][:, c, s * 128:(s + 1) * 128], r_h2[k][s],
                      A2, SH2, v, rot)

            def storeH2(i):
                k = i % 2
                DMA("sp", HT2[i, :, :, :], h2T[k][:, :, :], r_h2[k], [], "h2s%d" % k)

            def tail_pieces(i):
                return [
                    [lambda: outproj(i, 0)],
                    [lambda: chainC(i, 0), lambda: outproj(i, 1)],
                    [lambda: trC(i, 0), lambda: chainC(i, 1)],
                    [lambda: outproj(i, 2)],
                    [lambda: trC(i, 1), lambda: chainC(i, 2)],
                    [lambda: outproj(i, 3)],
                    [lambda: trC(i, 2), lambda: chainC(i, 3)],
                    [lambda: trC(i, 3), lambda: storeH2(i)],
                ]

            loadC1(0)
            loadC1(1)
            for f in range(8):
                fchunk(0, f)
            for i in range(10):
                if i + 2 < 10:
                    loadC1(i + 2)
                loadX(i)
                pieces = tail_pieces(i)
                for f in range(8):
                    if i + 1 < 10:
                        fchunk(i + 1, f)
                    for fn_ in pieces[f]:
                        fn_()
            P.flush()

        if stage < 4:
            P.flush(final=True)
            return nc

        with ExitStack() as st:
            W1 = sbuf(st, "W1", [128, 8, 4096], BF16)
            W2 = sbuf(st, "W2", [128, 32, 1024], BF16)
            r_w = P.res("wC2")
            w1v = W1b.rearrange("(c p) n -> p c n", p=128)
            w2v = W2b.rearrange("(c p) n -> p c n", p=128)
            r_w1 = [P.res("w1_%d" % q4) for q4 in range(4)]
            r_w2 = [P.res("w2_%d" % q4) for q4 in range(4)]
            for q4 in range(4):
                DMA("pool", W1[:, :, q4 * 1024:(q4 + 1) * 1024],
                    w1v[:, :, q4 * 1024:(q4 + 1) * 1024], [], [r_w1[q4]], "w%d" % q4)
            for q4 in range(4):
                DMA("pool", W2[:, q4 * 8:(q4 + 1) * 8, :], w2v[:, q4 * 8:(q4 + 1) * 8, :], [],
                    [r_w2[q4]], "w%d" % (4 + q4))
            G2 = [sbuf(st, "G2_%d" % v, [128, 1024], F32) for v in range(2)]
            GF = sbuf(st, "GF", [128, 1024], F32)
            for v in range(2):
                DMA("sp", G2[v][:, :], bass.AP(GMOD.tensor, (2 + v) * 1024, [[0, 128], [1, 1024]]), [], [r_w], "c%d" % (5 + v))
            DMA("sp", GF[:, :], bass.AP(gf.tensor, 0, [[0, 128], [1, 1024]]), [], [r_w], "c7")
            h2T = [sbuf(st, "h2d%d" % k, [128, 8, 256], BF16) for k in range(2)]
            x1t = [sbuf(st, "x1d%d" % k, [128, 2, 1024], F32) for k in range(2)]
            r_in = [P.res("inD%d" % k) for k in range(2)]
            UT = sbuf(st, "UT", [128, 32, 256], BF16)
            r_ut = [P.res("UT%d" % f) for f in range(16)]
            rl = [sbuf(st, "rl%d" % k, [128, 512], F32) for k in range(2)]
            r_rl = [P.res("rl%d" % k) for k in range(2)]
            tt_ = [sbuf(st, "ttd%d" % k, [128, 512], F32) for k in range(2)]
            r_tt = [P.res("ttd%d" % k) for k in range(2)]
            x2 = [sbuf(st, "x2_%d" % k, [128, 1024], F32) for k in range(2)]
            r_x2 = [P.res("x2_%d" % k) for k in range(2)]
            yo = [sbuf(st, "yo%d" % k, [128, 1024], F32) for k in range(2)]
            r_yo = [P.res("yo%d" % k) for k in range(2)]
            junk = sbuf(st, "junkd", [128, 1024], BF16)
            r_junk = P.res("junkd")
            stat = [sbuf(st, "statd%d" % k, [128, 4], F32) for k in range(2)]
            r_stat = [P.res("statd%d" % k) for k in range(2)]
            rot = Rot([0, 1, 2, 3, 4, 5, 6, 7])

            def loadC2(i):
                k = i % 2
                tg0 = i * 256
                DMA("sp", h2T[k][:, :, :], HT2[i // 2, :, :, (i % 2) * 256:(i % 2) * 256 + 256], [], [r_in[k]],
                    "c2h%d" % k)
                DMA("sp", x1t[k][:, :, :], X1[tg0:tg0 + 256, :].rearrange("(s p) d -> p s d", p=128), [], [r_in[k]],
                    "c2x%d" % k)

            loadC2(0)
            yi = 0
            for i in range(20):
                k = i % 2
                tg0 = i * 256
                v = 0 if tg0 < 4096 else 1
                if i + 1 < 20:
                    loadC2(i + 1)
                rin = [r_in[k], r_w]
                for fp in range(16):
                    b = rot()
                    for e2 in range(2):
                        f = fp * 2 + e2
                        for c in range(8):
                            MM(ps[b][:, e2 * 256:(e2 + 1) * 256], W1[:, c, f * 128:(f + 1) * 128], h2T[k][:, c, :],
                               c == 0, c == 7, [r_in[k], r_w1[f // 8]], [r_ps[b]])
                    a = fp % 2
                    ACT(rl[a][:, :], ps[b][:, :], AF.Relu, [r_ps[b]], [r_rl[a]])
                    TT(UT[:, fp * 2:fp * 2 + 2, :], rl[a][:, :].rearrange("p (e n) -> p e n", e=2),
                       rl[a][:, :].rearrange("p (e n) -> p e n", e=2), ALU.mult, [r_rl[a]], [r_ut[fp]],
                       eng=("pool" if fp % 2 else "dve"))
                for s in range(2):
                    q = yi % 2
                    yi += 1
                    for hh in range(2):
                        b = rot()
                        for f in range(32):
                            MM(ps[b][:, :], UT[:, f, s * 128:(s + 1) * 128], W2[:, f, hh * 512:(hh + 1) * 512],
                               f == 0, f == 31, r_ut + [r_w2[f // 8]], [r_ps[b]])
                        TT(tt_[hh][:, :], ps[b][:, :], G2[v][:, hh * 512:(hh + 1) * 512], ALU.mult, [r_ps[b], r_w],
                           [r_tt[hh]])
                        TT(x2[q][:, hh * 512:(hh + 1) * 512], tt_[hh][:, :], x1t[k][:, s, hh * 512:(hh + 1) * 512],
                           ALU.add, [r_tt[hh], r_in[k]], [r_x2[q]], eng="pool")
                    ACT(junk[:, :], x2[q][:, :], AF.Square, [r_x2[q]], [r_junk, r_stat[q]], scale=1.0 / 32.0,
                        accum=stat[q][:, 0:1])
                    TS(stat[q][:, 1:2], stat[q][:, 0:1], EPS, None, ALU.add, None, [r_stat[q]], [r_stat[q]])
                    ACT(stat[q][:, 2:3], stat[q][:, 1:2], AF.Sqrt, [r_stat[q]], [r_stat[q]])
                    RECIP(stat[q][:, 3:4], stat[q][:, 2:3], [r_stat[q]], [r_stat[q]])
                    STT(yo[q][:, :], x2[q][:, :], stat[q][:, 3:4], GF[:, :], ALU.mult, ALU.mult,
                        [r_x2[q], r_stat[q], r_w], [r_yo[q]])
                    DMA("sp", yrows(tg0 + s * 128, 128), yo[q][:, :], [r_yo[q]], [], "yo%d" % q)
            P.flush(final=True)
    return nc


def _rope_tables():
    pos = np.arange(4096)
    row = (pos // 64).astype(np.float32)
    col = (pos % 64).astype(np.float32)
    inv = (np.float32(10000.0) ** (-np.arange(16, dtype=np.float32) / np.float32(16))).astype(np.float32)
    ar = row[:, None] * inv
    ac = col[:, None] * inv
    cr, sr, cc, sc = np.cos(ar), np.sin(ar), np.cos(ac), np.sin(ac)
    C = np.concatenate([cr, cr, cc, cc], axis=1).astype(np.float32)
    S = np.concatenate([-sr, sr, -sc, sc], axis=1).astype(np.float32)
    out = np.empty((4096, 2, 512), np.float32)
    out[:, 0, :] = np.tile(C, (1, 8))
    out[:, 1, :] = np.tile(S, (1, 8))
    return out


def _bias_table(nat_bias):
    nb = nat_bias[0]
    kc = np.arange(64)[:, None]
    qc = np.arange(64)[None, :]
    cstart = np.clip(qc - 8, 0, 48)
    inwin = (kc >= cstart) & (kc < cstart + 16)
    dc = np.clip(kc - qc + 15, 0, 30)
    tab = np.full((2, 64, 8, 16, 64), NEG, np.float32)
    def blk(d):
        g = nb[:, d, :][:, dc]
        return np.where(inwin[None], g, np.float32(NEG)).transpose(1, 0, 2)
    for e in range(14):
        tab[0, :, :, e, :] = blk(e)
        tab[1, :, :, e, :] = blk(e + 1)
    tab[1, :, :, 14, :] = blk(3)
    tab[0, :, :, 15, :] = blk(10)
    return np.ascontiguousarray(tab.reshape(128, 8 * 16 * 64))


_NC_CACHE = {}


def make_in_maps(x_prompt, x_sample, cache_a_k, cache_a_v, cache_b_k, cache_b_v, c, c_ctx,
                 w_mod, b_mod, norm1_g, norm2_g, w_in, q_norm_g, k_norm_g, nat_bias,
                 w_br_a, w_br_b, w_out, w_mlp_in, w_mlp_out, final_norm_g):
    f = lambda a: np.ascontiguousarray(np.asarray(a, dtype=np.float32))
    rope = _rope_tables()
    btab = _bias_table(f(nat_bias))
    bm = f(b_mod)[0].reshape(48, 128).T
    shared = {
        "w_mod": f(w_mod)[0], "bmodT2": np.ascontiguousarray(np.repeat(bm, 2, axis=1)),
        "n1T2": np.ascontiguousarray(np.repeat(f(norm1_g)[0].reshape(8, 128).T, 2, axis=1)),
        "n2T2": np.ascontiguousarray(np.repeat(f(norm2_g)[0].reshape(8, 128).T, 2, axis=1)),
        "w_in": f(w_in)[0], "qg8": np.ascontiguousarray(np.tile(f(q_norm_g)[0], 8)),
        "kg2": np.ascontiguousarray(np.tile(f(k_norm_g)[0], 2)), "btab": btab,
        "w_br_a": f(w_br_a)[0], "w_br_b": f(w_br_b)[0], "w_out": f(w_out)[0],
        "w1": f(w_mlp_in)[0], "w2": f(w_mlp_out)[0], "gf": f(final_norm_g),
        "ident": np.eye(128, dtype=np.float32), "rope": rope,
    }
    xs_, xp_ = f(x_sample), f(x_prompt)
    cc = f(c_ctx)
    maps = []
    for b in range(8):
        cT = np.empty((128, 8, 2), np.float32)
        cT[:, :, 0] = f(c)[b].reshape(8, 128).T
        cT[:, :, 1] = cc.reshape(8, 128).T
        m = dict(shared)
        m.update({
            "xs": xs_[b], "xp": np.ascontiguousarray(xp_[4 * b:4 * b + 4].reshape(1024, 1024)),
            "cak": np.ascontiguousarray(f(cache_a_k)[b, 0].reshape(512, 128)),
            "cav": np.ascontiguousarray(f(cache_a_v)[b, 0].reshape(512, 128)),
            "cbk": np.ascontiguousarray(f(cache_b_k)[b, 0].reshape(512, 512)),
            "cbv": np.ascontiguousarray(f(cache_b_v)[b, 0].reshape(512, 512)),
            "cT": np.ascontiguousarray(cT.reshape(128, 16)),
        })
        maps.append(m)
    return maps


def kernel(**inputs):
    if "nc" not in _NC_CACHE:
        _NC_CACHE["nc"] = build()
    nc = _NC_CACHE["nc"]
    maps = make_in_maps(**inputs)
    res = run_bass_kernel_spmd(nc, maps, core_ids=list(range(8)))
    R = res.results
    y_sample = np.stack([R[b]["ys"] for b in range(8)], axis=0)
    y_prompt = np.concatenate([R[b]["yp"].reshape(4, 256, 1024) for b in range(8)], axis=0)
    nak = np.concatenate([R[b]["nak"].reshape(4, 1, 256, 2, 64) for b in range(8)], axis=0)
    nav = np.concatenate([R[b]["nav"].reshape(4, 1, 256, 2, 64) for b in range(8)], axis=0)
    nbk = np.concatenate([R[b]["nbk"].reshape(4, 1, 256, 8, 64) for b in range(8)], axis=0)
    nbv = np.concatenate([R[b]["nbv"].reshape(4, 1, 256, 8, 64) for b in range(8)], axis=0)
    return (y_prompt.astype(np.float32), y_sample.astype(np.float32), nak.astype(np.float32),
            nav.astype(np.float32), nbk.astype(np.float32), nbv.astype(np.float32))
```

```python
from contextlib import ExitStack
import numpy as np
import concourse.bass as bass
import concourse.mybir as mybir
from concourse.bass_utils import run_bass_kernel_spmd

F32 = mybir.dt.float32
BF16 = mybir.dt.bfloat16
AF = mybir.ActivationFunctionType
ALU = mybir.AluOpType
AX = mybir.AxisListType

ENGS = ("pe", "act", "dve", "pool", "sp")
EPS = 1e-6
NEG = -30000.0


class Res:
    __slots__ = ("name", "w", "r", "excl")

    def __init__(self, name, excl=False):
        self.name = name
        self.w = {}
        self.r = []
        self.excl = excl


class Ins:
    __slots__ = ("eng", "fn", "deps", "dma", "stream", "flag", "sem", "val", "waits", "clock")

    def __init__(self, eng, fn, dma, stream):
        self.eng = eng
        self.fn = fn
        self.deps = []
        self.dma = dma
        self.stream = stream
        self.flag = False
        self.sem = None
        self.val = 0
        self.waits = []
        self.clock = None


class Prog:
    def __init__(self, nc, stack):
        self.nc = nc
        self.stack = stack
        self.pending = []
        self.esem = {}
        for e in ENGS[:4]:
            self.esem[e] = stack.enter_context(nc.semaphore("sem_" + e))
        self.ecount = {e: 0 for e in ENGS}
        self.ssem = {}
        self.scount = {}
        self.know = {e: {} for e in ENGS}
        self.last = {e: None for e in ENGS}
        self.last_dma = {}
        self.barrier_deps = {e: [] for e in ENGS}
        self.all_res = []
        self.n_ins = 0
        self.n_wait = 0

    def res(self, name, excl=False):
        r = Res(name, excl)
        self.all_res.append(r)
        return r

    def add(self, eng, fn, reads=(), writes=(), dma=False, stream=None):
        ins = Ins(eng, fn, dma, stream)
        if dma:
            ins.flag = True
        writes = list(writes) + [r for r in reads if r.excl]
        reads = [r for r in reads if not r.excl]
        deps = []
        for r in reads:
            for w in r.w.values():
                deps.append((w, "raw"))
        for r in writes:
            for w in r.w.values():
                deps.append((w, "waw"))
            for rd in r.r:
                deps.append((rd, "war"))
        for d in self.barrier_deps[eng]:
            deps.append((d, "raw"))
        self.barrier_deps[eng] = []
        seen = set()
        for d, kind in deps:
            if d is ins or id(d) in seen:
                continue
            if (not d.dma) and (not dma) and d.eng == eng and eng == "pe":
                continue
            seen.add(id(d))
            d.flag = True
            ins.deps.append(d)
        for r in reads:
            r.r.append(ins)
        key = ("d", stream) if dma else eng
        for r in writes:
            r.w[key] = ins
            r.r = []
        self.pending.append(ins)
        if dma:
            self.last_dma[stream] = ins
        else:
            self.last[eng] = ins
        return ins

    def barrier(self):
        alls = [i for i in self.last.values() if i is not None] + list(self.last_dma.values())
        for e in ENGS:
            self.barrier_deps[e] = list(alls)

    @staticmethod
    def _semkey(ins):
        return ("s", ins.stream) if ins.dma else ("e", ins.eng)

    def flush(self, final=False):
        nc = self.nc
        lasts = [i for i in self.last.values() if i is not None] + list(self.last_dma.values())
        for d in lasts:
            if d.val == 0:
                d.flag = True
        if final:
            fin = Ins("sp", None, False, None)
            fin.deps = lasts
            self.pending.append(fin)
        per = {e: [] for e in ENGS}
        for ins in self.pending:
            e = ins.eng
            K = self.know[e]
            for d in ins.deps:
                key = self._semkey(d)
                assert d.val > 0, "dep not yet numbered"
                if K.get(key, 0) >= d.val:
                    continue
                ins.waits.append((d.sem, d.val))
                for k2, v2 in d.clock.items():
                    if K.get(k2, 0) < v2:
                        K[k2] = v2
            if ins.flag:
                if ins.dma:
                    if ins.stream not in self.ssem:
                        self.ssem[ins.stream] = self.stack.enter_context(
                            nc.semaphore("sd_%d" % len(self.ssem)))
                        self.scount[ins.stream] = 0
                    self.scount[ins.stream] += 16
                    ins.sem = self.ssem[ins.stream]
                    ins.val = self.scount[ins.stream]
                else:
                    self.ecount[e] += 1
                    ins.sem = self.esem[e]
                    ins.val = self.ecount[e]
                ck = dict(K)
                ck[self._semkey(ins)] = ins.val
                ins.clock = ck
            per[e].append(ins)
            self.n_ins += 1
            self.n_wait += len(ins.waits)
        self.pending = []
        for r in self.all_res:
            r.w = {}
            r.r = []
        self.barrier()

        def replay(lst):
            def f(eng):
                for ins in lst:
                    if ins.fn is None:
                        for (s, v) in ins.waits:
                            eng.wait_ge(s, v)
                        continue
                    for (s, v) in ins.waits[:-1]:
                        eng.wait_ge(s, v)
                    r = ins.fn(eng)
                    if ins.waits:
                        s, v = ins.waits[-1]
                        r._wait_ge(s, v)
                    if ins.flag:
                        r.then_inc(ins.sem, 16 if ins.dma else 1)
            return f

        with nc.Block() as block:
            if per["sp"]:
                block.sync(replay(per["sp"]))
            if per["pool"]:
                block.gpsimd(replay(per["pool"]))
            if per["act"]:
                block.scalar(replay(per["act"]))
            if per["dve"]:
                block.vector(replay(per["dve"]))
            if per["pe"]:
                block.tensor(replay(per["pe"]))


def build(stage=99, debug=False):
    nc = bass.Bass("TRN2", target_bir_lowering=False)

    def din(name, shape, dt=F32):
        return nc.dram_tensor(name, list(shape), dt, kind="ExternalInput").ap()

    def dout(name, shape, dt=F32):
        return nc.dram_tensor(name, list(shape), dt, kind="ExternalOutput").ap()

    def dscr(name, shape, dt):
        return nc.dram_tensor(name, list(shape), dt).ap()

    xs = din("xs", [4096, 1024])
    xp = din("xp", [1024, 1024])
    cak = din("cak", [512, 128])
    cav = din("cav", [512, 128])
    cbk = din("cbk", [512, 512])
    cbv = din("cbv", [512, 512])
    cT = din("cT", [128, 16])
    w_mod = din("w_mod", [1024, 6144])
    bmodT2 = din("bmodT2", [128, 96])
    n1T2 = din("n1T2", [128, 16])
    n2T2 = din("n2T2", [128, 16])
    w_in = din("w_in", [1024, 4352])
    qg8 = din("qg8", [512])
    kg2 = din("kg2", [128])
    btab = din("btab", [128, 8 * 16 * 64])
    w_br_a = din("w_br_a", [512, 1024])
    w_br_b = din("w_br_b", [512, 1024])
    w_out = din("w_out", [1024, 1024])
    w1 = din("w1", [1024, 4096])
    w2 = din("w2", [4096, 1024])
    gf = din("gf", [1024])
    ident = din("ident", [128, 128])
    rope = din("rope", [4096, 2, 512])

    ys = dout("ys", [4096, 1024])
    yp = dout("yp", [1024, 1024])
    nak = dout("nak", [1024, 128])
    nav = dout("nav", [1024, 128])
    nbk = dout("nbk", [1024, 512])
    nbv = dout("nbv", [1024, 512])

    HT1 = dscr("HT1", [10, 128, 8, 512], BF16)
    HT2 = dscr("HT2", [10, 128, 8, 512], BF16)
    AOAd = dscr("AOAd", [10, 64, 8, 512], BF16)
    AOBd = dscr("AOBd", [10, 128, 4, 512], BF16)
    X1 = dscr("X1", [5120, 1024], F32)
    GMOD = dscr("GMOD", [4, 1024], F32)

    def xrows(tg0, n):
        if tg0 < 4096:
            return xs[tg0:tg0 + n, :]
        return xp[tg0 - 4096:tg0 - 4096 + n, :]

    def yrows(tg0, n):
        if tg0 < 4096:
            return ys[tg0:tg0 + n, :]
        return yp[tg0 - 4096:tg0 - 4096 + n, :]

    with ExitStack() as top:
        P = Prog(nc, top)

        def sbuf(st, name, shape, dt):
            return st.enter_context(nc.sbuf_tensor(name, list(shape), dt))

        def MM(out, lhsT, rhs, start, stop, reads, writes):
            return P.add("pe", lambda e: e.matmul(out, lhsT=lhsT, rhs=rhs, start=start, stop=stop,
                                                  skip_group_check=True), reads, writes)

        def TR(out, in_, idn, reads, writes):
            return P.add("pe", lambda e: e.transpose(out=out, in_=in_, identity=idn), reads, writes)

        def ACT(out, in_, func, reads, writes, scale=None, accum=None):
            kw = {}
            if scale is not None:
                kw["scale"] = scale
            if accum is not None:
                kw["accum_out"] = accum
            return P.add("act", lambda e: e.activation(out=out, in_=in_, func=func, **kw), reads, writes)

        def TS(out, in0, s1, s2, op0, op1, reads, writes, eng="dve"):
            if s2 is None:
                return P.add(eng, lambda e: e.tensor_scalar(out=out, in0=in0, scalar1=s1, scalar2=None,
                                                            op0=op0), reads, writes)
            return P.add(eng, lambda e: e.tensor_scalar(out=out, in0=in0, scalar1=s1, scalar2=s2,
                                                        op0=op0, op1=op1), reads, writes)

        def TT(out, in0, in1, op, reads, writes, eng="dve"):
            return P.add(eng, lambda e: e.tensor_tensor(out=out, in0=in0, in1=in1, op=op), reads, writes)

        def STT(out, in0, scalar, in1, op0, op1, reads, writes, eng="dve"):
            return P.add(eng, lambda e: e.scalar_tensor_tensor(out=out, in0=in0, scalar=scalar, in1=in1,
                                                               op0=op0, op1=op1), reads, writes)

        def CP(out, in_, reads, writes, eng="dve"):
            return P.add(eng, lambda e: e.tensor_copy(out=out, in_=in_), reads, writes)

        def RECIP(out, in_, reads, writes):
            return P.add("dve", lambda e: e.reciprocal(out=out, in_=in_), reads, writes)

        def RED(out, in_, reads, writes):
            return P.add("dve", lambda e: e.tensor_reduce(out=out, in_=in_, axis=AX.X, op=ALU.add), reads, writes)

        def MEMSET(ap, val, writes, eng="dve"):
            return P.add(eng, lambda e: e.memset(ap, val), [], writes)

        def DMA(q, out, in_, reads, writes, stream, slow=False):
            if slow:
                return P.add(q, lambda e: e.dma_start(out=out, in_=in_, allow_slow_non_contiguous=True),
                             reads, writes, dma=True, stream=stream)
            return P.add(q, lambda e: e.dma_start(out=out, in_=in_), reads, writes, dma=True, stream=stream)

        ps = [top.enter_context(nc.psum_tensor("ps%d" % i, [128, 512], F32)) for i in range(8)]
        r_ps = [P.res("ps%d" % i, excl=True) for i in range(8)]

        class Rot:
            def __init__(self, idxs):
                self.idxs = idxs
                self.i = 0

            def __call__(self):
                k = self.idxs[self.i % len(self.idxs)]
                self.i += 1
                return k

        idf = sbuf(top, "idf", [128, 128], F32)
        idb = sbuf(top, "idb", [128, 128], BF16)
        onesf = sbuf(top, "onesf", [128, 128], F32)
        epst = sbuf(top, "epst", [128, 1], F32)
        MODS = sbuf(top, "MODS", [128, 4, 16], F32)
        r_c = P.res("consts")

        def tile_info(i):
            return (i * 512, 0 if i < 8 else 1)

        with ExitStack() as st:
            cTt = sbuf(st, "cTt", [128, 16], F32)
            sT = sbuf(st, "sT", [128, 16], BF16)
            wm = [sbuf(st, "wm%d" % k, [128, 8, 512], BF16) for k in range(2)]
            r_wm = [P.res("wm%d" % k) for k in range(2)]
            modT = sbuf(st, "modT", [128, 96], F32)
            bmt = sbuf(st, "bmt", [128, 96], F32)
            n1t = sbuf(st, "n1t", [128, 16], F32)
            n2t = sbuf(st, "n2t", [128, 16], F32)
            r_l = P.res("a0loads")
            r_sT = P.res("sT")
            r_mod = P.res("modT")
            DMA("sp", idf[:, :], ident[:, :], [], [r_c], "c0")
            DMA("sp", cTt[:, :], cT[:, :], [], [r_l], "c1")
            DMA("sp", bmt[:, :], bmodT2[:, :], [], [r_l], "c2")
            DMA("sp", n1t[:, :], n1T2[:, :], [], [r_l], "c3")
            DMA("sp", n2t[:, :], n2T2[:, :], [], [r_l], "c4")
            MEMSET(onesf[:, :], 1.0, [r_c])
            MEMSET(epst[:, :], EPS, [r_c])
            CP(idb[:, :], idf[:, :], [r_c], [r_c])
            ACT(sT[:, :], cTt[:, :], AF.Silu, [r_l], [r_sT])
            wmv = w_mod.rearrange("(c p) n -> p c n", p=128)
            for k in range(12):
                DMA("pool", wm[k % 2][:, :, :], wmv[:, :, k * 512:(k + 1) * 512], [], [r_wm[k % 2]], "wm%d" % (k % 2))
                for j in range(4):
                    fc = 4 * k + j
                    for c in range(8):
                        MM(ps[0][:, fc * 2:fc * 2 + 2], wm[k % 2][:, c, j * 128:(j + 1) * 128],
                           sT[:, c * 2:c * 2 + 2], c == 0, c == 7, [r_wm[k % 2], r_sT], [r_ps[0]])
            TT(modT[:, :], ps[0][:, 0:96], bmt[:, :], ALU.add, [r_ps[0], r_l], [r_mod])
            STT(MODS[:, 0, :], modT[:, 16:32], 1.0, n1t[:, :], ALU.add, ALU.mult, [r_mod, r_l], [r_c])
            CP(MODS[:, 1, :], modT[:, 0:16], [r_mod], [r_c])
            STT(MODS[:, 2, :], modT[:, 64:80], 1.0, n2t[:, :], ALU.add, ALU.mult, [r_mod, r_l], [r_c])
            CP(MODS[:, 3, :], modT[:, 48:64], [r_mod], [r_c])
            for which, base in ((0, 32), (1, 80)):
                for v in range(2):
                    row = which * 2 + v
                    dst = bass.AP(GMOD.tensor, row * 1024, [[1, 128], [128, 8]])
                    s0 = modT[:, base + v:base + v + 1]
                    src = bass.AP(s0.tensor, s0.offset, [[s0.ap[0][0], 128], [2, 8]])
                    DMA("sp", dst, src, [r_mod], [], "gm%d" % row, slow=True)
            P.flush()

        A1 = lambda c, v: MODS[:, 0, c * 2 + v:c * 2 + v + 1]
        SH1 = lambda c, v: MODS[:, 1, c * 2 + v:c * 2 + v + 1]
        A2 = lambda c, v: MODS[:, 2, c * 2 + v:c * 2 + v + 1]
        SH2 = lambda c, v: MODS[:, 3, c * 2 + v:c * 2 + v + 1]

        def hT_chain(src_ap, r_src, junk, r_junk, stat, r_stat, xn, r_xn):
            ACT(junk[:, :], src_ap, AF.Square, [r_src], [r_junk, r_stat], scale=1.0 / 32.0, accum=stat[:, 0:1])
            TS(stat[:, 1:2], stat[:, 0:1], EPS, None, ALU.add, None, [r_stat], [r_stat])
            ACT(stat[:, 2:3], stat[:, 1:2], AF.Sqrt, [r_stat], [r_stat])
            RECIP(stat[:, 3:4], stat[:, 2:3], [r_stat], [r_stat])
            ACT(xn[:, :], src_ap, AF.Copy, [r_src, r_stat], [r_xn], scale=stat[:, 3:4])

        def hT_tr(xn, r_xn, dst_fn, r_dst, Afn, Sfn, v, rot):
            for half in range(2):
                b = rot()
                for cc in range(4):
                    c = half * 4 + cc
                    TR(ps[b][:, cc * 128:(cc + 1) * 128], xn[:, c * 128:(c + 1) * 128], idf[:, :],
                       [r_xn, r_c], [r_ps[b]])
                for cc in range(4):
                    c = half * 4 + cc
                    TS(dst_fn(c), ps[b][:, cc * 128:(cc + 1) * 128], Afn(c, v), Sfn(c, v), ALU.mult, ALU.add,
                       [r_ps[b], r_c], [r_dst])

        if stage < 1:
            P.flush(final=True)
            return nc

        with ExitStack() as kv:
            KTA = sbuf(kv, "KTA", [128, 2, 4608], BF16)
            VA = sbuf(kv, "VA", [128, 37, 2, 65], BF16)
            KTB = sbuf(kv, "KTB", [128, 4, 4608], BF16)
            LB = sbuf(kv, "LB", [128, 36, 4, 160], BF16)
            r_kt = [P.res("kt%d" % t) for t in range(36)]

            def phaseA(tag, tilesA, do_ctx):
              with ExitStack() as st0:
                _sb = sbuf
                def sbuf_(st_, name, shape, dt):
                    return _sb(st_, name + tag, shape, dt)
                st = st0
                wA = sbuf_(st, "wA", [128, 8, 256], BF16)
                wBK = sbuf_(st, "wBK", [128, 8, 512], BF16)
                wBV = sbuf_(st, "wBV", [128, 8, 512], BF16)
                r_w = P.res("wA")
                wv = w_in.rearrange("(c p) n -> p c n", p=128)
                DMA("pool", wA[:, :, :], wv[:, :, 512:768], [], [r_w], "w0")
                DMA("pool", wBK[:, :, :], wv[:, :, 1280:1792], [], [r_w], "w1")
                DMA("pool", wBV[:, :, :], wv[:, :, 1792:2304], [], [r_w], "w2")
                kgt = sbuf_(st, "kgt", [128, 128], F32)
                DMA("sp", kgt[:, :], bass.AP(kg2.tensor, 0, [[0, 128], [1, 128]]), [], [r_w], "c5")
                if do_ctx:
                    MEMSET(VA[:, :, :, :], 1.0, r_kt)
                    MEMSET(KTA[64:128, :, :], 0.0, r_kt)
                    MEMSET(LB[:, :, :, 64:96], 0.0, r_kt)
                    MEMSET(LB[:, :, :, 64:65], 1.0, r_kt)

                xt = [sbuf_(st, "xt%d" % k, [128, 4, 1024], F32) for k in range(2)]
                r_xt = [P.res("xt%d" % k) for k in range(2)]
                junk = sbuf_(st, "junk", [128, 1024], BF16)
                r_junk = P.res("junk")
                stat = [sbuf_(st, "stat%d" % k, [128, 4], F32) for k in range(2)]
                r_stat = [P.res("stat%d" % k) for k in range(2)]
                xn = [sbuf_(st, "xn%d" % k, [128, 1024], F32) for k in range(2)]
                r_xn = [P.res("xn%d" % k) for k in range(2)]
                hT = [sbuf_(st, "hT%d" % k, [128, 8, 512], BF16) for k in range(2)]
                r_hT = [[P.res("hT%d_%d" % (k, s)) for s in range(4)] for k in range(2)]
                ropeT = [sbuf_(st, "ropeT%d" % k, [128, 2, 128], F32) for k in range(2)]
                r_rope = [P.res("rope%d" % k) for k in range(2)]
                akf = [sbuf_(st, "akf%d" % k, [128, 128], F32) for k in range(2)]
                r_akf = [P.res("akf%d" % k) for k in range(2)]
                sqk = sbuf_(st, "sqk", [128, 128], F32)
                kst = [sbuf_(st, "kst%d" % k, [128, 8], F32) for k in range(2)]
                akn = [sbuf_(st, "akn%d" % k, [128, 128], F32) for k in range(2)]
                r_akn = [P.res("akn%d" % k) for k in range(2)]
                akr = [sbuf_(st, "akr%d" % k, [128, 128], F32) for k in range(2)]
                r_akr = [P.res("akr%d" % k) for k in range(2)]
                t1 = sbuf_(st, "t1", [128, 128], F32)
                t2 = sbuf_(st, "t2", [128, 128], F32)
                r_tmp = P.res("tmpA")
                r_t1 = P.res("t1A")
                r_t2 = P.res("t2A")
                r_kst = [P.res("kst%d" % k) for k in range(2)]
                stg = [sbuf_(st, "stg%d" % k, [128, 512], F32) for k in range(3)]
                r_stg = [P.res("stg%d" % k) for k in range(3)]
                stg_i = [0]
                ctx32 = [sbuf_(st, "ctx32_%d" % k, [128, 512], F32) for k in range(2)]
                r_ctx = [P.res("ctx32_%d" % k) for k in range(2)]
                rot = Rot([0, 1, 2, 3, 4, 5, 6, 7])

                def next_stg():
                    k = stg_i[0] % 3
                    stg_i[0] += 1
                    return k

                for t in (range(4) if do_ctx else []):
                    rk = [r_kt[t]]
                    a = ctx32[t % 2]
                    ra = r_ctx[t % 2]
                    DMA("sp", a[:, 0:128], cak[t * 128:(t + 1) * 128, :], [], [ra], "cx0")
                    b = rot()
                    for g in range(2):
                        TR(ps[b][0:64, g * 128:(g + 1) * 128], a[:, g * 64:(g + 1) * 64], idf[:, :], [ra, r_c], [r_ps[b]])
                    CP(KTA[0:64, :, t * 128:(t + 1) * 128], ps[b][0:64, 0:256].rearrange("p (g n) -> p g n", g=2),
                       [r_ps[b]], rk)
                    DMA("sp", a[:, 128:256], cav[t * 128:(t + 1) * 128, :], [], [ra], "cx1")
                    CP(VA[:, t, :, 0:64], a[:, 128:256].rearrange("p (g d) -> p g d", g=2), [ra], rk)
                    a2 = ctx32[(t + 1) % 2]
                    ra2 = r_ctx[(t + 1) % 2]
                    DMA("sp", a2[:, :], cbk[t * 128:(t + 1) * 128, :], [], [ra2], "cx2")
                    b = rot()
                    for j in range(4):
                        TR(ps[b][:, j * 128:(j + 1) * 128], a2[:, j * 128:(j + 1) * 128], idf[:, :], [ra2, r_c], [r_ps[b]])
                    CP(KTB[:, :, t * 128:(t + 1) * 128], ps[b][:, :].rearrange("p (j n) -> p j n", j=4), [r_ps[b]], rk)
                    DMA("sp", a[:, :], cbv[t * 128:(t + 1) * 128, :], [], [ra], "cx3")
                    av4 = a[:, :].rearrange("p (j e d) -> p j e d", j=4, e=2)
                    CP(LB[:, t, :, 0:64], av4[:, :, 0, :], [ra], rk)
                    CP(LB[:, t, :, 96:160], av4[:, :, 1, :], [ra], rk)


                def loadA(idx):
                    tg0, T, v, kb, isp, pr0 = tilesA[idx]
                    k = idx % 2
                    ns = T // 128
                    DMA("sp", xt[k][:, 0:ns, :], xrows(tg0, T).rearrange("(s p) d -> p s d", p=128), [], [r_xt[k]],
                        "xt%d" % k)

                pend_tr = [None]
                nT = len(tilesA)

                def chainA(idx, s_):
                    k_ = idx % 2
                    q_ = s_ % 2
                    hT_chain(xt[k_][:, s_, :], r_xt[k_], junk, r_junk, stat[q_], r_stat[q_], xn[q_], r_xn[q_])

                def trA(idx, s_):
                    k_ = idx % 2
                    q_ = s_ % 2
                    v_ = tilesA[idx][2]
                    hT_tr(xn[q_], r_xn[q_], lambda c, k_=k_, s_=s_: hT[k_][:, c, s_ * 128:(s_ + 1) * 128],
                          r_hT[k_][s_], A1, SH1, v_, rot)

                def bounceA(idx):
                    tg0, T, v, kb, isp, pr0 = tilesA[idx]
                    k_ = idx % 2
                    ns_ = T // 128
                    ti = tg0 // 512
                    co = tg0 % 512
                    DMA("sp", HT1[ti, :, :, co:co + T], hT[k_][:, :, 0:T], r_hT[k_][0:ns_], [], "ht%d" % k_)

                def stage2_sub(idx, s):
                    tg0, T, v, kb, isp, pr0 = tilesA[idx]
                    k = idx % 2
                    q = s % 2
                    kt = (kb + s * 128) // 128
                    rk = [r_kt[kt]]
                    koff = kb + s * 128
                    rh = [r_hT[k][s], r_w]
                    b = rot()
                    for c in range(8):
                        MM(ps[b][:, 0:256], hT[k][:, c, s * 128:(s + 1) * 128], wA[:, c, :], c == 0, c == 7,
                           rh, [r_ps[b]])
                    ACT(akf[q][:, :], ps[b][:, 0:128], AF.Copy, [r_ps[b]], [r_akf[q]])
                    CP(VA[:, kt, :, 0:64], ps[b][:, 128:256].rearrange("p (g d) -> p g d", g=2), [r_ps[b]], rk)
                    if isp:
                        sk = next_stg()
                        ACT(stg[sk][:, 0:128], ps[b][:, 128:256], AF.Copy, [r_ps[b]], [r_stg[sk]])
                        DMA("sp", nav[pr0 + s * 128:pr0 + (s + 1) * 128, :], stg[sk][:, 0:128], [r_stg[sk]], [],
                            "stg%d" % sk)
                    TT(sqk[:, :], akf[q][:, :], akf[q][:, :], ALU.mult, [r_akf[q]], [r_tmp], eng="pool")
                    RED(kst[q][:, 0:2], sqk[:, :].rearrange("p (g d) -> p g d", g=2), [r_tmp], [r_kst[q]])
                    TS(kst[q][:, 2:4], kst[q][:, 0:2], 1.0 / 64.0, EPS, ALU.mult, ALU.add, [r_kst[q]], [r_kst[q]])
                    ACT(kst[q][:, 4:6], kst[q][:, 2:4], AF.Sqrt, [r_kst[q]], [r_kst[q]])
                    RECIP(kst[q][:, 6:8], kst[q][:, 4:6], [r_kst[q]], [r_kst[q]])
                    for g in range(2):
                        TS(akn[q][:, g * 64:(g + 1) * 64], akf[q][:, g * 64:(g + 1) * 64], kst[q][:, 6 + g:7 + g],
                           None, ALU.mult, None, [r_akf[q], r_kst[q]], [r_akn[q]])
                    TT(akn[q][:, :], akn[q][:, :], kgt[:, :], ALU.mult, [r_akn[q], r_w], [r_akn[q]], eng="pool")
                    if isp:
                        DMA("sp", nak[pr0 + s * 128:pr0 + (s + 1) * 128, :], akn[q][:, :], [r_akn[q]], [],
                            "akn%d" % q)
                        ksrc, rks = akn[q], r_akn[q]
                    else:
                        DMA("sp", ropeT[q][:, :, :], rope[tg0 + s * 128:tg0 + (s + 1) * 128, :, 0:128], [],
                            [r_rope[q]], "rope%d" % q)
                        xv_ = akn[q][:, :].rearrange("p (a h d) -> p a h d", a=4, h=2)
                        sv_ = ropeT[q][:, 1, :].rearrange("p (a h d) -> p a h d", a=4, h=2)
                        t2v = t2[:, :].rearrange("p (a h d) -> p a h d", a=4, h=2)
                        TT(t1[:, :], akn[q][:, :], ropeT[q][:, 0, :], ALU.mult, [r_akn[q], r_rope[q]], [r_t1], eng="pool")
                        TT(t2v[:, :, 0, :], xv_[:, :, 1, :], sv_[:, :, 0, :], ALU.mult, [r_akn[q], r_rope[q]], [r_t2], eng="pool")
                        TT(t2v[:, :, 1, :], xv_[:, :, 0, :], sv_[:, :, 1, :], ALU.mult, [r_akn[q], r_rope[q]], [r_t2], eng="pool")
                        TT(akr[q][:, :], t1[:, :], t2[:, :], ALU.add, [r_t1, r_t2], [r_akr[q]], eng="pool")
                        ksrc, rks = akr[q], r_akr[q]

                    def k_tr(ksrc=ksrc, rks=rks, koff=koff, rk=rk):
                        b2 = rot()
                        for g in range(2):
                            TR(ps[b2][0:64, g * 128:(g + 1) * 128], ksrc[:, g * 64:(g + 1) * 64], idf[:, :],
                               [rks, r_c], [r_ps[b2]])
                        ACT(KTA[0:64, :, koff:koff + 128],
                            ps[b2][0:64, 0:256].rearrange("p (g n) -> p g n", g=2), AF.Copy, [r_ps[b2]], rk)
                    b = rot()
                    for c in range(8):
                        MM(ps[b][:, :], hT[k][:, c, s * 128:(s + 1) * 128], wBV[:, c, :], c == 0, c == 7, rh, [r_ps[b]])
                    pv4 = ps[b][:, :].rearrange("p (j e d) -> p j e d", j=4, e=2)
                    ACT(LB[:, kt, :, 0:64], pv4[:, :, 0, :], AF.Copy, [r_ps[b]], rk)
                    CP(LB[:, kt, :, 96:160], pv4[:, :, 1, :], [r_ps[b]], rk)
                    if isp:
                        sk = next_stg()
                        ACT(stg[sk][:, :], ps[b][:, :], AF.Copy, [r_ps[b]], [r_stg[sk]])
                        DMA("sp", nbv[pr0 + s * 128:pr0 + (s + 1) * 128, :], stg[sk][:, :], [r_stg[sk]], [],
                            "stg%d" % sk)
                        b = rot()
                        for c in range(8):
                            MM(ps[b][:, :], hT[k][:, c, s * 128:(s + 1) * 128], wBK[:, c, :], c == 0, c == 7, rh,
                               [r_ps[b]])
                        sk = next_stg()
                        CP(stg[sk][:, :], ps[b][:, :], [r_ps[b]], [r_stg[sk]])
                        DMA("sp", nbk[pr0 + s * 128:pr0 + (s + 1) * 128, :], stg[sk][:, :], [r_stg[sk]], [],
                            "stg%d" % sk)
                    if pend_tr[0] is not None:
                        pend_tr[0]()
                    pend_tr[0] = k_tr

                def stage2_tail(idx):
                    tg0, T, v, kb, isp, pr0 = tilesA[idx]
                    k = idx % 2
                    ns = T // 128
                    if pend_tr[0] is not None:
                        pend_tr[0]()
                        pend_tr[0] = None
                    kts = [r_kt[(kb + s * 128) // 128] for s in range(ns)]
                    for j in range(4):
                        b = rot()
                        for c in range(8):
                            MM(ps[b][:, 0:T], wBK[:, c, j * 128:(j + 1) * 128], hT[k][:, c, 0:T], c == 0, c == 7,
                               r_hT[k][0:ns] + [r_w], [r_ps[b]])
                        if j % 2 == 0:
                            ACT(KTB[:, j, kb:kb + T], ps[b][:, 0:T], AF.Copy, [r_ps[b]], kts)
                        else:
                            CP(KTB[:, j, kb:kb + T], ps[b][:, 0:T], [r_ps[b]], kts)

                nsA = tilesA[0][1] // 128
                loadA(0)
                if nT > 1:
                    loadA(1)
                chainA(0, 0)
                for s in range(nsA):
                    if s + 1 < nsA:
                        chainA(0, s + 1)
                    trA(0, s)
                bounceA(0)
                for idx in range(nT):
                    nxt = idx + 1 < nT
                    if idx + 2 < nT:
                        loadA(idx + 2)
                    if nxt:
                        chainA(idx + 1, 0)
                    for s in range(nsA):
                        stage2_sub(idx, s)
                        if nxt:
                            if s + 1 < nsA:
                                chainA(idx + 1, s + 1)
                            trA(idx + 1, s)
                    stage2_tail(idx)
                    if nxt:
                        bounceA(idx + 1)
                P.flush()

            tilesS = [(i * 512, 512, 0, 512 + i * 512, False, 0) for i in range(8)]
            tilesP = [(4096 + p * 256, 256, 1, p * 256, True, p * 256) for p in range(4)]
            qtS = [(i, 0, 512, False, i) for i in range(8)]
            qtP = [(8 + p // 2, (p % 2) * 256, 256, True, p) for p in range(4)]
            phaseA("s", tilesS, True)
            if stage < 2:
                if debug:
                    dbg = dout("dbg_kta", [128, 2, 4608], BF16)
                    dbg2 = dout("dbg_ktb", [128, 4, 4608], BF16)
                    dbg3 = dout("dbg_va", [128, 37 * 2 * 65], BF16)
                    dbg4 = dout("dbg_lb", [128, 36 * 4 * 160], BF16)
                    DMA("sp", dbg[:, :, :], KTA[:, :, :], [], [], "dbg0")
                    DMA("sp", dbg2[:, :, :], KTB[:, :, :], [], [], "dbg1")
                    DMA("sp", dbg3[:, :], VA[:, :, :, :].rearrange("p a b c -> p (a b c)"), [], [], "dbg2")
                    DMA("sp", dbg4[:, :], LB[:, :, :, :].rearrange("p a b c -> p (a b c)"), [], [], "dbg3")
                P.flush(final=True)
                return nc

            def phaseB(tag, qtiles):
              with ExitStack() as st0:
                _sb = sbuf
                def sbuf_(st_, name, shape, dt):
                    return _sb(st_, name + tag, shape, dt)
                st = st0
                wAQ = sbuf_(st, "wAQ", [128, 8, 512], BF16)
                wBQ = sbuf_(st, "wBQ", [128, 8, 512], BF16)
                r_w = P.res("wB")
                wv = w_in.rearrange("(c p) n -> p c n", p=128)
                DMA("pool", wAQ[:, :, :], wv[:, :, 0:512], [], [r_w], "w0")
                DMA("pool", wBQ[:, :, :], wv[:, :, 768:1280], [], [r_w], "w1")
                BT = sbuf_(st, "BT", [128, 8, 16, 64], BF16)
                DMA("pool", BT[:, :, :, :], btab.rearrange("p (h e q) -> p h e q", h=8, e=16), [], [r_w], "w2")
                qgt = sbuf_(st, "qgt", [128, 512], F32)
                DMA("sp", qgt[:, :], bass.AP(qg8.tensor, 0, [[0, 128], [1, 512]]), [], [r_w], "c5")

                hT = [sbuf_(st, "hTb%d" % k, [128, 8, 512], BF16) for k in range(1)] * 2
                r_hT = [P.res("hTb%d" % k) for k in range(1)] * 2
                QTA = sbuf_(st, "QTA", [128, 8, 512], BF16)
                r_qta = [P.res("qta%d" % s) for s in range(4)]
                QTBe = sbuf_(st, "QTBe", [128, 4, 512], BF16)
                QTBo = sbuf_(st, "QTBo", [128, 4, 512], BF16)
                r_qtb = [P.res("qtb%d" % j) for j in range(4)]
                MEMSET(QTA[64:128, :, :], 0.0, r_qta)
                MEMSET(QTBe[64:128, :, :], 0.0, r_qtb)
                MEMSET(QTBo[0:64, :, :], 0.0, r_qtb)
                PT = [sbuf_(st, "PT%d" % k, [128, 512], BF16) for k in range(4)]
                r_pt = [P.res("PT%d" % k) for k in range(4)]
                pt_i = [0]
                AOA = [sbuf_(st, "AOA%d" % k, [128, 8, 512], BF16) for k in range(1)] * 2
                r_aoa = [P.res("AOA%d" % k) for k in range(1)] * 2
                AOB = [sbuf_(st, "AOB%d" % k, [128, 4, 512], BF16) for k in range(1)] * 2
                r_aob = [P.res("AOB%d" % k) for k in range(1)] * 2
                ropeQ = [sbuf_(st, "ropeQ%d" % k, [128, 2, 512], F32) for k in range(1)] * 2
                r_rope = [P.res("ropeQ%d" % k) for k in range(1)] * 2
                aqf = [sbuf_(st, "aqf%d" % k, [128, 512], F32) for k in range(1)] * 2
                r_aqf = [P.res("aqf%d" % k) for k in range(1)] * 2
                aqn = [sbuf_(st, "aqn%d" % k, [128, 512], F32) for k in range(2)]
                r_aqn = [P.res("aqn%d" % k) for k in range(2)]
                tq1 = sbuf_(st, "tq1", [128, 512], F32)
                tq2 = sbuf_(st, "tq2", [128, 512], F32)
                qst = [sbuf_(st, "qst%d" % k, [128, 32], F32) for k in range(2)]
                r_tmp = P.res("tmpB")
                r_tq1 = P.res("tq1B")
                r_tq2 = P.res("tq2B")
                r_qst = [P.res("qstB%d" % k) for k in range(2)]
                oT = [sbuf_(st, "oT%d" % k, [128, 512], F32) for k in range(2)]
                r_oT = [P.res("oT%d" % k) for k in range(2)]
                rrow = [sbuf_(st, "rrow%d" % k, [128, 512], F32) for k in range(2)]
                r_rrow = [P.res("rrow%d" % k) for k in range(2)]
                fin_i = [0]
                deferred = []
                defer_n = [2]
                rotS = Rot([0, 1, 2, 3])
                rotO = Rot([4, 5])
                rotX = Rot([6, 7])

                def next_pt():
                    k = pt_i[0] % 4
                    pt_i[0] += 1
                    return k

                def finalize(bo, T, rows, dp, dst_ap, r_dst):
                    f = fin_i[0] % 2
                    fin_i[0] += 1
                    r0, r1 = rows
                    ACT(rrow[f][dp:dp + 1, 0:T], ps[bo][dp:dp + 1, 0:T], AF.Ln, [r_ps[bo]], [r_rrow[f]])
                    ACT(rrow[f][dp:dp + 1, 0:T], rrow[f][dp:dp + 1, 0:T], AF.Exp, [r_rrow[f]], [r_rrow[f]], scale=-1.0)
                    CP(oT[f][r0:r1, 0:T], ps[bo][r0:r1, 0:T], [r_ps[bo]], [r_oT[f]])

                    def part_b():
                        bx = rotX()
                        MM(ps[bx][:, 0:T], onesf[dp:dp + 1, :], rrow[f][dp:dp + 1, 0:T], True, True,
                           [r_rrow[f], r_c], [r_ps[bx]])
                        TT(dst_ap, oT[f][r0:r1, 0:T], ps[bx][r0:r1, 0:T], ALU.mult, [r_oT[f], r_ps[bx]], [r_dst])
                    deferred.append([defer_n[0], part_b])


                def loadB(qi):
                    ti, co, T, isp, sp_ = qtiles[qi]
                    k = qi % 2
                    DMA("sp", hT[k][:, :, 0:T], HT1[ti, :, :, co:co + T], [], [r_hT[k]], "hb0")

                loadB(0)
                for qi in range(len(qtiles)):
                    ti, co, T, isp, sp_ = qtiles[qi]
                    k = qi % 2
                    ns = T // 128
                    defer_n[0] = 2 if isp else 6
                    rh = [r_hT[k], r_w]
                    def aq_chain(s):
                        q = s % 2
                        b = rotS()
                        for c in range(8):
                            MM(ps[b][:, :], hT[k][:, c, s * 128:(s + 1) * 128], wAQ[:, c, :], c == 0, c == 7, rh, [r_ps[b]])
                        CP(aqf[q][:, :], ps[b][:, :], [r_ps[b]], [r_aqf[q]])
                        TT(tq1[:, :], aqf[q][:, :], aqf[q][:, :], ALU.mult, [r_aqf[q]], [r_tq1], eng="pool")
                        RED(qst[q][:, 0:8], tq1[:, :].rearrange("p (g d) -> p g d", g=8), [r_tq1], [r_qst[q]])
                        TS(qst[q][:, 8:16], qst[q][:, 0:8], 1.0 / 64.0, EPS, ALU.mult, ALU.add, [r_qst[q]], [r_qst[q]])
                        ACT(qst[q][:, 16:24], qst[q][:, 8:16], AF.Ln, [r_qst[q]], [r_qst[q]])
                        ACT(qst[q][:, 24:32], qst[q][:, 16:24], AF.Exp, [r_qst[q]], [r_qst[q]], scale=-0.5)
                        for h in range(8):
                            TS(aqn[q][:, h * 64:(h + 1) * 64], aqf[q][:, h * 64:(h + 1) * 64], qst[q][:, 24 + h:25 + h],
                               0.125, ALU.mult, ALU.mult, [r_aqf[q], r_qst[q]], [r_aqn[q]])
                        TT(aqn[q][:, :], aqn[q][:, :], qgt[:, :], ALU.mult, [r_aqn[q], r_w], [r_aqn[q]], eng="pool")
                        if not isp:
                            t0 = sp_ * 512 + s * 128
                            DMA("sp", ropeQ[q][:, :, :], rope[t0:t0 + 128, :, :], [], [r_rope[q]], "ropeq0")
                            xv_ = aqn[q][:, :].rearrange("p (a h d) -> p a h d", a=16, h=2)
                            sv_ = ropeQ[q][:, 1, :].rearrange("p (a h d) -> p a h d", a=16, h=2)
                            t2v = tq2[:, :].rearrange("p (a h d) -> p a h d", a=16, h=2)
                            TT(tq1[:, :], aqn[q][:, :], ropeQ[q][:, 0, :], ALU.mult, [r_aqn[q], r_rope[q]], [r_tq1], eng="pool")
                            TT(t2v[:, :, 0, :], xv_[:, :, 1, :], sv_[:, :, 0, :], ALU.mult, [r_aqn[q], r_rope[q]], [r_tq2], eng="pool")
                            TT(t2v[:, :, 1, :], xv_[:, :, 0, :], sv_[:, :, 1, :], ALU.mult, [r_aqn[q], r_rope[q]], [r_tq2], eng="pool")
                            TT(aqn[q][:, :], tq1[:, :], tq2[:, :], ALU.add, [r_tq1, r_tq2], [r_aqn[q]], eng="pool")

                    def aq_tr(s):
                        q = s % 2
                        for hb in range(2):
                            b2 = rotS()
                            for hh in range(4):
                                h = hb * 4 + hh
                                TR(ps[b2][0:64, hh * 128:(hh + 1) * 128], aqn[q][:, h * 64:(h + 1) * 64], idf[:, :],
                                   [r_aqn[q], r_c], [r_ps[b2]])
                            CP(QTA[0:64, hb * 4:hb * 4 + 4, s * 128:(s + 1) * 128],
                               ps[b2][0:64, :].rearrange("p (g n) -> p g n", g=4), [r_ps[b2]], [r_qta[s]])
                    for j in range(4):
                        b = rotS()
                        for c in range(8):
                            MM(ps[b][:, 0:T], wBQ[:, c, j * 128:(j + 1) * 128], hT[k][:, c, 0:T], c == 0, c == 7, rh,
                               [r_ps[b]])
                        TS(QTBe[0:64, j, 0:T], ps[b][0:64, 0:T], 0.125, None, ALU.mult, None, [r_ps[b]], [r_qtb[j]])
                        TS(QTBo[64:128, j, 0:T], ps[b][64:128, 0:T], 0.125, None, ALU.mult, None, [r_ps[b]], [r_qtb[j]])
                    steps = []

                    def add_dense_step(KT_ap, Q_ap, rd, V_ap, vr, bo, M, first, last, fin):
                        cell = {}

                        def front():
                            b_ = rotS()
                            MM(ps[b_][:, 0:T], KT_ap, Q_ap, True, True, rd, [r_ps[b_]])
                            pk = next_pt()
                            cell["pk"] = pk
                            ACT(PT[pk][:, 0:T], ps[b_][:, 0:T], AF.Exp, [r_ps[b_]], [r_pt[pk]])

                        def back():
                            pk = cell["pk"]
                            MM(ps[bo][0:M, 0:T], V_ap, PT[pk][:, 0:T], first, last, vr + [r_pt[pk]], [r_ps[bo]])
                            if fin is not None:
                                fin()
                        steps.append((front, back))

                    def add_local_step(blocks, Qcols, rq_, h_, bo, M, fin):
                        cell = {}
                        cnt = len(blocks)

                        def front():
                            b_ = rotS()
                            for jj, (KT_ap, rk_, e_, V_ap) in enumerate(blocks):
                                MM(ps[b_][:, jj * 64:(jj + 1) * 64], KT_ap, Qcols, True, False, [rk_] + rq_, [r_ps[b_]])
                                MM(ps[b_][:, jj * 64:(jj + 1) * 64], idb[:, :], BT[:, h_, e_, :], False, True,
                                   [r_c, r_w], [r_ps[b_]])
                            pk = next_pt()
                            cell["pk"] = pk
                            ACT(PT[pk][:, 0:cnt * 64], ps[b_][:, 0:cnt * 64], AF.Exp, [r_ps[b_]], [r_pt[pk]])

                        def back():
                            pk = cell["pk"]
                            for jj, (KT_ap, rk_, e_, V_ap) in enumerate(blocks):
                                MM(V_ap[0], V_ap[1], PT[pk][:, jj * 64:(jj + 1) * 64], False, jj == cnt - 1,
                                   [rk_, r_pt[pk]], [r_ps[bo]])
                            if fin is not None:
                                fin()
                        steps.append((front, back))

                    def mkfin(bo, rows, dp, dst, r_dst):
                        return lambda: finalize(bo, T, rows, dp, dst, r_dst)

                    steps = []
                    if isp:
                        ktl = [2 * sp_, 2 * sp_ + 1]
                    else:
                        ktl = list(range(36))
                    for h in range(8):
                        g = h // 4
                        bo = rotO()
                        for n_, kt in enumerate(ktl):
                            last = n_ == len(ktl) - 1
                            v0 = VA[:, kt, g, 0:1]
                            vfull = bass.AP(v0.tensor, v0.offset, [[v0.ap[0][0], 128], [1, 128]])
                            add_dense_step(KTA[:, g, kt * 128:(kt + 1) * 128], QTA[:, h, 0:T],
                                           [r_kt[kt]] + r_qta[0:ns], vfull, [r_kt[kt]], bo, 128,
                                           n_ == 0, last,
                                           mkfin(bo, (0, 64), 64, AOA[k][0:64, h, 0:T], r_aoa[k]) if last else None)
                    stepsA = steps
                    steps = []
                    head_end = []
                    for h in range(8):
                        j = h // 2
                        half = h % 2
                        P0 = 64 * half
                        if half == 0:
                            l0, l1, dp, M = 0, 128, 64, 128
                            QTB = QTBe
                        else:
                            l0, l1, dp, M = 32, 160, 32, 128
                            QTB = QTBo
                        bo = rotO()
                        rq = [r_qtb[j]]
                        fin = mkfin(bo, (P0, P0 + 64), dp, AOB[k][P0:P0 + 64, j, 0:T], r_aob[k])
                        if isp:
                            ktl = [2 * sp_, 2 * sp_ + 1]
                        else:
                            ktl = [0, 1, 2, 3]
                        for n_, kt in enumerate(ktl):
                            last = isp and n_ == len(ktl) - 1
                            add_dense_step(KTB[:, j, kt * 128:(kt + 1) * 128], QTB[:, j, 0:T],
                                           [r_kt[kt]] + rq, LB[:, kt, j, l0:l1], [r_kt[kt]], bo, M, n_ == 0, last,
                                           fin if last else None)
                        if not isp:
                            for rr in range(8):
                                r = sp_ * 8 + rr
                                rs = min(max(r - 4, 0), 56)
                                if rs % 2 == 1:
                                    n0, cnt = rs - 1, 5
                                else:
                                    n0, cnt = rs, 4
                                blocks = []
                                for jj in range(cnt):
                                    n = n0 + 2 * jj
                                    kt = 4 + n // 2
                                    off = 512 + n * 64
                                    e_ = n - r + 7
                                    if cnt == 5 and jj == 0:
                                        e_ = 14
                                    elif cnt == 5 and jj == 4:
                                        e_ = 15
                                    blocks.append((KTB[:, j, off:off + 128], r_kt[kt], e_,
                                                   (ps[bo][0:M, rr * 64:(rr + 1) * 64], LB[:, kt, j, l0:l1])))
                                add_local_step(blocks, QTB[:, j, rr * 64:(rr + 1) * 64], rq, h, bo, M,
                                               fin if rr == 7 else None)
                    stepsB = steps
                    nop = lambda: None
                    spb = len(stepsB) // 8
                    inj = {}
                    if qi + 1 < len(qtiles):
                        pre = [lambda qn=qi + 1: loadB(qn)]
                    else:
                        pre = []
                    if ns == 4:
                        inj = {2: [lambda: aq_tr(0), lambda: aq_chain(2)],
                               4: [lambda: aq_tr(1), lambda: aq_chain(3)] + pre,
                               6: [lambda: aq_tr(2)], 8: [lambda: aq_tr(3)]}
                    else:
                        inj = {1: pre, 4: [lambda: aq_tr(0)], 8: [lambda: aq_tr(1)]}
                    steps = [(lambda: aq_chain(0), nop), (lambda: aq_chain(1), nop)]
                    for hh_ in range(8):
                        steps += stepsB[hh_ * spb:(hh_ + 1) * spb]
                        for fn_ in inj.get(hh_ + 1, []):
                            steps.append((fn_, nop))
                    steps += stepsA
                    LA = 3
                    for i_ in range(len(steps) + LA):
                        if i_ < len(steps):
                            steps[i_][0]()
                        for d_ in deferred:
                            d_[0] -= 1
                        while deferred and deferred[0][0] <= 0:
                            deferred.pop(0)[1]()
                        if i_ >= LA:
                            steps[i_ - LA][1]()
                    while deferred:
                        deferred.pop(0)[1]()
                    DMA("sp", AOAd[ti, :, :, co:co + T], AOA[k][0:64, :, 0:T], [r_aoa[k]], [], "aoa0")
                    DMA("sp", AOBd[ti, :, :, co:co + T], AOB[k][:, :, 0:T], [r_aob[k]], [], "aob0")
                P.flush()

            phaseB("s", qtS)
            phaseA("p", tilesP, False)
            phaseB("p", qtP)

        if stage < 3:
            P.flush(final=True)
            return nc

        with ExitStack() as st:
            wGA = sbuf(st, "wGA", [128, 8, 1024], BF16)
            wGB = sbuf(st, "wGB", [128, 8, 1024], BF16)
            wBRA = sbuf(st, "wBRA", [128, 8, 1024], BF16)
            wBRB = sbuf(st, "wBRB", [128, 4, 1024], BF16)
            wOUT = sbuf(st, "wOUT", [128, 8, 1024], BF16)
            r_w = P.res("wC1")
            wv = w_in.rearrange("(c p) n -> p c n", p=128)
            r_wga, r_wgb, r_wbra, r_wbrb, r_wout = [P.res("wc1_%d" % i_) for i_ in range(5)]
            DMA("pool", wGA[:, :, :], wv[:, :, 2304:3328], [], [r_wga], "w0")
            DMA("pool", wBRA[0:64, :, :], w_br_a.rearrange("(h d) n -> d h n", d=64), [], [r_wbra], "w2")
            DMA("pool", wGB[:, :, :], wv[:, :, 3328:4352], [], [r_wgb], "w1")
            DMA("pool", wBRB[:, :, :], w_br_b.rearrange("(c p) n -> p c n", p=128), [], [r_wbrb], "w3")
            DMA("pool", wOUT[:, :, :], w_out.rearrange("(c p) n -> p c n", p=128), [], [r_wout], "w4")
            G1 = [sbuf(st, "G1_%d" % v, [128, 1024], F32) for v in range(2)]
            for v in range(2):
                DMA("sp", G1[v][:, :], bass.AP(GMOD.tensor, v * 1024, [[0, 128], [1, 1024]]), [], [r_w], "c%d" % (5 + v))
            hT = [sbuf(st, "hTc%d" % k, [128, 8, 512], BF16) for k in range(2)]
            AOA = [sbuf(st, "AOAc%d" % k, [128, 8, 512], BF16) for k in range(2)]
            AOB = [sbuf(st, "AOBc%d" % k, [128, 4, 512], BF16) for k in range(2)]
            xsb = [sbuf(st, "xsb%d" % k, [128, 1024], F32) for k in range(4)]
            r_xsb = [P.res("xsb%d" % k) for k in range(4)]
            r_in = [P.res("inC%d" % k) for k in range(2)]
            sg = [sbuf(st, "sg%d" % k, [128, 512], F32) for k in range(2)]
            r_sg = [P.res("sg%d" % k) for k in range(2)]
            m1 = [sbuf(st, "m1_%d" % k, [128, 512], F32) for k in range(2)]
            r_m1 = [P.res("m1_%d" % k) for k in range(2)]
            m2 = [sbuf(st, "m2_%d" % k, [128, 512], F32) for k in range(2)]
            r_m2 = [P.res("m2_%d" % k) for k in range(2)]
            MT = [sbuf(st, "MT%d" % k, [128, 8, 512], BF16) for k in range(2)]
            r_mt = [[P.res("MT%d_%d" % (k, f)) for f in range(8)] for k in range(2)]
            x1 = [sbuf(st, "x1_%d" % k, [128, 1024], F32) for k in range(2)]
            r_x1 = [P.res("x1_%d" % k) for k in range(2)]
            tt_ = [sbuf(st, "ttc%d" % k, [128, 512], F32) for k in range(2)]
            r_tt = [P.res("ttc%d" % k) for k in range(2)]
            junk = sbuf(st, "junkc", [128, 1024], BF16)
            r_junk = P.res("junkc")
            stat = [sbuf(st, "statc%d" % k, [128, 4], F32) for k in range(2)]
            r_stat = [P.res("statc%d" % k) for k in range(2)]
            xn = [sbuf(st, "xnc%d" % k, [128, 1024], F32) for k in range(2)]
            r_xn = [P.res("xnc%d" % k) for k in range(2)]
            h2T = [sbuf(st, "h2T%d" % k, [128, 8, 512], BF16) for k in range(2)]
            r_h2 = [[P.res("h2T%d_%d" % (k, s)) for s in range(4)] for k in range(2)]
            rot = Rot([0, 1, 2, 3, 4, 5, 6, 7])

            def loadC1(i):
                k = i % 2
                DMA("sp", hT[k][:, :, :], HT1[i, :, :, :], [], [r_in[k]], "c1h%d" % k)
                DMA("sp", AOA[k][0:64, :, :], AOAd[i, :, :, :], [], [r_in[k]], "c1a%d" % k)
                DMA("sp", AOB[k][:, :, :], AOBd[i, :, :, :], [], [r_in[k]], "c1b%d" % k)

            def loadX(i):
                tg0, v = tile_info(i)
                for s_ in range(4):
                    DMA("sp", xsb[s_][:, :], xrows(tg0 + s_ * 128, 128), [], [r_xsb[s_]], "c1x%d" % s_)

            gi_ = [0]

            def fchunk(i, f):
                k = i % 2
                fs = slice(f * 128, (f + 1) * 128)
                b = rot()
                for c in range(8):
                    MM(ps[b][:, :], wGA[:, c, fs], hT[k][:, c, :], c == 0, c == 7, [r_in[k], r_wga], [r_ps[b]])
                ga = gi_[0] % 2
                gi_[0] += 1
                ACT(sg[ga][:, :], ps[b][:, :], AF.Sigmoid, [r_ps[b]], [r_sg[ga]])
                b = rot()
                for h in range(8):
                    MM(ps[b][:, :], wBRA[0:64, h, fs], AOA[k][0:64, h, :], h == 0, h == 7, [r_in[k], r_wbra], [r_ps[b]])
                mi = f % 2
                TT(m1[mi][:, :], sg[ga][:, :], ps[b][:, :], ALU.mult, [r_sg[ga], r_ps[b]], [r_m1[mi]])
                b = rot()
                for c in range(8):
                    MM(ps[b][:, :], wGB[:, c, fs], hT[k][:, c, :], c == 0, c == 7, [r_in[k], r_wgb], [r_ps[b]])
                gb = gi_[0] % 2
                gi_[0] += 1
                ACT(sg[gb][:, :], ps[b][:, :], AF.Sigmoid, [r_ps[b]], [r_sg[gb]])
                b = rot()
                for j in range(4):
                    MM(ps[b][:, :], wBRB[:, j, fs], AOB[k][:, j, :], j == 0, j == 3, [r_in[k], r_wbrb], [r_ps[b]])
                TT(m2[mi][:, :], sg[gb][:, :], ps[b][:, :], ALU.mult, [r_sg[gb], r_ps[b]], [r_m2[mi]])
                TT(MT[k][:, f, :], m1[mi][:, :], m2[mi][:, :], ALU.add, [r_m1[mi], r_m2[mi]], [r_mt[k][f]], eng="pool")

            def outproj(i, s):
                tg0, v = tile_info(i)
                k = i % 2
                q = s % 2
                for hh in range(2):
                    b = rot()
                    for c in range(8):
                        MM(ps[b][:, :], MT[k][:, c, s * 128:(s + 1) * 128], wOUT[:, c, hh * 512:(hh + 1) * 512],
                           c == 0, c == 7, r_mt[k] + [r_wout], [r_ps[b]])
                    TT(tt_[hh][:, :], ps[b][:, :], G1[v][:, hh * 512:(hh + 1) * 512], ALU.mult, [r_ps[b], r_w],
                       [r_tt[hh]])
                    TT(x1[q][:, hh * 512:(hh + 1) * 512], tt_[hh][:, :], xsb[s][:, hh * 512:(hh + 1) * 512],
                       ALU.add, [r_tt[hh], r_xsb[s]], [r_x1[q]], eng="pool")
                DMA("sp", X1[tg0 + s * 128:tg0 + (s + 1) * 128, :], x1[q][:, :], [r_x1[q]], [], "x1s%d" % q)

            def chainC(i, s):
                q = s % 2
                hT_chain(x1[q][:, :], r_x1[q], junk, r_junk, stat[q], r_stat[q], xn[q], r_xn[q])

            def trC(i, s):
                tg0, v = tile_info(i)
                k = i % 2
                q = s % 2
                hT_tr(xn[q], r_xn[q], lambda c, k=k, s=s: h2T[k][:, c, s * 128:(s + 1) * 128], r_h2[k][s],
                      A2, SH2, v, rot)

            def storeH2(i):
                k = i % 2
                DMA("sp", HT2[i, :, :, :], h2T[k][:, :, :], r_h2[k], [], "h2s%d" % k)

            def tail_pieces(i):
                return [
                    [lambda: outproj(i, 0)],
                    [lambda: chainC(i, 0), lambda: outproj(i, 1)],
                    [lambda: trC(i, 0), lambda: chainC(i, 1)],
                    [lambda: outproj(i, 2)],
                    [lambda: trC(i, 1), lambda: chainC(i, 2)],
                    [lambda: outproj(i, 3)],
                    [lambda: trC(i, 2), lambda: chainC(i, 3)],
                    [lambda: trC(i, 3), lambda: storeH2(i)],
                ]

            loadC1(0)
            loadC1(1)
            for f in range(8):
                fchunk(0, f)
            for i in range(10):
                if i + 2 < 10:
                    loadC1(i + 2)
                loadX(i)
                pieces = tail_pieces(i)
                for f in range(8):
                    if i + 1 < 10:
                        fchunk(i + 1, f)
                    for fn_ in pieces[f]:
                        fn_()
            P.flush()

        if stage < 4:
            P.flush(final=True)
            return nc

        with ExitStack() as st:
            W1 = sbuf(st, "W1", [128, 8, 4096], BF16)
            W2 = sbuf(st, "W2", [128, 32, 1024], BF16)
            r_w = P.res("wC2")
            w1v = w1.rearrange("(c p) n -> p c n", p=128)
            w2v = w2.rearrange("(c p) n -> p c n", p=128)
            r_w1 = [P.res("w1_%d" % q4) for q4 in range(4)]
            r_w2 = [P.res("w2_%d" % q4) for q4 in range(4)]
            for q4 in range(4):
                DMA("pool", W1[:, :, q4 * 1024:(q4 + 1) * 1024], w1v[:, :, q4 * 1024:(q4 + 1) * 1024], [], [r_w1[q4]],
                    "w%d" % q4)
            for q4 in range(4):
                DMA("pool", W2[:, q4 * 8:(q4 + 1) * 8, :], w2v[:, q4 * 8:(q4 + 1) * 8, :], [], [r_w2[q4]],
                    "w%d" % (4 + q4))
            G2 = [sbuf(st, "G2_%d" % v, [128, 1024], F32) for v in range(2)]
            GF = sbuf(st, "GF", [128, 1024], F32)
            for v in range(2):
                DMA("sp", G2[v][:, :], bass.AP(GMOD.tensor, (2 + v) * 1024, [[0, 128], [1, 1024]]), [], [r_w], "c%d" % (5 + v))
            DMA("sp", GF[:, :], bass.AP(gf.tensor, 0, [[0, 128], [1, 1024]]), [], [r_w], "c7")
            h2T = [sbuf(st, "h2d%d" % k, [128, 8, 256], BF16) for k in range(2)]
            x1t = [sbuf(st, "x1d%d" % k, [128, 2, 1024], F32) for k in range(2)]
            r_in = [P.res("inD%d" % k) for k in range(2)]
            UT = sbuf(st, "UT", [128, 32, 256], BF16)
            r_ut = [P.res("UT%d" % f) for f in range(16)]
            rl = [sbuf(st, "rl%d" % k, [128, 512], F32) for k in range(2)]
            r_rl = [P.res("rl%d" % k) for k in range(2)]
            tt_ = [sbuf(st, "ttd%d" % k, [128, 512], F32) for k in range(2)]
            r_tt = [P.res("ttd%d" % k) for k in range(2)]
            x2 = [sbuf(st, "x2_%d" % k, [128, 1024], F32) for k in range(2)]
            r_x2 = [P.res("x2_%d" % k) for k in range(2)]
            yo = [sbuf(st, "yo%d" % k, [128, 1024], F32) for k in range(2)]
            r_yo = [P.res("yo%d" % k) for k in range(2)]
            junk = sbuf(st, "junkd", [128, 1024], BF16)
            r_junk = P.res("junkd")
            stat = [sbuf(st, "statd%d" % k, [128, 4], F32) for k in range(2)]
            r_stat = [P.res("statd%d" % k) for k in range(2)]
            rot = Rot([0, 1, 2, 3, 4, 5, 6, 7])

            def loadC2(i):
                k = i % 2
                tg0 = i * 256
                DMA("sp", h2T[k][:, :, :], HT2[i // 2, :, :, (i % 2) * 256:(i % 2) * 256 + 256], [], [r_in[k]],
                    "c2h%d" % k)
                DMA("sp", x1t[k][:, :, :], X1[tg0:tg0 + 256, :].rearrange("(s p) d -> p s d", p=128), [], [r_in[k]],
                    "c2x%d" % k)

            loadC2(0)
            yi = 0
            for i in range(20):
                k = i % 2
                tg0 = i * 256
                v = 0 if tg0 < 4096 else 1
                if i + 1 < 20:
                    loadC2(i + 1)
                rin = [r_in[k], r_w]
                for fp in range(16):
                    b = rot()
                    for e2 in range(2):
                        f = fp * 2 + e2
                        for c in range(8):
                            MM(ps[b][:, e2 * 256:(e2 + 1) * 256], W1[:, c, f * 128:(f + 1) * 128], h2T[k][:, c, :],
                               c == 0, c == 7, [r_in[k], r_w1[f // 8]], [r_ps[b]])
                    a = fp % 2
                    ACT(rl[a][:, :], ps[b][:, :], AF.Relu, [r_ps[b]], [r_rl[a]])
                    TT(UT[:, fp * 2:fp * 2 + 2, :], rl[a][:, :].rearrange("p (e n) -> p e n", e=2),
                       rl[a][:, :].rearrange("p (e n) -> p e n", e=2), ALU.mult, [r_rl[a]], [r_ut[fp]],
                       eng=("pool" if fp % 2 else "dve"))
                for s in range(2):
                    q = yi % 2
                    yi += 1
                    for hh in range(2):
                        b = rot()
                        for f in range(32):
                            MM(ps[b][:, :], UT[:, f, s * 128:(s + 1) * 128], W2[:, f, hh * 512:(hh + 1) * 512],
                               f == 0, f == 31, r_ut + [r_w2[f // 8]], [r_ps[b]])
                        TT(tt_[hh][:, :], ps[b][:, :], G2[v][:, hh * 512:(hh + 1) * 512], ALU.mult, [r_ps[b], r_w],
                           [r_tt[hh]])
                        TT(x2[q][:, hh * 512:(hh + 1) * 512], tt_[hh][:, :], x1t[k][:, s, hh * 512:(hh + 1) * 512],
                           ALU.add, [r_tt[hh], r_in[k]], [r_x2[q]], eng="pool")
                    ACT(junk[:, :], x2[q][:, :], AF.Square, [r_x2[q]], [r_junk, r_stat[q]], scale=1.0 / 32.0,
                        accum=stat[q][:, 0:1])
                    TS(stat[q][:, 1:2], stat[q][:, 0:1], EPS, None, ALU.add, None, [r_stat[q]], [r_stat[q]])
                    ACT(stat[q][:, 2:3], stat[q][:, 1:2], AF.Sqrt, [r_stat[q]], [r_stat[q]])
                    RECIP(stat[q][:, 3:4], stat[q][:, 2:3], [r_stat[q]], [r_stat[q]])
                    STT(yo[q][:, :], x2[q][:, :], stat[q][:, 3:4], GF[:, :], ALU.mult, ALU.mult,
                        [r_x2[q], r_stat[q], r_w], [r_yo[q]])
                    DMA("sp", yrows(tg0 + s * 128, 128), yo[q][:, :], [r_yo[q]], [], "yo%d" % q)
            P.flush(final=True)
    return nc


def _rope_tables():
    pos = np.arange(4096)
    row = (pos // 64).astype(np.float32)
    col = (pos % 64).astype(np.float32)
    inv = (np.float32(10000.0) ** (-np.arange(16, dtype=np.float32) / np.float32(16))).astype(np.float32)
    ar = row[:, None] * inv
    ac = col[:, None] * inv
    cr, sr, cc, sc = np.cos(ar), np.sin(ar), np.cos(ac), np.sin(ac)
    C = np.concatenate([cr, cr, cc, cc], axis=1).astype(np.float32)
    S = np.concatenate([-sr, sr, -sc, sc], axis=1).astype(np.float32)
    out = np.empty((4096, 2, 512), np.float32)
    out[:, 0, :] = np.tile(C, (1, 8))
    out[:, 1, :] = np.tile(S, (1, 8))
    return out


def _bias_table(nat_bias):
    nb = nat_bias[0]
    kc = np.arange(64)[:, None]
    qc = np.arange(64)[None, :]
    cstart = np.clip(qc - 8, 0, 48)
    inwin = (kc >= cstart) & (kc < cstart + 16)
    dc = np.clip(kc - qc + 15, 0, 30)
    tab = np.full((2, 64, 8, 16, 64), NEG, np.float32)
    def blk(d):
        g = nb[:, d, :][:, dc]
        return np.where(inwin[None], g, np.float32(NEG)).transpose(1, 0, 2)
    for e in range(14):
        tab[0, :, :, e, :] = blk(e)
        tab[1, :, :, e, :] = blk(e + 1)
    tab[1, :, :, 14, :] = blk(3)
    tab[0, :, :, 15, :] = blk(10)
    return np.ascontiguousarray(tab.reshape(128, 8 * 16 * 64))


_NC_CACHE = {}


def make_in_maps(x_prompt, x_sample, cache_a_k, cache_a_v, cache_b_k, cache_b_v, c, c_ctx,
                 w_mod, b_mod, norm1_g, norm2_g, w_in, q_norm_g, k_norm_g, nat_bias,
                 w_br_a, w_br_b, w_out, w_mlp_in, w_mlp_out, final_norm_g):
    f = lambda a: np.ascontiguousarray(np.asarray(a, dtype=np.float32))
    rope = _rope_tables()
    btab = _bias_table(f(nat_bias))
    bm = f(b_mod)[0].reshape(48, 128).T
    shared = {
        "w_mod": f(w_mod)[0], "bmodT2": np.ascontiguousarray(np.repeat(bm, 2, axis=1)),
        "n1T2": np.ascontiguousarray(np.repeat(f(norm1_g)[0].reshape(8, 128).T, 2, axis=1)),
        "n2T2": np.ascontiguousarray(np.repeat(f(norm2_g)[0].reshape(8, 128).T, 2, axis=1)),
        "w_in": f(w_in)[0], "qg8": np.ascontiguousarray(np.tile(f(q_norm_g)[0], 8)),
        "kg2": np.ascontiguousarray(np.tile(f(k_norm_g)[0], 2)), "btab": btab,
        "w_br_a": f(w_br_a)[0], "w_br_b": f(w_br_b)[0], "w_out": f(w_out)[0],
        "w1": f(w_mlp_in)[0], "w2": f(w_mlp_out)[0], "gf": f(final_norm_g),
        "ident": np.eye(128, dtype=np.float32), "rope": rope,
    }
    xs_, xp_ = f(x_sample), f(x_prompt)
    cc = f(c_ctx)
    maps = []
    for b in range(8):
        cT = np.empty((128, 8, 2), np.float32)
        cT[:, :, 0] = f(c)[b].reshape(8, 128).T
        cT[:, :, 1] = cc.reshape(8, 128).T
        m = dict(shared)
        m.update({
            "xs": xs_[b], "xp": np.ascontiguousarray(xp_[4 * b:4 * b + 4].reshape(1024, 1024)),
            "cak": np.ascontiguousarray(f(cache_a_k)[b, 0].reshape(512, 128)),
            "cav": np.ascontiguousarray(f(cache_a_v)[b, 0].reshape(512, 128)),
            "cbk": np.ascontiguousarray(f(cache_b_k)[b, 0].reshape(512, 512)),
            "cbv": np.ascontiguousarray(f(cache_b_v)[b, 0].reshape(512, 512)),
            "cT": np.ascontiguousarray(cT.reshape(128, 16)),
        })
        maps.append(m)
    return maps


def kernel(**inputs):
    if "nc" not in _NC_CACHE:
        _NC_CACHE["nc"] = build()
    nc = _NC_CACHE["nc"]
    maps = make_in_maps(**inputs)
    res = run_bass_kernel_spmd(nc, maps, core_ids=list(range(8)))
    R = res.results
    y_sample = np.stack([R[b]["ys"] for b in range(8)], axis=0)
    y_prompt = np.concatenate([R[b]["yp"].reshape(4, 256, 1024) for b in range(8)], axis=0)
    nak = np.concatenate([R[b]["nak"].reshape(4, 1, 256, 2, 64) for b in range(8)], axis=0)
    nav = np.concatenate([R[b]["nav"].reshape(4, 1, 256, 2, 64) for b in range(8)], axis=0)
    nbk = np.concatenate([R[b]["nbk"].reshape(4, 1, 256, 8, 64) for b in range(8)], axis=0)
    nbv = np.concatenate([R[b]["nbv"].reshape(4, 1, 256, 8, 64) for b in range(8)], axis=0)
    return (y_prompt.astype(np.float32), y_sample.astype(np.float32), nak.astype(np.float32),
            nav.astype(np.float32), nbk.astype(np.float32), nbv.astype(np.float32))
```

```python
from contextlib import ExitStack
import numpy as np
import concourse.bass as bass
import concourse.mybir as mybir
from concourse.bass_utils import run_bass_kernel_spmd

F32 = mybir.dt.float32
BF16 = mybir.dt.bfloat16
AF = mybir.ActivationFunctionType
ALU = mybir.AluOpType
AX = mybir.AxisListType

ENGS = ("pe", "act", "dve", "pool", "sp")
EPS = 1e-6
NEG = -30000.0


class Res:
    __slots__ = ("name", "w", "r", "excl")

    def __init__(self, name, excl=False):
        self.name = name
        self.w = {}
        self.r = []
        self.excl = excl


class Ins:
    __slots__ = ("eng", "fn", "deps", "dma", "stream", "flag", "sem", "val", "waits", "clock")

    def __init__(self, eng, fn, dma, stream):
        self.eng = eng
        self.fn = fn
        self.deps = []
        self.dma = dma
        self.stream = stream
        self.flag = False
        self.sem = None
        self.val = 0
        self.waits = []
        self.clock = None


class Prog:
    def __init__(self, nc, stack):
        self.nc = nc
        self.stack = stack
        self.pending = []
        self.esem = {}
        for e in ENGS[:4]:
            self.esem[e] = stack.enter_context(nc.semaphore("sem_" + e))
        self.ecount = {e: 0 for e in ENGS}
        self.ssem = {}
        self.scount = {}
        self.know = {e: {} for e in ENGS}
        self.last = {e: None for e in ENGS}
        self.last_dma = {}
        self.barrier_deps = {e: [] for e in ENGS}
        self.all_res = []
        self.n_ins = 0
        self.n_wait = 0

    def res(self, name, excl=False):
        r = Res(name, excl)
        self.all_res.append(r)
        return r

    def add(self, eng, fn, reads=(), writes=(), dma=False, stream=None):
        ins = Ins(eng, fn, dma, stream)
        if dma:
            ins.flag = True
        xreads = [r for r in reads if r.excl]
        reads = [r for r in reads if not r.excl]
        deps = []
        for r in reads:
            for w in r.w.values():
                deps.append((w, "raw"))
        for r in writes:
            for w in r.w.values():
                deps.append((w, "waw"))
            for rd in r.r:
                deps.append((rd, "war"))
        for r in xreads:
            for key_, w in r.w.items():
                if key_ != eng:
                    deps.append((w, "raw"))
        writes = list(writes) + xreads
        for d in self.barrier_deps[eng]:
            deps.append((d, "raw"))
        self.barrier_deps[eng] = []
        seen = set()
        for d, kind in deps:
            if d is ins or id(d) in seen:
                continue
            if (not d.dma) and (not dma) and d.eng == eng and eng == "pe":
                continue
            seen.add(id(d))
            d.flag = True
            ins.deps.append(d)
        for r in reads:
            r.r.append(ins)
        key = ("d", stream) if dma else eng
        for r in writes:
            r.w[key] = ins
            r.r = []
        self.pending.append(ins)
        if dma:
            self.last_dma[stream] = ins
        else:
            self.last[eng] = ins
        return ins

    def barrier(self):
        alls = [i for i in self.last.values() if i is not None] + list(self.last_dma.values())
        for e in ENGS:
            self.barrier_deps[e] = list(alls)

    @staticmethod
    def _semkey(ins):
        return ("s", ins.stream) if ins.dma else ("e", ins.eng)

    def flush(self, final=False):
        nc = self.nc
        lasts = [i for i in self.last.values() if i is not None] + list(self.last_dma.values())
        for d in lasts:
            if d.val == 0:
                d.flag = True
        if final:
            fin = Ins("sp", None, False, None)
            fin.deps = lasts
            self.pending.append(fin)
        per = {e: [] for e in ENGS}
        for ins in self.pending:
            e = ins.eng
            K = self.know[e]
            need = {}
            for d in ins.deps:
                key = self._semkey(d)
                assert d.val > 0, "dep not yet numbered"
                if K.get(key, 0) >= d.val:
                    continue
                if key not in need or need[key][1] < d.val:
                    need[key] = (d.sem, d.val)
                for k2, v2 in d.clock.items():
                    if K.get(k2, 0) < v2:
                        K[k2] = v2
            for key, (sem_, val_) in need.items():
                ins.waits.append((sem_, val_))
            if ins.flag:
                if ins.dma:
                    if ins.stream not in self.ssem:
                        self.ssem[ins.stream] = self.stack.enter_context(
                            nc.semaphore("sd_%d" % len(self.ssem)))
                        self.scount[ins.stream] = 0
                    self.scount[ins.stream] += 16
                    ins.sem = self.ssem[ins.stream]
                    ins.val = self.scount[ins.stream]
                else:
                    self.ecount[e] += 1
                    ins.sem = self.esem[e]
                    ins.val = self.ecount[e]
                ck = dict(K)
                ck[self._semkey(ins)] = ins.val
                ins.clock = ck
            per[e].append(ins)
            self.n_ins += 1
            self.n_wait += len(ins.waits)
        self.pending = []
        for r in self.all_res:
            r.w = {}
            r.r = []
        self.barrier()

        def replay(lst):
            def f(eng):
                for ins in lst:
                    if ins.fn is None:
                        for (s, v) in ins.waits:
                            eng.wait_ge(s, v)
                        continue
                    for (s, v) in ins.waits[:-1]:
                        eng.wait_ge(s, v)
                    r = ins.fn(eng)
                    if ins.waits:
                        s, v = ins.waits[-1]
                        r._wait_ge(s, v)
                    if ins.flag:
                        r.then_inc(ins.sem, 16 if ins.dma else 1)
            return f

        with nc.Block() as block:
            if per["sp"]:
                block.sync(replay(per["sp"]))
            if per["pool"]:
                block.gpsimd(replay(per["pool"]))
            if per["act"]:
                block.scalar(replay(per["act"]))
            if per["dve"]:
                block.vector(replay(per["dve"]))
            if per["pe"]:
                block.tensor(replay(per["pe"]))


def build(stage=99, debug=False):
    nc = bass.Bass("TRN2", target_bir_lowering=False)

    def din(name, shape, dt=F32):
        return nc.dram_tensor(name, list(shape), dt, kind="ExternalInput").ap()

    def dout(name, shape, dt=F32):
        return nc.dram_tensor(name, list(shape), dt, kind="ExternalOutput").ap()

    def dscr(name, shape, dt):
        return nc.dram_tensor(name, list(shape), dt).ap()

    xs = din("xs", [4096, 1024])
    xp = din("xp", [1024, 1024])
    cak = din("cak", [512, 128])
    cav = din("cav", [512, 128])
    cbk = din("cbk", [512, 512])
    cbv = din("cbv", [512, 512])
    cT = din("cT", [128, 16])
    w_mod = din("w_mod", [1024, 6144])
    bmodT2 = din("bmodT2", [128, 96])
    n1T2 = din("n1T2", [128, 16])
    n2T2 = din("n2T2", [128, 16])
    w_in = din("w_in", [1024, 4352])
    qg8 = din("qg8", [512])
    kg2 = din("kg2", [128])
    btab = din("btab", [128, 8 * 16 * 64])
    w_br_a = din("w_br_a", [512, 1024])
    w_br_b = din("w_br_b", [512, 1024])
    w_out = din("w_out", [1024, 1024])
    w1 = din("w1", [1024, 4096])
    w2 = din("w2", [4096, 1024])
    gf = din("gf", [1024])
    ident = din("ident", [128, 128])
    rope = din("rope", [4096, 2, 512])

    ys = dout("ys", [4096, 1024])
    yp = dout("yp", [1024, 1024])
    nak = dout("nak", [1024, 128])
    nav = dout("nav", [1024, 128])
    nbk = dout("nbk", [1024, 512])
    nbv = dout("nbv", [1024, 512])

    HT1 = dscr("HT1", [10, 128, 8, 512], BF16)
    HT2 = dscr("HT2", [10, 128, 8, 512], BF16)
    AOAd = dscr("AOAd", [10, 64, 8, 512], BF16)
    AOBd = dscr("AOBd", [10, 128, 4, 512], BF16)
    X1 = dscr("X1", [5120, 1024], F32)
    GMOD = dscr("GMOD", [4, 1024], F32)

    def xrows(tg0, n):
        if tg0 < 4096:
            return xs[tg0:tg0 + n, :]
        return xp[tg0 - 4096:tg0 - 4096 + n, :]

    def yrows(tg0, n):
        if tg0 < 4096:
            return ys[tg0:tg0 + n, :]
        return yp[tg0 - 4096:tg0 - 4096 + n, :]

    with ExitStack() as top:
        P = Prog(nc, top)

        def sbuf(st, name, shape, dt):
            return st.enter_context(nc.sbuf_tensor(name, list(shape), dt))

        def MM(out, lhsT, rhs, start, stop, reads, writes):
            return P.add("pe", lambda e: e.matmul(out, lhsT=lhsT, rhs=rhs, start=start, stop=stop,
                                                  skip_group_check=True), reads, writes)

        def TR(out, in_, idn, reads, writes):
            return P.add("pe", lambda e: e.transpose(out=out, in_=in_, identity=idn), reads, writes)

        def ACT(out, in_, func, reads, writes, scale=None, accum=None):
            kw = {}
            if scale is not None:
                kw["scale"] = scale
            if accum is not None:
                kw["accum_out"] = accum
            return P.add("act", lambda e: e.activation(out=out, in_=in_, func=func, **kw), reads, writes)

        def TS(out, in0, s1, s2, op0, op1, reads, writes, eng="dve"):
            if s2 is None:
                return P.add(eng, lambda e: e.tensor_scalar(out=out, in0=in0, scalar1=s1, scalar2=None,
                                                            op0=op0), reads, writes)
            return P.add(eng, lambda e: e.tensor_scalar(out=out, in0=in0, scalar1=s1, scalar2=s2,
                                                        op0=op0, op1=op1), reads, writes)

        def TT(out, in0, in1, op, reads, writes, eng="dve"):
            return P.add(eng, lambda e: e.tensor_tensor(out=out, in0=in0, in1=in1, op=op), reads, writes)

        def STT(out, in0, scalar, in1, op0, op1, reads, writes, eng="dve"):
            return P.add(eng, lambda e: e.scalar_tensor_tensor(out=out, in0=in0, scalar=scalar, in1=in1,
                                                               op0=op0, op1=op1), reads, writes)

        def CP(out, in_, reads, writes, eng="dve"):
            return P.add(eng, lambda e: e.tensor_copy(out=out, in_=in_), reads, writes)

        def RECIP(out, in_, reads, writes):
            return P.add("dve", lambda e: e.reciprocal(out=out, in_=in_), reads, writes)

        def RED(out, in_, reads, writes):
            return P.add("dve", lambda e: e.tensor_reduce(out=out, in_=in_, axis=AX.X, op=ALU.add), reads, writes)

        def MEMSET(ap, val, writes, eng="dve"):
            return P.add(eng, lambda e: e.memset(ap, val), [], writes)

        def DMA(q, out, in_, reads, writes, stream, slow=False):
            if slow:
                return P.add(q, lambda e: e.dma_start(out=out, in_=in_, allow_slow_non_contiguous=True),
                             reads, writes, dma=True, stream=stream)
            return P.add(q, lambda e: e.dma_start(out=out, in_=in_), reads, writes, dma=True, stream=stream)

        ps = [top.enter_context(nc.psum_tensor("ps%d" % i, [128, 512], F32)) for i in range(8)]
        r_ps = [P.res("ps%d" % i, excl=True) for i in range(8)]

        class Rot:
            def __init__(self, idxs):
                self.idxs = idxs
                self.i = 0

            def __call__(self):
                k = self.idxs[self.i % len(self.idxs)]
                self.i += 1
                return k

        idf = sbuf(top, "idf", [128, 128], F32)
        idb = sbuf(top, "idb", [128, 128], BF16)
        onesf = sbuf(top, "onesf", [128, 128], F32)
        epst = sbuf(top, "epst", [128, 1], F32)
        MODS = sbuf(top, "MODS", [128, 4, 16], F32)
        r_c = P.res("consts")

        def tile_info(i):
            return (i * 512, 0 if i < 8 else 1)

        with ExitStack() as st:
            cTt = sbuf(st, "cTt", [128, 16], F32)
            sT = sbuf(st, "sT", [128, 16], BF16)
            wm = [sbuf(st, "wm%d" % k, [128, 8, 512], BF16) for k in range(2)]
            r_wm = [P.res("wm%d" % k) for k in range(2)]
            modT = sbuf(st, "modT", [128, 96], F32)
            bmt = sbuf(st, "bmt", [128, 96], F32)
            n1t = sbuf(st, "n1t", [128, 16], F32)
            n2t = sbuf(st, "n2t", [128, 16], F32)
            r_l = P.res("a0loads")
            r_sT = P.res("sT")
            r_mod = P.res("modT")
            DMA("sp", idf[:, :], ident[:, :], [], [r_c], "c0")
            DMA("sp", cTt[:, :], cT[:, :], [], [r_l], "c1")
            DMA("sp", bmt[:, :], bmodT2[:, :], [], [r_l], "c2")
            DMA("sp", n1t[:, :], n1T2[:, :], [], [r_l], "c3")
            DMA("sp", n2t[:, :], n2T2[:, :], [], [r_l], "c4")
            MEMSET(onesf[:, :], 1.0, [r_c])
            MEMSET(epst[:, :], EPS, [r_c])
            CP(idb[:, :], idf[:, :], [r_c], [r_c])
            ACT(sT[:, :], cTt[:, :], AF.Silu, [r_l], [r_sT])
            wmv = w_mod.rearrange("(c p) n -> p c n", p=128)
            for k in range(12):
                DMA("pool", wm[k % 2][:, :, :], wmv[:, :, k * 512:(k + 1) * 512], [], [r_wm[k % 2]], "wm%d" % (k % 2))
                for j in range(4):
                    fc = 4 * k + j
                    for c in range(8):
                        MM(ps[0][:, fc * 2:fc * 2 + 2], wm[k % 2][:, c, j * 128:(j + 1) * 128],
                           sT[:, c * 2:c * 2 + 2], c == 0, c == 7, [r_wm[k % 2], r_sT], [r_ps[0]])
            TT(modT[:, :], ps[0][:, 0:96], bmt[:, :], ALU.add, [r_ps[0], r_l], [r_mod])
            STT(MODS[:, 0, :], modT[:, 16:32], 1.0, n1t[:, :], ALU.add, ALU.mult, [r_mod, r_l], [r_c])
            CP(MODS[:, 1, :], modT[:, 0:16], [r_mod], [r_c])
            STT(MODS[:, 2, :], modT[:, 64:80], 1.0, n2t[:, :], ALU.add, ALU.mult, [r_mod, r_l], [r_c])
            CP(MODS[:, 3, :], modT[:, 48:64], [r_mod], [r_c])
            for which, base in ((0, 32), (1, 80)):
                for v in range(2):
                    row = which * 2 + v
                    dst = bass.AP(GMOD.tensor, row * 1024, [[1, 128], [128, 8]])
                    s0 = modT[:, base + v:base + v + 1]
                    src = bass.AP(s0.tensor, s0.offset, [[s0.ap[0][0], 128], [2, 8]])
                    DMA("sp", dst, src, [r_mod], [], "gm%d" % row, slow=True)
            P.flush()

        A1 = lambda c, v: MODS[:, 0, c * 2 + v:c * 2 + v + 1]
        SH1 = lambda c, v: MODS[:, 1, c * 2 + v:c * 2 + v + 1]
        A2 = lambda c, v: MODS[:, 2, c * 2 + v:c * 2 + v + 1]
        SH2 = lambda c, v: MODS[:, 3, c * 2 + v:c * 2 + v + 1]

        def hT_chain(src_ap, r_src, junk, r_junk, stat, r_stat, xn, r_xn):
            ACT(junk[:, :], src_ap, AF.Square, [r_src], [r_junk, r_stat], scale=1.0 / 32.0, accum=stat[:, 0:1])
            TS(stat[:, 1:2], stat[:, 0:1], EPS, None, ALU.add, None, [r_stat], [r_stat])
            ACT(stat[:, 2:3], stat[:, 1:2], AF.Sqrt, [r_stat], [r_stat])
            RECIP(stat[:, 3:4], stat[:, 2:3], [r_stat], [r_stat])
            ACT(xn[:, :], src_ap, AF.Copy, [r_src, r_stat], [r_xn], scale=stat[:, 3:4])

        def hT_tr(xn, r_xn, dst_fn, r_dst, Afn, Sfn, v, rot):
            for half in range(2):
                b = rot()
                for cc in range(4):
                    c = half * 4 + cc
                    TR(ps[b][:, cc * 128:(cc + 1) * 128], xn[:, c * 128:(c + 1) * 128], idf[:, :],
                       [r_xn, r_c], [r_ps[b]])
                for cc in range(4):
                    c = half * 4 + cc
                    TS(dst_fn(c), ps[b][:, cc * 128:(cc + 1) * 128], Afn(c, v), Sfn(c, v), ALU.mult, ALU.add,
                       [r_ps[b], r_c], [r_dst])

        if stage < 1:
            P.flush(final=True)
            return nc

        with ExitStack() as kv:
            KTA = sbuf(kv, "KTA", [128, 2, 4608], BF16)
            VA = sbuf(kv, "VA", [128, 37, 2, 65], BF16)
            KTB = sbuf(kv, "KTB", [128, 4, 4608], BF16)
            LB = sbuf(kv, "LB", [128, 36, 4, 160], BF16)
            r_kt = [P.res("kt%d" % t) for t in range(36)]

            def phaseA(tag, tilesA, do_ctx):
              with ExitStack() as st0:
                _sb = sbuf
                def sbuf_(st_, name, shape, dt):
                    return _sb(st_, name + tag, shape, dt)
                st = st0
                wA = sbuf_(st, "wA", [128, 8, 256], BF16)
                wBK = sbuf_(st, "wBK", [128, 8, 512], BF16)
                wBV = sbuf_(st, "wBV", [128, 8, 512], BF16)
                r_w = P.res("wA")
                wv = w_in.rearrange("(c p) n -> p c n", p=128)
                DMA("pool", wA[:, :, :], wv[:, :, 512:768], [], [r_w], "w0")
                DMA("pool", wBK[:, :, :], wv[:, :, 1280:1792], [], [r_w], "w1")
                DMA("pool", wBV[:, :, :], wv[:, :, 1792:2304], [], [r_w], "w2")
                kgt = sbuf_(st, "kgt", [128, 128], F32)
                DMA("sp", kgt[:, :], bass.AP(kg2.tensor, 0, [[0, 128], [1, 128]]), [], [r_w], "c5")
                if do_ctx:
                    MEMSET(VA[:, :, :, :], 1.0, r_kt)
                    MEMSET(KTA[64:128, :, :], 0.0, r_kt)
                    MEMSET(LB[:, :, :, 64:96], 0.0, r_kt)
                    MEMSET(LB[:, :, :, 64:65], 1.0, r_kt)

                xt = [sbuf_(st, "xt%d" % k, [128, 4, 1024], F32) for k in range(2)]
                r_xt = [P.res("xt%d" % k) for k in range(2)]
                junk = sbuf_(st, "junk", [128, 1024], BF16)
                r_junk = P.res("junk")
                stat = [sbuf_(st, "stat%d" % k, [128, 4], F32) for k in range(2)]
                r_stat = [P.res("stat%d" % k) for k in range(2)]
                xn = [sbuf_(st, "xn%d" % k, [128, 1024], F32) for k in range(2)]
                r_xn = [P.res("xn%d" % k) for k in range(2)]
                hT = [sbuf_(st, "hT%d" % k, [128, 8, 512], BF16) for k in range(2)]
                r_hT = [[P.res("hT%d_%d" % (k, s)) for s in range(4)] for k in range(2)]
                ropeT = [sbuf_(st, "ropeT%d" % k, [128, 2, 128], F32) for k in range(2)]
                r_rope = [P.res("rope%d" % k) for k in range(2)]
                akf = [sbuf_(st, "akf%d" % k, [128, 128], F32) for k in range(2)]
                r_akf = [P.res("akf%d" % k) for k in range(2)]
                sqk = sbuf_(st, "sqk", [128, 128], F32)
                kst = [sbuf_(st, "kst%d" % k, [128, 8], F32) for k in range(2)]
                akn = [sbuf_(st, "akn%d" % k, [128, 128], F32) for k in range(2)]
                r_akn = [P.res("akn%d" % k) for k in range(2)]
                akr = [sbuf_(st, "akr%d" % k, [128, 128], F32) for k in range(2)]
                r_akr = [P.res("akr%d" % k) for k in range(2)]
                t1 = sbuf_(st, "t1", [128, 128], F32)
                t2 = sbuf_(st, "t2", [128, 128], F32)
                r_tmp = P.res("tmpA")
                r_t1 = P.res("t1A")
                r_t2 = P.res("t2A")
                r_kst = [P.res("kst%d" % k) for k in range(2)]
                stg = [sbuf_(st, "stg%d" % k, [128, 512], F32) for k in range(3)]
                r_stg = [P.res("stg%d" % k) for k in range(3)]
                stg_i = [0]
                ctx32 = [sbuf_(st, "ctx32_%d" % k, [128, 512], F32) for k in range(2)]
                r_ctx = [P.res("ctx32_%d" % k) for k in range(2)]
                rot = Rot([0, 1, 2, 3, 4, 5, 6, 7])

                def next_stg():
                    k = stg_i[0] % 3
                    stg_i[0] += 1
                    return k

                for t in (range(4) if do_ctx else []):
                    rk = [r_kt[t]]
                    a = ctx32[t % 2]
                    ra = r_ctx[t % 2]
                    DMA("sp", a[:, 0:128], cak[t * 128:(t + 1) * 128, :], [], [ra], "cx0")
                    b = rot()
                    for g in range(2):
                        TR(ps[b][0:64, g * 128:(g + 1) * 128], a[:, g * 64:(g + 1) * 64], idf[:, :], [ra, r_c], [r_ps[b]])
                    CP(KTA[0:64, :, t * 128:(t + 1) * 128], ps[b][0:64, 0:256].rearrange("p (g n) -> p g n", g=2),
                       [r_ps[b]], rk)
                    DMA("sp", a[:, 128:256], cav[t * 128:(t + 1) * 128, :], [], [ra], "cx1")
                    CP(VA[:, t, :, 0:64], a[:, 128:256].rearrange("p (g d) -> p g d", g=2), [ra], rk)
                    a2 = ctx32[(t + 1) % 2]
                    ra2 = r_ctx[(t + 1) % 2]
                    DMA("sp", a2[:, :], cbk[t * 128:(t + 1) * 128, :], [], [ra2], "cx2")
                    b = rot()
                    for j in range(4):
                        TR(ps[b][:, j * 128:(j + 1) * 128], a2[:, j * 128:(j + 1) * 128], idf[:, :], [ra2, r_c], [r_ps[b]])
                    CP(KTB[:, :, t * 128:(t + 1) * 128], ps[b][:, :].rearrange("p (j n) -> p j n", j=4), [r_ps[b]], rk)
                    DMA("sp", a[:, :], cbv[t * 128:(t + 1) * 128, :], [], [ra], "cx3")
                    av4 = a[:, :].rearrange("p (j e d) -> p j e d", j=4, e=2)
                    CP(LB[:, t, :, 0:64], av4[:, :, 0, :], [ra], rk)
                    CP(LB[:, t, :, 96:160], av4[:, :, 1, :], [ra], rk)


                def loadA(idx):
                    tg0, T, v, kb, isp, pr0 = tilesA[idx]
                    k = idx % 2
                    ns = T // 128
                    DMA("sp", xt[k][:, 0:ns, :], xrows(tg0, T).rearrange("(s p) d -> p s d", p=128), [], [r_xt[k]],
                        "xt%d" % k)

                pend_tr = [None]
                nT = len(tilesA)

                def chainA(idx, s_):
                    k_ = idx % 2
                    q_ = s_ % 2
                    hT_chain(xt[k_][:, s_, :], r_xt[k_], junk, r_junk, stat[q_], r_stat[q_], xn[q_], r_xn[q_])

                def trA(idx, s_):
                    k_ = idx % 2
                    q_ = s_ % 2
                    v_ = tilesA[idx][2]
                    hT_tr(xn[q_], r_xn[q_], lambda c, k_=k_, s_=s_: hT[k_][:, c, s_ * 128:(s_ + 1) * 128],
                          r_hT[k_][s_], A1, SH1, v_, rot)

                def bounceA(idx):
                    tg0, T, v, kb, isp, pr0 = tilesA[idx]
                    k_ = idx % 2
                    ns_ = T // 128
                    ti = tg0 // 512
                    co = tg0 % 512
                    DMA("sp", HT1[ti, :, :, co:co + T], hT[k_][:, :, 0:T], r_hT[k_][0:ns_], [], "ht%d" % k_)

                def stage2_sub(idx, s):
                    tg0, T, v, kb, isp, pr0 = tilesA[idx]
                    k = idx % 2
                    q = s % 2
                    kt = (kb + s * 128) // 128
                    rk = [r_kt[kt]]
                    koff = kb + s * 128
                    rh = [r_hT[k][s], r_w]
                    b = rot()
                    for c in range(8):
                        MM(ps[b][:, 0:256], hT[k][:, c, s * 128:(s + 1) * 128], wA[:, c, :], c == 0, c == 7,
                           rh, [r_ps[b]])
                    ACT(akf[q][:, :], ps[b][:, 0:128], AF.Copy, [r_ps[b]], [r_akf[q]])
                    CP(VA[:, kt, :, 0:64], ps[b][:, 128:256].rearrange("p (g d) -> p g d", g=2), [r_ps[b]], rk)
                    if isp:
                        sk = next_stg()
                        ACT(stg[sk][:, 0:128], ps[b][:, 128:256], AF.Copy, [r_ps[b]], [r_stg[sk]])
                        DMA("sp", nav[pr0 + s * 128:pr0 + (s + 1) * 128, :], stg[sk][:, 0:128], [r_stg[sk]], [],
                            "stg%d" % sk)
                    TT(sqk[:, :], akf[q][:, :], akf[q][:, :], ALU.mult, [r_akf[q]], [r_tmp], eng="pool")
                    RED(kst[q][:, 0:2], sqk[:, :].rearrange("p (g d) -> p g d", g=2), [r_tmp], [r_kst[q]])
                    TS(kst[q][:, 2:4], kst[q][:, 0:2], 1.0 / 64.0, EPS, ALU.mult, ALU.add, [r_kst[q]], [r_kst[q]])
                    ACT(kst[q][:, 4:6], kst[q][:, 2:4], AF.Sqrt, [r_kst[q]], [r_kst[q]])
                    RECIP(kst[q][:, 6:8], kst[q][:, 4:6], [r_kst[q]], [r_kst[q]])
                    for g in range(2):
                        TS(akn[q][:, g * 64:(g + 1) * 64], akf[q][:, g * 64:(g + 1) * 64], kst[q][:, 6 + g:7 + g],
                           None, ALU.mult, None, [r_akf[q], r_kst[q]], [r_akn[q]])
                    TT(akn[q][:, :], akn[q][:, :], kgt[:, :], ALU.mult, [r_akn[q], r_w], [r_akn[q]], eng="pool")
                    if isp:
                        DMA("sp", nak[pr0 + s * 128:pr0 + (s + 1) * 128, :], akn[q][:, :], [r_akn[q]], [],
                            "akn%d" % q)
                        ksrc, rks = akn[q], r_akn[q]
                    else:
                        DMA("sp", ropeT[q][:, :, :], rope[tg0 + s * 128:tg0 + (s + 1) * 128, :, 0:128], [],
                            [r_rope[q]], "rope%d" % q)
                        xv_ = akn[q][:, :].rearrange("p (a h d) -> p a h d", a=4, h=2)
                        sv_ = ropeT[q][:, 1, :].rearrange("p (a h d) -> p a h d", a=4, h=2)
                        t2v = t2[:, :].rearrange("p (a h d) -> p a h d", a=4, h=2)
                        TT(t1[:, :], akn[q][:, :], ropeT[q][:, 0, :], ALU.mult, [r_akn[q], r_rope[q]], [r_t1], eng="pool")
                        TT(t2v[:, :, 0, :], xv_[:, :, 1, :], sv_[:, :, 0, :], ALU.mult, [r_akn[q], r_rope[q]], [r_t2], eng="pool")
                        TT(t2v[:, :, 1, :], xv_[:, :, 0, :], sv_[:, :, 1, :], ALU.mult, [r_akn[q], r_rope[q]], [r_t2], eng="pool")
                        TT(akr[q][:, :], t1[:, :], t2[:, :], ALU.add, [r_t1, r_t2], [r_akr[q]], eng="pool")
                        ksrc, rks = akr[q], r_akr[q]

                    def k_tr(ksrc=ksrc, rks=rks, koff=koff, rk=rk):
                        b2 = rot()
                        for g in range(2):
                            TR(ps[b2][0:64, g * 128:(g + 1) * 128], ksrc[:, g * 64:(g + 1) * 64], idf[:, :],
                               [rks, r_c], [r_ps[b2]])
                        ACT(KTA[0:64, :, koff:koff + 128],
                            ps[b2][0:64, 0:256].rearrange("p (g n) -> p g n", g=2), AF.Copy, [r_ps[b2]], rk)
                    b = rot()
                    for c in range(8):
                        MM(ps[b][:, :], hT[k][:, c, s * 128:(s + 1) * 128], wBV[:, c, :], c == 0, c == 7, rh, [r_ps[b]])
                    pv4 = ps[b][:, :].rearrange("p (j e d) -> p j e d", j=4, e=2)
                    ACT(LB[:, kt, :, 0:64], pv4[:, :, 0, :], AF.Copy, [r_ps[b]], rk)
                    CP(LB[:, kt, :, 96:160], pv4[:, :, 1, :], [r_ps[b]], rk)
                    if isp:
                        sk = next_stg()
                        ACT(stg[sk][:, :], ps[b][:, :], AF.Copy, [r_ps[b]], [r_stg[sk]])
                        DMA("sp", nbv[pr0 + s * 128:pr0 + (s + 1) * 128, :], stg[sk][:, :], [r_stg[sk]], [],
                            "stg%d" % sk)
                        b = rot()
                        for c in range(8):
                            MM(ps[b][:, :], hT[k][:, c, s * 128:(s + 1) * 128], wBK[:, c, :], c == 0, c == 7, rh,
                               [r_ps[b]])
                        sk = next_stg()
                        CP(stg[sk][:, :], ps[b][:, :], [r_ps[b]], [r_stg[sk]])
                        DMA("sp", nbk[pr0 + s * 128:pr0 + (s + 1) * 128, :], stg[sk][:, :], [r_stg[sk]], [],
                            "stg%d" % sk)
                    if pend_tr[0] is not None:
                        pend_tr[0]()
                    pend_tr[0] = k_tr

                def stage2_tail(idx):
                    tg0, T, v, kb, isp, pr0 = tilesA[idx]
                    k = idx % 2
                    ns = T // 128
                    if pend_tr[0] is not None:
                        pend_tr[0]()
                        pend_tr[0] = None
                    kts = [r_kt[(kb + s * 128) // 128] for s in range(ns)]
                    for j in range(4):
                        b = rot()
                        for c in range(8):
                            MM(ps[b][:, 0:T], wBK[:, c, j * 128:(j + 1) * 128], hT[k][:, c, 0:T], c == 0, c == 7,
                               r_hT[k][0:ns] + [r_w], [r_ps[b]])
                        if j % 2 == 0:
                            ACT(KTB[:, j, kb:kb + T], ps[b][:, 0:T], AF.Copy, [r_ps[b]], kts)
                        else:
                            CP(KTB[:, j, kb:kb + T], ps[b][:, 0:T], [r_ps[b]], kts)

                nsA = tilesA[0][1] // 128
                loadA(0)
                if nT > 1:
                    loadA(1)
                chainA(0, 0)
                for s in range(nsA):
                    if s + 1 < nsA:
                        chainA(0, s + 1)
                    trA(0, s)
                bounceA(0)
                for idx in range(nT):
                    nxt = idx + 1 < nT
                    if idx + 2 < nT:
                        loadA(idx + 2)
                    if nxt:
                        chainA(idx + 1, 0)
                    for s in range(nsA):
                        stage2_sub(idx, s)
                        if nxt:
                            if s + 1 < nsA:
                                chainA(idx + 1, s + 1)
                            trA(idx + 1, s)
                    stage2_tail(idx)
                    if nxt:
                        bounceA(idx + 1)
                P.flush()

            tilesS = [(i * 512, 512, 0, 512 + i * 512, False, 0) for i in range(8)]
            tilesP = [(4096 + p * 256, 256, 1, p * 256, True, p * 256) for p in range(4)]
            qtS = [(i, 0, 512, False, i) for i in range(8)]
            qtP = [(8 + p // 2, (p % 2) * 256, 256, True, p) for p in range(4)]
            phaseA("s", tilesS, True)
            if stage < 2:
                if debug:
                    dbg = dout("dbg_kta", [128, 2, 4608], BF16)
                    dbg2 = dout("dbg_ktb", [128, 4, 4608], BF16)
                    dbg3 = dout("dbg_va", [128, 37 * 2 * 65], BF16)
                    dbg4 = dout("dbg_lb", [128, 36 * 4 * 160], BF16)
                    DMA("sp", dbg[:, :, :], KTA[:, :, :], [], [], "dbg0")
                    DMA("sp", dbg2[:, :, :], KTB[:, :, :], [], [], "dbg1")
                    DMA("sp", dbg3[:, :], VA[:, :, :, :].rearrange("p a b c -> p (a b c)"), [], [], "dbg2")
                    DMA("sp", dbg4[:, :], LB[:, :, :, :].rearrange("p a b c -> p (a b c)"), [], [], "dbg3")
                P.flush(final=True)
                return nc

            def phaseB(tag, qtiles):
              with ExitStack() as st0:
                _sb = sbuf
                def sbuf_(st_, name, shape, dt):
                    return _sb(st_, name + tag, shape, dt)
                st = st0
                wAQ = sbuf_(st, "wAQ", [128, 8, 512], BF16)
                wBQ = sbuf_(st, "wBQ", [128, 8, 512], BF16)
                r_w = P.res("wB")
                wv = w_in.rearrange("(c p) n -> p c n", p=128)
                DMA("pool", wAQ[:, :, :], wv[:, :, 0:512], [], [r_w], "w0")
                DMA("pool", wBQ[:, :, :], wv[:, :, 768:1280], [], [r_w], "w1")
                BT = sbuf_(st, "BT", [128, 8, 16, 64], BF16)
                DMA("pool", BT[:, :, :, :], btab.rearrange("p (h e q) -> p h e q", h=8, e=16), [], [r_w], "w2")
                qgt = sbuf_(st, "qgt", [128, 512], F32)
                DMA("sp", qgt[:, :], bass.AP(qg8.tensor, 0, [[0, 128], [1, 512]]), [], [r_w], "c5")

                hT = [sbuf_(st, "hTb%d" % k, [128, 8, 512], BF16) for k in range(1)] * 2
                r_hT = [P.res("hTb%d" % k) for k in range(1)] * 2
                QTA = sbuf_(st, "QTA", [128, 8, 512], BF16)
                r_qta = [P.res("qta%d" % s) for s in range(4)]
                QTBe = sbuf_(st, "QTBe", [128, 4, 512], BF16)
                QTBo = sbuf_(st, "QTBo", [128, 4, 512], BF16)
                r_qtb = [P.res("qtb%d" % j) for j in range(4)]
                MEMSET(QTA[64:128, :, :], 0.0, r_qta)
                MEMSET(QTBe[64:128, :, :], 0.0, r_qtb)
                MEMSET(QTBo[0:64, :, :], 0.0, r_qtb)
                PT = [sbuf_(st, "PT%d" % k, [128, 512], BF16) for k in range(4)]
                r_pt = [P.res("PT%d" % k) for k in range(4)]
                pt_i = [0]
                AOA = [sbuf_(st, "AOA%d" % k, [128, 8, 512], BF16) for k in range(1)] * 2
                r_aoa = [P.res("AOA%d" % k) for k in range(1)] * 2
                AOB = [sbuf_(st, "AOB%d" % k, [128, 4, 512], BF16) for k in range(1)] * 2
                r_aob = [P.res("AOB%d" % k) for k in range(1)] * 2
                ropeQ = [sbuf_(st, "ropeQ%d" % k, [128, 2, 512], F32) for k in range(1)] * 2
                r_rope = [P.res("ropeQ%d" % k) for k in range(1)] * 2
                aqf = [sbuf_(st, "aqf%d" % k, [128, 512], F32) for k in range(1)] * 2
                r_aqf = [P.res("aqf%d" % k) for k in range(1)] * 2
                aqn = [sbuf_(st, "aqn%d" % k, [128, 512], F32) for k in range(2)]
                r_aqn = [P.res("aqn%d" % k) for k in range(2)]
                tq1 = sbuf_(st, "tq1", [128, 512], F32)
                tq2 = sbuf_(st, "tq2", [128, 512], F32)
                qst = [sbuf_(st, "qst%d" % k, [128, 32], F32) for k in range(2)]
                r_tmp = P.res("tmpB")
                r_tq1 = P.res("tq1B")
                r_tq2 = P.res("tq2B")
                r_qst = [P.res("qstB%d" % k) for k in range(2)]
                oT = [sbuf_(st, "oT%d" % k, [128, 512], F32) for k in range(2)]
                r_oT = [P.res("oT%d" % k) for k in range(2)]
                rrow = [sbuf_(st, "rrow%d" % k, [128, 512], F32) for k in range(2)]
                r_rrow = [P.res("rrow%d" % k) for k in range(2)]
                fin_i = [0]
                deferred = []
                defer_n = [2]
                rotS = Rot([0, 1, 2, 3])
                rotO = Rot([4, 5])
                rotX = Rot([6, 7])

                def next_pt():
                    k = pt_i[0] % 4
                    pt_i[0] += 1
                    return k

                def finalize(bo, T, rows, dp, dst_ap, r_dst):
                    f = fin_i[0] % 2
                    fin_i[0] += 1
                    r0, r1 = rows
                    ACT(rrow[f][dp:dp + 1, 0:T], ps[bo][dp:dp + 1, 0:T], AF.Ln, [r_ps[bo]], [r_rrow[f]])
                    ACT(rrow[f][dp:dp + 1, 0:T], rrow[f][dp:dp + 1, 0:T], AF.Exp, [r_rrow[f]], [r_rrow[f]], scale=-1.0)
                    CP(oT[f][r0:r1, 0:T], ps[bo][r0:r1, 0:T], [r_ps[bo]], [r_oT[f]])

                    def part_b():
                        bx = rotX()
                        MM(ps[bx][:, 0:T], onesf[dp:dp + 1, :], rrow[f][dp:dp + 1, 0:T], True, True,
                           [r_rrow[f], r_c], [r_ps[bx]])
                        TT(dst_ap, oT[f][r0:r1, 0:T], ps[bx][r0:r1, 0:T], ALU.mult, [r_oT[f], r_ps[bx]], [r_dst])
                    deferred.append([defer_n[0], part_b])


                def loadB(qi):
                    ti, co, T, isp, sp_ = qtiles[qi]
                    k = qi % 2
                    DMA("sp", hT[k][:, :, 0:T], HT1[ti, :, :, co:co + T], [], [r_hT[k]], "hb0")

                loadB(0)
                for qi in range(len(qtiles)):
                    ti, co, T, isp, sp_ = qtiles[qi]
                    k = qi % 2
                    ns = T // 128
                    defer_n[0] = 2 if isp else 6
                    rh = [r_hT[k], r_w]
                    def aq_chain(s):
                        q = s % 2
                        b = rotS()
                        for c in range(8):
                            MM(ps[b][:, :], hT[k][:, c, s * 128:(s + 1) * 128], wAQ[:, c, :], c == 0, c == 7, rh, [r_ps[b]])
                        CP(aqf[q][:, :], ps[b][:, :], [r_ps[b]], [r_aqf[q]])
                        TT(tq1[:, :], aqf[q][:, :], aqf[q][:, :], ALU.mult, [r_aqf[q]], [r_tq1], eng="pool")
                        RED(qst[q][:, 0:8], tq1[:, :].rearrange("p (g d) -> p g d", g=8), [r_tq1], [r_qst[q]])
                        TS(qst[q][:, 8:16], qst[q][:, 0:8], 1.0 / 64.0, EPS, ALU.mult, ALU.add, [r_qst[q]], [r_qst[q]])
                        ACT(qst[q][:, 16:24], qst[q][:, 8:16], AF.Ln, [r_qst[q]], [r_qst[q]])
                        ACT(qst[q][:, 24:32], qst[q][:, 16:24], AF.Exp, [r_qst[q]], [r_qst[q]], scale=-0.5)
                        for h in range(8):
                            TS(aqn[q][:, h * 64:(h + 1) * 64], aqf[q][:, h * 64:(h + 1) * 64], qst[q][:, 24 + h:25 + h],
                               0.125, ALU.mult, ALU.mult, [r_aqf[q], r_qst[q]], [r_aqn[q]])
                        TT(aqn[q][:, :], aqn[q][:, :], qgt[:, :], ALU.mult, [r_aqn[q], r_w], [r_aqn[q]], eng="pool")
                        if not isp:
                            t0 = sp_ * 512 + s * 128
                            DMA("sp", ropeQ[q][:, :, :], rope[t0:t0 + 128, :, :], [], [r_rope[q]], "ropeq0")
                            xv_ = aqn[q][:, :].rearrange("p (a h d) -> p a h d", a=16, h=2)
                            sv_ = ropeQ[q][:, 1, :].rearrange("p (a h d) -> p a h d", a=16, h=2)
                            t2v = tq2[:, :].rearrange("p (a h d) -> p a h d", a=16, h=2)
                            TT(tq1[:, :], aqn[q][:, :], ropeQ[q][:, 0, :], ALU.mult, [r_aqn[q], r_rope[q]], [r_tq1], eng="pool")
                            TT(t2v[:, :, 0, :], xv_[:, :, 1, :], sv_[:, :, 0, :], ALU.mult, [r_aqn[q], r_rope[q]], [r_tq2], eng="pool")
                            TT(t2v[:, :, 1, :], xv_[:, :, 0, :], sv_[:, :, 1, :], ALU.mult, [r_aqn[q], r_rope[q]], [r_tq2], eng="pool")
                            TT(aqn[q][:, :], tq1[:, :], tq2[:, :], ALU.add, [r_tq1, r_tq2], [r_aqn[q]], eng="pool")

                    def aq_tr(s):
                        q = s % 2
                        for hb in range(2):
                            b2 = rotS()
                            for hh in range(4):
                                h = hb * 4 + hh
                                TR(ps[b2][0:64, hh * 128:(hh + 1) * 128], aqn[q][:, h * 64:(h + 1) * 64], idf[:, :],
                                   [r_aqn[q], r_c], [r_ps[b2]])
                            CP(QTA[0:64, hb * 4:hb * 4 + 4, s * 128:(s + 1) * 128],
                               ps[b2][0:64, :].rearrange("p (g n) -> p g n", g=4), [r_ps[b2]], [r_qta[s]])
                    for j in range(4):
                        b = rotS()
                        for c in range(8):
                            MM(ps[b][:, 0:T], wBQ[:, c, j * 128:(j + 1) * 128], hT[k][:, c, 0:T], c == 0, c == 7, rh,
                               [r_ps[b]])
                        TS(QTBe[0:64, j, 0:T], ps[b][0:64, 0:T], 0.125, None, ALU.mult, None, [r_ps[b]], [r_qtb[j]])
                        TS(QTBo[64:128, j, 0:T], ps[b][64:128, 0:T], 0.125, None, ALU.mult, None, [r_ps[b]], [r_qtb[j]])
                    steps = []

                    def add_dense_step(KT_ap, Q_ap, rd, V_ap, vr, bo, M, first, last, fin):
                        cell = {}

                        def front():
                            b_ = rotS()
                            MM(ps[b_][:, 0:T], KT_ap, Q_ap, True, True, rd, [r_ps[b_]])
                            pk = next_pt()
                            cell["pk"] = pk
                            ACT(PT[pk][:, 0:T], ps[b_][:, 0:T], AF.Exp, [r_ps[b_]], [r_pt[pk]])

                        def back():
                            pk = cell["pk"]
                            MM(ps[bo][0:M, 0:T], V_ap, PT[pk][:, 0:T], first, last, vr + [r_pt[pk]], [r_ps[bo]])
                            if fin is not None:
                                fin()
                        steps.append((front, back))

                    def add_local_step(blocks, Qcols, rq_, h_, bo, M, fin):
                        cell = {}
                        cnt = len(blocks)

                        def front():
                            b_ = rotS()
                            for jj, (KT_ap, rk_, e_, V_ap) in enumerate(blocks):
                                MM(ps[b_][:, jj * 64:(jj + 1) * 64], KT_ap, Qcols, True, False, [rk_] + rq_, [r_ps[b_]])
                                MM(ps[b_][:, jj * 64:(jj + 1) * 64], idb[:, :], BT[:, h_, e_, :], False, True,
                                   [r_c, r_w], [r_ps[b_]])
                            pk = next_pt()
                            cell["pk"] = pk
                            ACT(PT[pk][:, 0:cnt * 64], ps[b_][:, 0:cnt * 64], AF.Exp, [r_ps[b_]], [r_pt[pk]])

                        def back():
                            pk = cell["pk"]
                            for jj, (KT_ap, rk_, e_, V_ap) in enumerate(blocks):
                                MM(V_ap[0], V_ap[1], PT[pk][:, jj * 64:(jj + 1) * 64], False, jj == cnt - 1,
                                   [rk_, r_pt[pk]], [r_ps[bo]])
                            if fin is not None:
                                fin()
                        steps.append((front, back))

                    def mkfin(bo, rows, dp, dst, r_dst):
                        return lambda: finalize(bo, T, rows, dp, dst, r_dst)

                    steps = []
                    if isp:
                        ktl = [2 * sp_, 2 * sp_ + 1]
                    else:
                        ktl = list(range(36))
                    for h in range(8):
                        g = h // 4
                        bo = rotO()
                        for n_, kt in enumerate(ktl):
                            last = n_ == len(ktl) - 1
                            v0 = VA[:, kt, g, 0:1]
                            vfull = bass.AP(v0.tensor, v0.offset, [[v0.ap[0][0], 128], [1, 128]])
                            add_dense_step(KTA[:, g, kt * 128:(kt + 1) * 128], QTA[:, h, 0:T],
                                           [r_kt[kt]] + r_qta[0:ns], vfull, [r_kt[kt]], bo, 128,
                                           n_ == 0, last,
                                           mkfin(bo, (0, 64), 64, AOA[k][0:64, h, 0:T], r_aoa[k]) if last else None)
                    stepsA = steps
                    steps = []
                    head_end = []
                    for h in range(8):
                        j = h // 2
                        half = h % 2
                        P0 = 64 * half
                        if half == 0:
                            l0, l1, dp, M = 0, 128, 64, 128
                            QTB = QTBe
                        else:
                            l0, l1, dp, M = 32, 160, 32, 128
                            QTB = QTBo
                        bo = rotO()
                        rq = [r_qtb[j]]
                        fin = mkfin(bo, (P0, P0 + 64), dp, AOB[k][P0:P0 + 64, j, 0:T], r_aob[k])
                        if isp:
                            ktl = [2 * sp_, 2 * sp_ + 1]
                        else:
                            ktl = [0, 1, 2, 3]
                        for n_, kt in enumerate(ktl):
                            last = isp and n_ == len(ktl) - 1
                            add_dense_step(KTB[:, j, kt * 128:(kt + 1) * 128], QTB[:, j, 0:T],
                                           [r_kt[kt]] + rq, LB[:, kt, j, l0:l1], [r_kt[kt]], bo, M, n_ == 0, last,
                                           fin if last else None)
                        if not isp:
                            for rr in range(8):
                                r = sp_ * 8 + rr
                                rs = min(max(r - 4, 0), 56)
                                if rs % 2 == 1:
                                    n0, cnt = rs - 1, 5
                                else:
                                    n0, cnt = rs, 4
                                blocks = []
                                for jj in range(cnt):
                                    n = n0 + 2 * jj
                                    kt = 4 + n // 2
                                    off = 512 + n * 64
                                    e_ = n - r + 7
                                    if cnt == 5 and jj == 0:
                                        e_ = 14
                                    elif cnt == 5 and jj == 4:
                                        e_ = 15
                                    blocks.append((KTB[:, j, off:off + 128], r_kt[kt], e_,
                                                   (ps[bo][0:M, rr * 64:(rr + 1) * 64], LB[:, kt, j, l0:l1])))
                                add_local_step(blocks, QTB[:, j, rr * 64:(rr + 1) * 64], rq, h, bo, M,
                                               fin if rr == 7 else None)
                    stepsB = steps
                    nop = lambda: None
                    spb = len(stepsB) // 8
                    inj = {}
                    if qi + 1 < len(qtiles):
                        pre = [lambda qn=qi + 1: loadB(qn)]
                    else:
                        pre = []
                    if ns == 4:
                        inj = {2: [lambda: aq_tr(0), lambda: aq_chain(2)],
                               4: [lambda: aq_tr(1), lambda: aq_chain(3)] + pre,
                               6: [lambda: aq_tr(2)], 8: [lambda: aq_tr(3)]}
                    else:
                        inj = {1: pre, 4: [lambda: aq_tr(0)], 8: [lambda: aq_tr(1)]}
                    steps = [(lambda: aq_chain(0), nop), (lambda: aq_chain(1), nop)]
                    for hh_ in range(8):
                        steps += stepsB[hh_ * spb:(hh_ + 1) * spb]
                        for fn_ in inj.get(hh_ + 1, []):
                            steps.append((fn_, nop))
                    steps += stepsA
                    LA = 3
                    for i_ in range(len(steps) + LA):
                        if i_ < len(steps):
                            steps[i_][0]()
                        for d_ in deferred:
                            d_[0] -= 1
                        while deferred and deferred[0][0] <= 0:
                            deferred.pop(0)[1]()
                        if i_ >= LA:
                            steps[i_ - LA][1]()
                    while deferred:
                        deferred.pop(0)[1]()
                    DMA("sp", AOAd[ti, :, :, co:co + T], AOA[k][0:64, :, 0:T], [r_aoa[k]], [], "aoa0")
                    DMA("sp", AOBd[ti, :, :, co:co + T], AOB[k][:, :, 0:T], [r_aob[k]], [], "aob0")
                P.flush()

            phaseB("s", qtS)
            phaseA("p", tilesP, False)
            phaseB("p", qtP)

        if stage < 3:
            P.flush(final=True)
            return nc

        with ExitStack() as st:
            wGA = sbuf(st, "wGA", [128, 8, 1024], BF16)
            wGB = sbuf(st, "wGB", [128, 8, 1024], BF16)
            wBRA = sbuf(st, "wBRA", [128, 8, 1024], BF16)
            wBRB = sbuf(st, "wBRB", [128, 4, 1024], BF16)
            wOUT = sbuf(st, "wOUT", [128, 8, 1024], BF16)
            r_w = P.res("wC1")
            wv = w_in.rearrange("(c p) n -> p c n", p=128)
            r_wga, r_wgb, r_wbra, r_wbrb, r_wout = [P.res("wc1_%d" % i_) for i_ in range(5)]
            DMA("pool", wGA[:, :, :], wv[:, :, 2304:3328], [], [r_wga], "w0")
            DMA("pool", wBRA[0:64, :, :], w_br_a.rearrange("(h d) n -> d h n", d=64), [], [r_wbra], "w2")
            DMA("pool", wGB[:, :, :], wv[:, :, 3328:4352], [], [r_wgb], "w1")
            DMA("pool", wBRB[:, :, :], w_br_b.rearrange("(c p) n -> p c n", p=128), [], [r_wbrb], "w3")
            DMA("pool", wOUT[:, :, :], w_out.rearrange("(c p) n -> p c n", p=128), [], [r_wout], "w4")
            G1 = [sbuf(st, "G1_%d" % v, [128, 1024], F32) for v in range(2)]
            for v in range(2):
                DMA("sp", G1[v][:, :], bass.AP(GMOD.tensor, v * 1024, [[0, 128], [1, 1024]]), [], [r_w], "c%d" % (5 + v))
            hT = [sbuf(st, "hTc%d" % k, [128, 8, 512], BF16) for k in range(2)]
            AOA = [sbuf(st, "AOAc%d" % k, [128, 8, 512], BF16) for k in range(2)]
            AOB = [sbuf(st, "AOBc%d" % k, [128, 4, 512], BF16) for k in range(2)]
            xsb = [sbuf(st, "xsb%d" % k, [128, 1024], F32) for k in range(4)]
            r_xsb = [P.res("xsb%d" % k) for k in range(4)]
            r_in = [P.res("inC%d" % k) for k in range(2)]
            sg = [sbuf(st, "sg%d" % k, [128, 512], F32) for k in range(2)]
            r_sg = [P.res("sg%d" % k) for k in range(2)]
            m1 = [sbuf(st, "m1_%d" % k, [128, 512], F32) for k in range(2)]
            r_m1 = [P.res("m1_%d" % k) for k in range(2)]
            m2 = [sbuf(st, "m2_%d" % k, [128, 512], F32) for k in range(2)]
            r_m2 = [P.res("m2_%d" % k) for k in range(2)]
            MT = [sbuf(st, "MT%d" % k, [128, 8, 512], BF16) for k in range(2)]
            r_mt = [[P.res("MT%d_%d" % (k, f)) for f in range(8)] for k in range(2)]
            x1 = [sbuf(st, "x1_%d" % k, [128, 1024], F32) for k in range(2)]
            r_x1 = [P.res("x1_%d" % k) for k in range(2)]
            tt_ = [sbuf(st, "ttc%d" % k, [128, 512], F32) for k in range(2)]
            r_tt = [P.res("ttc%d" % k) for k in range(2)]
            junk = sbuf(st, "junkc", [128, 1024], BF16)
            r_junk = P.res("junkc")
            stat = [sbuf(st, "statc%d" % k, [128, 4], F32) for k in range(2)]
            r_stat = [P.res("statc%d" % k) for k in range(2)]
            xn = [sbuf(st, "xnc%d" % k, [128, 1024], F32) for k in range(2)]
            r_xn = [P.res("xnc%d" % k) for k in range(2)]
            h2T = [sbuf(st, "h2T%d" % k, [128, 8, 512], BF16) for k in range(2)]
            r_h2 = [[P.res("h2T%d_%d" % (k, s)) for s in range(4)] for k in range(2)]
            rot = Rot([0, 1, 2, 3, 4, 5, 6, 7])

            def loadC1(i):
                k = i % 2
                DMA("sp", hT[k][:, :, :], HT1[i, :, :, :], [], [r_in[k]], "c1h%d" % k)
                DMA("sp", AOA[k][0:64, :, :], AOAd[i, :, :, :], [], [r_in[k]], "c1a%d" % k)
                DMA("sp", AOB[k][:, :, :], AOBd[i, :, :, :], [], [r_in[k]], "c1b%d" % k)

            def loadX(i):
                tg0, v = tile_info(i)
                for s_ in range(4):
                    DMA("sp", xsb[s_][:, :], xrows(tg0 + s_ * 128, 128), [], [r_xsb[s_]], "c1x%d" % s_)

            gi_ = [0]

            def fchunk(i, f):
                k = i % 2
                fs = slice(f * 128, (f + 1) * 128)
                b = rot()
                for c in range(8):
                    MM(ps[b][:, :], wGA[:, c, fs], hT[k][:, c, :], c == 0, c == 7, [r_in[k], r_wga], [r_ps[b]])
                ga = gi_[0] % 2
                gi_[0] += 1
                ACT(sg[ga][:, :], ps[b][:, :], AF.Sigmoid, [r_ps[b]], [r_sg[ga]])
                b = rot()
                for h in range(8):
                    MM(ps[b][:, :], wBRA[0:64, h, fs], AOA[k][0:64, h, :], h == 0, h == 7, [r_in[k], r_wbra], [r_ps[b]])
                mi = f % 2
                TT(m1[mi][:, :], sg[ga][:, :], ps[b][:, :], ALU.mult, [r_sg[ga], r_ps[b]], [r_m1[mi]])
                b = rot()
                for c in range(8):
                    MM(ps[b][:, :], wGB[:, c, fs], hT[k][:, c, :], c == 0, c == 7, [r_in[k], r_wgb], [r_ps[b]])
                gb = gi_[0] % 2
                gi_[0] += 1
                ACT(sg[gb][:, :], ps[b][:, :], AF.Sigmoid, [r_ps[b]], [r_sg[gb]])
                b = rot()
                for j in range(4):
                    MM(ps[b][:, :], wBRB[:, j, fs], AOB[k][:, j, :], j == 0, j == 3, [r_in[k], r_wbrb], [r_ps[b]])
                TT(m2[mi][:, :], sg[gb][:, :], ps[b][:, :], ALU.mult, [r_sg[gb], r_ps[b]], [r_m2[mi]])
                TT(MT[k][:, f, :], m1[mi][:, :], m2[mi][:, :], ALU.add, [r_m1[mi], r_m2[mi]], [r_mt[k][f]], eng="pool")

            def outproj(i, s):
                tg0, v = tile_info(i)
                k = i % 2
                q = s % 2
                for hh in range(2):
                    b = rot()
                    for c in range(8):
                        MM(ps[b][:, :], MT[k][:, c, s * 128:(s + 1) * 128], wOUT[:, c, hh * 512:(hh + 1) * 512],
                           c == 0, c == 7, r_mt[k] + [r_wout], [r_ps[b]])
                    TT(tt_[hh][:, :], ps[b][:, :], G1[v][:, hh * 512:(hh + 1) * 512], ALU.mult, [r_ps[b], r_w],
                       [r_tt[hh]])
                    TT(x1[q][:, hh * 512:(hh + 1) * 512], tt_[hh][:, :], xsb[s][:, hh * 512:(hh + 1) * 512],
                       ALU.add, [r_tt[hh], r_xsb[s]], [r_x1[q]], eng="pool")
                DMA("sp", X1[tg0 + s * 128:tg0 + (s + 1) * 128, :], x1[q][:, :], [r_x1[q]], [], "x1s%d" % q)

            def chainC(i, s):
                q = s % 2
                hT_chain(x1[q][:, :], r_x1[q], junk, r_junk, stat[q], r_stat[q], xn[q], r_xn[q])

            def trC(i, s):
                tg0, v = tile_info(i)
                k = i % 2
                q = s % 2
                hT_tr(xn[q], r_xn[q], lambda c, k=k, s=s: h2T[k][:, c, s * 128:(s + 1) * 128], r_h2[k][s],
                      A2, SH2, v, rot)

            def storeH2(i):
                k = i % 2
                DMA("sp", HT2[i, :, :, :], h2T[k][:, :, :], r_h2[k], [], "h2s%d" % k)

            def tail_pieces(i):
                return [
                    [lambda: outproj(i, 0)],
                    [lambda: chainC(i, 0), lambda: outproj(i, 1)],
                    [lambda: trC(i, 0), lambda: chainC(i, 1)],
                    [lambda: outproj(i, 2)],
                    [lambda: trC(i, 1), lambda: chainC(i, 2)],
                    [lambda: outproj(i, 3)],
                    [lambda: trC(i, 2), lambda: chainC(i, 3)],
                    [lambda: trC(i, 3), lambda: storeH2(i)],
                ]

            loadC1(0)
            loadC1(1)
            for f in range(8):
                fchunk(0, f)
            for i in range(10):
                if i + 2 < 10:
                    loadC1(i + 2)
                loadX(i)
                pieces = tail_pieces(i)
                for f in range(8):
                    if i + 1 < 10:
                        fchunk(i + 1, f)
                    for fn_ in pieces[f]:
                        fn_()
            P.flush()

        if stage < 4:
            P.flush(final=True)
            return nc

        with ExitStack() as st:
            W1 = sbuf(st, "W1", [128, 8, 4096], BF16)
            W2 = sbuf(st, "W2", [128, 32, 1024], BF16)
            r_w = P.res("wC2")
            w1v = w1.rearrange("(c p) n -> p c n", p=128)
            w2v = w2.rearrange("(c p) n -> p c n", p=128)
            r_w1 = [P.res("w1_%d" % q4) for q4 in range(4)]
            r_w2 = [P.res("w2_%d" % q4) for q4 in range(4)]
            for q4 in range(4):
                DMA("pool", W1[:, :, q4 * 1024:(q4 + 1) * 1024], w1v[:, :, q4 * 1024:(q4 + 1) * 1024], [], [r_w1[q4]],
                    "w%d" % q4)
            for q4 in range(4):
                DMA("pool", W2[:, q4 * 8:(q4 + 1) * 8, :], w2v[:, q4 * 8:(q4 + 1) * 8, :], [], [r_w2[q4]],
                    "w%d" % (4 + q4))
            G2 = [sbuf(st, "G2_%d" % v, [128, 1024], F32) for v in range(2)]
            GF = sbuf(st, "GF", [128, 1024], F32)
            for v in range(2):
                DMA("sp", G2[v][:, :], bass.AP(GMOD.tensor, (2 + v) * 1024, [[0, 128], [1, 1024]]), [], [r_w], "c%d" % (5 + v))
            DMA("sp", GF[:, :], bass.AP(gf.tensor, 0, [[0, 128], [1, 1024]]), [], [r_w], "c7")
            h2T = [sbuf(st, "h2d%d" % k, [128, 8, 256], BF16) for k in range(2)]
            x1t = [sbuf(st, "x1d%d" % k, [128, 2, 1024], F32) for k in range(2)]
            r_in = [P.res("inD%d" % k) for k in range(2)]
            UT = sbuf(st, "UT", [128, 32, 256], BF16)
            r_ut = [P.res("UT%d" % f) for f in range(16)]
            rl = [sbuf(st, "rl%d" % k, [128, 512], F32) for k in range(2)]
            r_rl = [P.res("rl%d" % k) for k in range(2)]
            tt_ = [sbuf(st, "ttd%d" % k, [128, 512], F32) for k in range(2)]
            r_tt = [P.res("ttd%d" % k) for k in range(2)]
            x2 = [sbuf(st, "x2_%d" % k, [128, 1024], F32) for k in range(2)]
            r_x2 = [P.res("x2_%d" % k) for k in range(2)]
            yo = [sbuf(st, "yo%d" % k, [128, 1024], F32) for k in range(2)]
            r_yo = [P.res("yo%d" % k) for k in range(2)]
            junk = sbuf(st, "junkd", [128, 1024], BF16)
            r_junk = P.res("junkd")
            stat = [sbuf(st, "statd%d" % k, [128, 4], F32) for k in range(2)]
            r_stat = [P.res("statd%d" % k) for k in range(2)]
            rot = Rot([0, 1, 2, 3, 4, 5, 6, 7])

            def loadC2(i):
                k = i % 2
                tg0 = i * 256
                DMA("sp", h2T[k][:, :, :], HT2[i // 2, :, :, (i % 2) * 256:(i % 2) * 256 + 256], [], [r_in[k]],
                    "c2h%d" % k)
                DMA("sp", x1t[k][:, :, :], X1[tg0:tg0 + 256, :].rearrange("(s p) d -> p s d", p=128), [], [r_in[k]],
                    "c2x%d" % k)

            loadC2(0)
            yi = 0
            for i in range(20):
                k = i % 2
                tg0 = i * 256
                v = 0 if tg0 < 4096 else 1
                if i + 1 < 20:
                    loadC2(i + 1)
                rin = [r_in[k], r_w]
                for fp in range(16):
                    b = rot()
                    for e2 in range(2):
                        f = fp * 2 + e2
                        for c in range(8):
                            MM(ps[b][:, e2 * 256:(e2 + 1) * 256], W1[:, c, f * 128:(f + 1) * 128], h2T[k][:, c, :],
                               c == 0, c == 7, [r_in[k], r_w1[f // 8]], [r_ps[b]])
                    a = fp % 2
                    ACT(rl[a][:, :], ps[b][:, :], AF.Relu, [r_ps[b]], [r_rl[a]])
                    TT(UT[:, fp * 2:fp * 2 + 2, :], rl[a][:, :].rearrange("p (e n) -> p e n", e=2),
                       rl[a][:, :].rearrange("p (e n) -> p e n", e=2), ALU.mult, [r_rl[a]], [r_ut[fp]],
                       eng=("pool" if fp % 2 else "dve"))
                for s in range(2):
                    q = yi % 2
                    yi += 1
                    for hh in range(2):
                        b = rot()
                        for f in range(32):
                            MM(ps[b][:, :], UT[:, f, s * 128:(s + 1) * 128], W2[:, f, hh * 512:(hh + 1) * 512],
                               f == 0, f == 31, r_ut + [r_w2[f // 8]], [r_ps[b]])
                        TT(tt_[hh][:, :], ps[b][:, :], G2[v][:, hh * 512:(hh + 1) * 512], ALU.mult, [r_ps[b], r_w],
                           [r_tt[hh]])
                        TT(x2[q][:, hh * 512:(hh + 1) * 512], tt_[hh][:, :], x1t[k][:, s, hh * 512:(hh + 1) * 512],
                           ALU.add, [r_tt[hh], r_in[k]], [r_x2[q]], eng="pool")
                    ACT(junk[:, :], x2[q][:, :], AF.Square, [r_x2[q]], [r_junk, r_stat[q]], scale=1.0 / 32.0,
                        accum=stat[q][:, 0:1])
                    TS(stat[q][:, 1:2], stat[q][:, 0:1], EPS, None, ALU.add, None, [r_stat[q]], [r_stat[q]])
                    ACT(stat[q][:, 2:3], stat[q][:, 1:2], AF.Sqrt, [r_stat[q]], [r_stat[q]])
                    RECIP(stat[q][:, 3:4], stat[q][:, 2:3], [r_stat[q]], [r_stat[q]])
                    STT(yo[q][:, :], x2[q][:, :], stat[q][:, 3:4], GF[:, :], ALU.mult, ALU.mult,
                        [r_x2[q], r_stat[q], r_w], [r_yo[q]])
                    DMA("sp", yrows(tg0 + s * 128, 128), yo[q][:, :], [r_yo[q]], [], "yo%d" % q)
            P.flush(final=True)
    return nc


def _rope_tables():
    pos = np.arange(4096)
    row = (pos // 64).astype(np.float32)
    col = (pos % 64).astype(np.float32)
    inv = (np.float32(10000.0) ** (-np.arange(16, dtype=np.float32) / np.float32(16))).astype(np.float32)
    ar = row[:, None] * inv
    ac = col[:, None] * inv
    cr, sr, cc, sc = np.cos(ar), np.sin(ar), np.cos(ac), np.sin(ac)
    C = np.concatenate([cr, cr, cc, cc], axis=1).astype(np.float32)
    S = np.concatenate([-sr, sr, -sc, sc], axis=1).astype(np.float32)
    out = np.empty((4096, 2, 512), np.float32)
    out[:, 0, :] = np.tile(C, (1, 8))
    out[:, 1, :] = np.tile(S, (1, 8))
    return out


def _bias_table(nat_bias):
    nb = nat_bias[0]
    kc = np.arange(64)[:, None]
    qc = np.arange(64)[None, :]
    cstart = np.clip(qc - 8, 0, 48)
    inwin = (kc >= cstart) & (kc < cstart + 16)
    dc = np.clip(kc - qc + 15, 0, 30)
    tab = np.full((2, 64, 8, 16, 64), NEG, np.float32)
    def blk(d):
        g = nb[:, d, :][:, dc]
        return np.where(inwin[None], g, np.float32(NEG)).transpose(1, 0, 2)
    for e in range(14):
        tab[0, :, :, e, :] = blk(e)
        tab[1, :, :, e, :] = blk(e + 1)
    tab[1, :, :, 14, :] = blk(3)
    tab[0, :, :, 15, :] = blk(10)
    return np.ascontiguousarray(tab.reshape(128, 8 * 16 * 64))


_NC_CACHE = {}


def make_in_maps(x_prompt, x_sample, cache_a_k, cache_a_v, cache_b_k, cache_b_v, c, c_ctx,
                 w_mod, b_mod, norm1_g, norm2_g, w_in, q_norm_g, k_norm_g, nat_bias,
                 w_br_a, w_br_b, w_out, w_mlp_in, w_mlp_out, final_norm_g):
    f = lambda a: np.ascontiguousarray(np.asarray(a, dtype=np.float32))
    rope = _rope_tables()
    btab = _bias_table(f(nat_bias))
    bm = f(b_mod)[0].reshape(48, 128).T
    shared = {
        "w_mod": f(w_mod)[0], "bmodT2": np.ascontiguousarray(np.repeat(bm, 2, axis=1)),
        "n1T2": np.ascontiguousarray(np.repeat(f(norm1_g)[0].reshape(8, 128).T, 2, axis=1)),
        "n2T2": np.ascontiguousarray(np.repeat(f(norm2_g)[0].reshape(8, 128).T, 2, axis=1)),
        "w_in": f(w_in)[0], "qg8": np.ascontiguousarray(np.tile(f(q_norm_g)[0], 8)),
        "kg2": np.ascontiguousarray(np.tile(f(k_norm_g)[0], 2)), "btab": btab,
        "w_br_a": f(w_br_a)[0], "w_br_b": f(w_br_b)[0], "w_out": f(w_out)[0],
        "w1": f(w_mlp_in)[0], "w2": f(w_mlp_out)[0], "gf": f(final_norm_g),
        "ident": np.eye(128, dtype=np.float32), "rope": rope,
    }
    xs_, xp_ = f(x_sample), f(x_prompt)
    cc = f(c_ctx)
    maps = []
    for b in range(8):
        cT = np.empty((128, 8, 2), np.float32)
        cT[:, :, 0] = f(c)[b].reshape(8, 128).T
        cT[:, :, 1] = cc.reshape(8, 128).T
        m = dict(shared)
        m.update({
            "xs": xs_[b], "xp": np.ascontiguousarray(xp_[4 * b:4 * b + 4].reshape(1024, 1024)),
            "cak": np.ascontiguousarray(f(cache_a_k)[b, 0].reshape(512, 128)),
            "cav": np.ascontiguousarray(f(cache_a_v)[b, 0].reshape(512, 128)),
            "cbk": np.ascontiguousarray(f(cache_b_k)[b, 0].reshape(512, 512)),
            "cbv": np.ascontiguousarray(f(cache_b_v)[b, 0].reshape(512, 512)),
            "cT": np.ascontiguousarray(cT.reshape(128, 16)),
        })
        maps.append(m)
    return maps


def kernel(**inputs):
    if "nc" not in _NC_CACHE:
        _NC_CACHE["nc"] = build()
    nc = _NC_CACHE["nc"]
    maps = make_in_maps(**inputs)
    res = run_bass_kernel_spmd(nc, maps, core_ids=list(range(8)))
    R = res.results
    y_sample = np.stack([R[b]["ys"] for b in range(8)], axis=0)
    y_prompt = np.concatenate([R[b]["yp"].reshape(4, 256, 1024) for b in range(8)], axis=0)
    nak = np.concatenate([R[b]["nak"].reshape(4, 1, 256, 2, 64) for b in range(8)], axis=0)
    nav = np.concatenate([R[b]["nav"].reshape(4, 1, 256, 2, 64) for b in range(8)], axis=0)
    nbk = np.concatenate([R[b]["nbk"].reshape(4, 1, 256, 8, 64) for b in range(8)], axis=0)
    nbv = np.concatenate([R[b]["nbv"].reshape(4, 1, 256, 8, 64) for b in range(8)], axis=0)
    return (y_prompt.astype(np.float32), y_sample.astype(np.float32), nak.astype(np.float32),
            nav.astype(np.float32), nbk.astype(np.float32), nbv.astype(np.float32))
```
